# Optimizing a Trainium2 kernel written in Bass

```python
import jax, jax.numpy as jnp
from jax import lax
import numpy as np

D_MODEL = 2048
BATCH = 8
SEQ = 4096
DEPTH = 1
DEC_BATCH = 32
DEC_SEQ = 16
PAST_LEN = 1024

CHUNK = 64
ROPE_THETA = 500000.0
NORM_EPS = 1e-6
Q_BLOCK = 128
NEG_INF = -1e30
MLA_HEADS = 8
Q_LORA = 512
KV_LORA = 256
MLA_NOPE = 128
MLA_ROPE = 64
MLA_V = 128
DSA_HEADS = 8
DSA_KV_HEADS = 2
DSA_HD = 128
DSA_ROT = DSA_HD // 4
IDX_HEADS = 16
IDX_DIM = 64
IDX_ROT = IDX_DIM // 4
TOPK_MAX = 256
D_FF = 5632
CONV_W = 3

IN_WIDTHS = (Q_LORA, KV_LORA + MLA_ROPE, DSA_HEADS * DSA_HD, 2 * DSA_KV_HEADS * DSA_HD,
             IDX_HEADS * IDX_DIM, IDX_DIM, IDX_HEADS, 2 * D_MODEL)
IN_COLS = sum(IN_WIDTHS)
IN_SPLITS = tuple(sum(IN_WIDTHS[:i + 1]) for i in range(len(IN_WIDTHS) - 1))

kernel_name = 'mla_dsa_gated_convffn_stream_step'


def rms_norm(x, g):
    xf = x.astype(jnp.float32)
    y = xf * lax.rsqrt(jnp.mean(xf * xf, axis=-1, keepdims=True) + NORM_EPS)
    return (y * g.astype(jnp.float32)).astype(x.dtype)


def rope(x, pos):
    half = x.shape[-1] // 2
    inv = ROPE_THETA ** (-jnp.arange(half, dtype=jnp.float32) / half)
    ang = pos.astype(jnp.float32)[:, None] * inv[None, :]
    shape = (ang.shape[0],) + (1,) * (x.ndim - 3) + (half,)
    cos = jnp.cos(ang).reshape(shape)
    sin = jnp.sin(ang).reshape(shape)
    xf = x.astype(jnp.float32)
    x1, x2 = xf[..., :half], xf[..., half:]
    return jnp.concatenate([x1 * cos - x2 * sin, x2 * cos + x1 * sin], axis=-1).astype(x.dtype)


def partial_rope(x, pos, rot):
    return jnp.concatenate([rope(x[..., :rot], pos), x[..., rot:]], axis=-1)


def chunk_visible(qpos, kpos):
    return (kpos // CHUNK)[None, :] <= (qpos // CHUNK)[:, None]


def to_blocks(a, blk):
    b, t = a.shape[:2]
    return jnp.moveaxis(a.reshape((b, t // blk, blk) + a.shape[2:]), 1, 0)


def from_blocks(a):
    a = jnp.moveaxis(a, 0, 1)
    return a.reshape((a.shape[0], a.shape[1] * a.shape[2]) + a.shape[3:])


def mla_attention(q, k, v, qpos, kpos):
    scale = (MLA_NOPE + MLA_ROPE) ** -0.5
    blk = min(Q_BLOCK, q.shape[1])

    def one(args):
        qb, pb = args
        s = jnp.einsum('bqhd,bkhd->bhqk', qb, k).astype(jnp.float32) * scale
        s = jnp.where(chunk_visible(pb, kpos)[None, None], s, NEG_INF)
        p = jax.nn.softmax(s, axis=-1).astype(v.dtype)
        return jnp.einsum('bhqk,bkhd->bqhd', p, v)

    return from_blocks(lax.map(one, (to_blocks(q, blk), qpos.reshape(-1, blk))))


def dsa_attention(q, qi, wi, k, v, ki, qpos, kpos):
    topk = min(TOPK_MAX, k.shape[1] // 4)
    blk = min(Q_BLOCK, q.shape[1])
    gather = jax.vmap(lambda rows, idx: rows[idx])
    rep = DSA_HEADS // DSA_KV_HEADS

    def one(args):
        qb, qib, wib, pb = args
        dots = jnp.einsum('bqhd,bkd->bqhk', qib, ki).astype(jnp.float32) * (IDX_DIM ** -0.5)
        score = jnp.einsum('bqh,bqhk->bqk', wib.astype(jnp.float32), jax.nn.relu(dots))
        score = jnp.where(chunk_visible(pb, kpos)[None], score, NEG_INF)
        _, sel = lax.top_k(score, topk)
        valid = (kpos[sel] // CHUNK) <= (pb // CHUNK)[None, :, None]
        ks = gather(k, sel)
        vs = gather(v, sel)
        qg = qb.reshape(qb.shape[:2] + (DSA_KV_HEADS, rep, DSA_HD))
        s = jnp.einsum('bqgrd,bqkgd->bqgrk', qg, ks).astype(jnp.float32) * (DSA_HD ** -0.5)
        s = jnp.where(valid[:, :, None, None, :], s, NEG_INF)
        p = jax.nn.softmax(s, axis=-1).astype(vs.dtype)
        o = jnp.einsum('bqgrk,bqkgd->bqgrd', p, vs)
        return o.reshape(qb.shape)

    args = (to_blocks(q, blk), to_blocks(qi, blk), to_blocks(wi, blk), qpos.reshape(-1, blk))
    return from_blocks(lax.map(one, args))


def _layer(x, pos, past, prm):
    (attn_norm, w_in, q_a_norm, w_q_up, kv_a_norm, w_kv_up, mla_q_nope_norm, mla_q_rope_norm,
     mla_k_nope_norm, mla_k_rope_norm, dsa_q_norm, dsa_k_norm, w_o_mla, w_o_dsa, w_out,
     ffn_norm, w_ffn_up, conv_w, conv_b, w_ffn_down) = prm
    b, t, _ = x.shape
    h = rms_norm(x, attn_norm)
    c_q, kv_a, q_d, kv_d, qi, ki, wi, gates = jnp.split(h @ w_in, IN_SPLITS, axis=-1)

    q = (rms_norm(c_q, q_a_norm) @ w_q_up).reshape(b, t, MLA_HEADS, MLA_NOPE + MLA_ROPE)
    q_nope = rms_norm(q[..., :MLA_NOPE], mla_q_nope_norm)
    q_pe = rope(rms_norm(q[..., MLA_NOPE:], mla_q_rope_norm), pos)
    c_kv = rms_norm(kv_a[..., :KV_LORA], kv_a_norm)
    k_pe = rope(rms_norm(kv_a[..., KV_LORA:], mla_k_rope_norm), pos)

    q_d = partial_rope(rms_norm(q_d.reshape(b, t, DSA_HEADS, DSA_HD), dsa_q_norm), pos, DSA_ROT)
    k_new, v_new = jnp.split(kv_d.reshape(b, t, 2 * DSA_KV_HEADS, DSA_HD), 2, axis=2)
    k_new = partial_rope(rms_norm(k_new, dsa_k_norm), pos, DSA_ROT)
    qi = partial_rope(qi.reshape(b, t, IDX_HEADS, IDX_DIM), pos, IDX_ROT)
    ki = partial_rope(ki, pos, IDX_ROT)
    wi = wi * (IDX_HEADS ** -0.5)

    if past is None:
        all_ckv, all_kpe, all_k, all_v, all_ki = c_kv, k_pe, k_new, v_new, ki
        conv_hist = jnp.zeros((b, CONV_W - 1, D_FF), x.dtype)
        kpos = pos
    else:
        p_ckv, p_kpe, p_k, p_v, p_ki, conv_hist = past
        all_ckv = jnp.concatenate([p_ckv, c_kv], axis=1)
        all_kpe = jnp.concatenate([p_kpe, k_pe], axis=1)
        all_k = jnp.concatenate([p_k, k_new], axis=1)
        all_v = jnp.concatenate([p_v, v_new], axis=1)
        all_ki = jnp.concatenate([p_ki, ki], axis=1)
        kpos = jnp.arange(all_ckv.shape[1], dtype=jnp.int32)
    s_len = all_ckv.shape[1]

    kv = (all_ckv @ w_kv_up).reshape(b, s_len, MLA_HEADS, MLA_NOPE + MLA_V)
    k_nope = rms_norm(kv[..., :MLA_NOPE], mla_k_nope_norm)
    k_full = jnp.concatenate(
        [k_nope, jnp.broadcast_to(all_kpe[:, :, None, :], (b, s_len, MLA_HEADS, MLA_ROPE))], axis=-1)
    q_full = jnp.concatenate([q_nope, q_pe], axis=-1)
    o_mla = mla_attention(q_full, k_full, kv[..., MLA_NOPE:], pos, kpos).reshape(b, t, MLA_HEADS * MLA_V)
    o_dsa = dsa_attention(q_d, qi, wi, all_k, all_v, all_ki, pos, kpos).reshape(b, t, DSA_HEADS * DSA_HD)

    g = jax.nn.sigmoid(gates.astype(jnp.float32)).astype(x.dtype)
    g_mla, g_dsa = jnp.split(g, 2, axis=-1)
    x = x + (g_mla * (o_mla @ w_o_mla) + g_dsa * (o_dsa @ w_o_dsa)) @ w_out

    gate_pre, up = jnp.split(rms_norm(x, ffn_norm) @ w_ffn_up, 2, axis=-1)
    padded = jnp.concatenate([conv_hist, gate_pre], axis=1)
    conv = conv_b + sum(conv_w[j] * padded[:, j:j + t] for j in range(CONV_W))
    y = x + (jax.nn.silu(conv) * up) @ w_ffn_down
    new_state = (c_kv, k_pe, k_new, v_new, ki, padded[:, -(CONV_W - 1):])
    return y, new_state


def setup_inputs(seed: int = 0) -> dict:
    key = jax.random.key(seed)
    ks = jax.random.split(key, 28)

    def nrm(k, shape, scale):
        return jax.random.normal(k, shape, jnp.float32) * scale

    def gain(k, n):
        return 1.0 + 0.02 * jax.random.normal(k, (DEPTH, n), jnp.float32)

    return {
        'x_prompt': nrm(ks[0], (BATCH, SEQ, D_MODEL), 1.0),
        'x_sample': nrm(ks[1], (DEC_BATCH, DEC_SEQ, D_MODEL), 1.0),
        'cache_mla_ckv': nrm(ks[2], (DEPTH, DEC_BATCH, PAST_LEN, KV_LORA), 1.0),
        'cache_mla_kpe': nrm(ks[3], (DEPTH, DEC_BATCH, PAST_LEN, MLA_ROPE), 1.0),
        'cache_dsa_k': nrm(ks[4], (DEPTH, DEC_BATCH, PAST_LEN, DSA_KV_HEADS, DSA_HD), 1.0),
        'cache_dsa_v': nrm(ks[5], (DEPTH, DEC_BATCH, PAST_LEN, DSA_KV_HEADS, DSA_HD), 1.0),
        'cache_idx_k': nrm(ks[6], (DEPTH, DEC_BATCH, PAST_LEN, IDX_DIM), 1.0),
        'state_ffn_conv': nrm(ks[7], (DEPTH, DEC_BATCH, CONV_W - 1, D_FF), 1.0),
        'attn_norm': gain(ks[8], D_MODEL),
        'w_in': nrm(ks[9], (DEPTH, D_MODEL, IN_COLS), D_MODEL ** -0.5),
        'q_a_norm': gain(ks[10], Q_LORA),
        'w_q_up': nrm(ks[11], (DEPTH, Q_LORA, MLA_HEADS * (MLA_NOPE + MLA_ROPE)), Q_LORA ** -0.5),
        'kv_a_norm': gain(ks[12], KV_LORA),
        'w_kv_up': nrm(ks[13], (DEPTH, KV_LORA, MLA_HEADS * (MLA_NOPE + MLA_V)), KV_LORA ** -0.5),
        'mla_q_nope_norm': gain(ks[14], MLA_NOPE),
        'mla_q_rope_norm': gain(ks[15], MLA_ROPE),
        'mla_k_nope_norm': gain(ks[16], MLA_NOPE),
        'mla_k_rope_norm': gain(ks[17], MLA_ROPE),
        'dsa_q_norm': gain(ks[18], DSA_HD),
        'dsa_k_norm': gain(ks[19], DSA_HD),
        'w_o_mla': nrm(ks[20], (DEPTH, MLA_HEADS * MLA_V, D_MODEL), (MLA_HEADS * MLA_V) ** -0.5),
        'w_o_dsa': nrm(ks[21], (DEPTH, DSA_HEADS * DSA_HD, D_MODEL), (DSA_HEADS * DSA_HD) ** -0.5),
        'w_out': nrm(ks[22], (DEPTH, D_MODEL, D_MODEL), D_MODEL ** -0.5),
        'ffn_norm': gain(ks[23], D_MODEL),
        'w_ffn_up': nrm(ks[24], (DEPTH, D_MODEL, 2 * D_FF), D_MODEL ** -0.5),
        'conv_w': nrm(ks[25], (DEPTH, CONV_W, D_FF), CONV_W ** -0.5),
        'conv_b': nrm(ks[26], (DEPTH, D_FF), 0.02),
        'w_ffn_down': nrm(ks[27], (DEPTH, D_FF, D_MODEL), D_FF ** -0.5),
    }


def reference(x_prompt, x_sample, cache_mla_ckv, cache_mla_kpe, cache_dsa_k, cache_dsa_v,
              cache_idx_k, state_ffn_conv, attn_norm, w_in, q_a_norm, w_q_up, kv_a_norm, w_kv_up,
              mla_q_nope_norm, mla_q_rope_norm, mla_k_nope_norm, mla_k_rope_norm, dsa_q_norm,
              dsa_k_norm, w_o_mla, w_o_dsa, w_out, ffn_norm, w_ffn_up, conv_w, conv_b, w_ffn_down):
    pos_p = jnp.arange(x_prompt.shape[1], dtype=jnp.int32)
    pos_s = cache_mla_ckv.shape[2] + jnp.arange(x_sample.shape[1], dtype=jnp.int32)
    xp, xs = x_prompt, x_sample
    p_states, s_states = [], []
    for l in range(DEPTH):
        prm = (attn_norm[l], w_in[l], q_a_norm[l], w_q_up[l], kv_a_norm[l], w_kv_up[l],
               mla_q_nope_norm[l], mla_q_rope_norm[l], mla_k_nope_norm[l], mla_k_rope_norm[l],
               dsa_q_norm[l], dsa_k_norm[l], w_o_mla[l], w_o_dsa[l], w_out[l], ffn_norm[l],
               w_ffn_up[l], conv_w[l], conv_b[l], w_ffn_down[l])
        past = (cache_mla_ckv[l], cache_mla_kpe[l], cache_dsa_k[l], cache_dsa_v[l],
                cache_idx_k[l], state_ffn_conv[l])
        xp, sp = _layer(xp, pos_p, None, prm)
        xs, ss = _layer(xs, pos_s, past, prm)
        p_states.append(sp)
        s_states.append(ss)
    p_mla_ckv, p_mla_kpe, p_dsa_k, p_dsa_v, p_idx_k, p_ffn_conv = [jnp.stack(a) for a in zip(*p_states)]
    s_mla_ckv, s_mla_kpe, s_dsa_k, s_dsa_v, s_idx_k, s_ffn_conv = [jnp.stack(a) for a in zip(*s_states)]
    return (xp, xs, p_mla_ckv, p_mla_kpe, p_dsa_k, p_dsa_v, p_idx_k, p_ffn_conv,
            s_mla_ckv, s_mla_kpe, s_dsa_k, s_dsa_v, s_idx_k, s_ffn_conv)
```

```python
import numpy as np
import concourse.bass as bass
import concourse.mybir as mybir
from concourse.bass_utils import run_bass_kernel_spmd

F32 = mybir.dt.float32
BF16 = mybir.dt.bfloat16
AF = mybir.ActivationFunctionType
ALU = mybir.AluOpType
AX = mybir.AxisListType

D_MODEL = 2048
CHUNK = 64
EPS = 1e-6
NEG = -1e30
Q_LORA, KV_LORA = 512, 256
NOPE, ROPE, MV, MH = 128, 64, 128, 8
DH, DKV, DHD, DROT = 8, 2, 128, 32
IH, IDIM, IROT = 16, 64, 16
TOPK_MAX = 256
D_FF = 5632
NA1 = 1856
NA2 = 1616
NA = NA1 + NA2
IN_COLS = NA + 2 * D_MODEL


class Cfg:
    def __init__(self, T=4096, NS=4, PAST=1024, DEC=16):
        self.T, self.NS, self.PAST, self.DEC = T, NS, PAST, DEC
        self.NT = T // 128
        self.TS = NS * DEC
        self.TT = T + self.TS
        self.SS = PAST + DEC
        self.KT = T + NS * self.SS


class Buf:
    __slots__ = ("name", "w", "r")

    def __init__(self, name):
        self.name, self.w, self.r = name, None, []


class Op:
    __slots__ = ("eng", "fn", "deps", "dma")

    def __init__(self, eng, fn, deps, dma):
        self.eng, self.fn, self.deps, self.dma = eng, fn, deps, dma


class Sched:
    ND = {"sp": 8, "pool": 8, "act": 4}

    def __init__(self, nc):
        self.nc = nc
        self.ops = []
        self.eng = {"pe": nc.tensor, "act": nc.scalar, "dve": nc.vector, "pool": nc.gpsimd, "sp": nc.sync}
        self.sb_off = 16512
        self.sb_end = 229376

    def mark(self):
        return self.sb_off

    def release(self, m):
        self.sb_off = m

    def sb(self, name, shape, dtype):
        nbytes = int(np.prod(shape[1:])) * (4 if dtype == F32 else 2)
        off = (self.sb_off + 63) // 64 * 64
        assert off + nbytes <= self.sb_end, f"SBUF overflow at {name}: {off}+{nbytes}"
        self.sb_off = off + nbytes
        self._n = getattr(self, "_n", 0) + 1
        return self.nc.alloc_sbuf_tensor_at(f"{name}_{self._n}", list(shape), dtype, offset=off)

    def op(self, eng, fn, reads=(), writes=(), dma=False):
        deps = set()
        for b in reads:
            if b.w is not None:
                deps.add(b.w)
        for b in writes:
            if b.w is not None:
                deps.add(b.w)
            deps.update(b.r)
        i = len(self.ops)
        self.ops.append(Op(eng, fn, deps, dma))
        for b in reads:
            b.r.append(i)
        for b in writes:
            b.w = i
            b.r = []
        return i

    def dma(self, q, out, in_, reads=(), writes=(), slow=False):
        e = self.eng[q]
        if slow:
            return self.op(q, lambda: e.dma_start(out=out, in_=in_, allow_slow_non_contiguous=True), reads, writes, dma=True)
        return self.op(q, lambda: e.dma_start(out=out, in_=in_), reads, writes, dma=True)

    def emit(self):
        nc, ops = self.nc, self.ops
        n = len(ops)
        need = [False] * n
        for o in ops:
            for d in o.deps:
                p = ops[d]
                if p.dma:
                    continue
                if p.eng == "pe" and o.eng == "pe" and not o.dma:
                    continue
                need[d] = True
        sems = {}

        def sem(key):
            if key not in sems:
                sems[key] = nc.alloc_semaphore("s_" + "_".join(str(k) for k in key))
            return sems[key]

        cnt, dcnt, tok = {}, {}, [None] * n
        for i, o in enumerate(ops):
            if o.dma:
                k = dcnt.get(o.eng, 0)
                dcnt[o.eng] = k + 1
                nd = self.ND[o.eng]
                tok[i] = (("d", o.eng, k % nd), 16 * (k // nd + 1))
            elif need[i]:
                cnt[o.eng] = cnt.get(o.eng, 0) + 1
                tok[i] = (("c", o.eng), cnt[o.eng])
        seen = {e: {} for e in self.eng}
        nwaits = 0
        for i, o in enumerate(ops):
            E = o.eng
            waits = {}
            for d in o.deps:
                p = ops[d]
                if (not p.dma) and p.eng == "pe" and E == "pe" and not o.dma:
                    continue
                s, v = tok[d]
                if waits.get(s, 0) < v:
                    waits[s] = v
            if o.dma:
                s, v = tok[i]
                if v > 16 and waits.get(s, 0) < v - 16:
                    waits[s] = v - 16
            for s, v in waits.items():
                if seen[E].get(s, 0) >= v:
                    continue
                seen[E][s] = v
                self.eng[E].wait_ge(sem(s), v)
                nwaits += 1
            ins = o.fn()
            if tok[i] is not None:
                ins.then_inc(sem(tok[i][0]), 16 if o.dma else 1)
        for q, k in dcnt.items():
            nd = self.ND[q]
            for slot in range(min(nd, k)):
                last = (k - 1 - slot) // nd * nd + slot
                v = 16 * (last // nd + 1)
                if seen["sp"].get(("d", q, slot), 0) < v:
                    nc.sync.wait_ge(sem(("d", q, slot)), v)
        for e, c in cnt.items():
            nc.sync.wait_ge(sem(("c", e)), c)
        self.stats = dict(n_ops=n, n_waits=nwaits, n_sems=len(sems))


class Builder:
    def __init__(self, cfg, debug=False):
        self.cfg = cfg
        self.debug = debug
        self.nc = bass.Bass("TRN2", target_bir_lowering=False)
        self.S = Sched(self.nc)
        self._bufn = 0

    def B(self, name="b"):
        self._bufn += 1
        return Buf(f"{name}{self._bufn}")

    def din(self, name, shape, dt=F32):
        return self.nc.dram_tensor(name, list(shape), dt, kind="ExternalInput").ap()

    def dout(self, name, shape, dt=F32):
        return self.nc.dram_tensor(name, list(shape), dt, kind="ExternalOutput").ap()

    def dscr(self, name, shape, dt=BF16):
        kind = "ExternalOutput" if (self.debug and not name.startswith("w_")) else "Internal"
        return self.nc.dram_tensor(name, list(shape), dt, kind=kind).ap()

    def pe(self, fn, r=(), w=()):
        return self.S.op("pe", fn, r, w)

    def act(self, fn, r=(), w=()):
        return self.S.op("act", fn, r, w)

    def dve(self, fn, r=(), w=()):
        return self.S.op("dve", fn, r, w)

    def pool(self, fn, r=(), w=()):
        return self.S.op("pool", fn, r, w)

    def dma(self, out, in_, r=(), w=(), q="sp", slow=False):
        return self.S.dma(q, out, in_, r, w, slow=slow)

    def mm(self, out, lhsT, rhs, start, stop, r=(), w=()):
        t = self.nc.tensor
        return self.pe(lambda: t.matmul(out, lhsT=lhsT, rhs=rhs, start=start, stop=stop), r, w)

    def tr(self, out, in_, ident, r=(), w=()):
        t = self.nc.tensor
        return self.pe(lambda: t.transpose(out, in_, ident), r, w)

    def declare(self):
        c = self.cfg
        T, NS, PAST, TS, TT, KT = c.T, c.NS, c.PAST, c.TS, c.TT, c.KT
        I = {}
        I["xp"] = self.din("xp", [T, D_MODEL])
        I["xs"] = self.din("xs", [TS, D_MODEL])
        I["c_ckv"] = self.din("c_ckv", [NS, PAST, KV_LORA])
        I["c_kpe"] = self.din("c_kpe", [NS, PAST, ROPE])
        I["c_dk"] = self.din("c_dk", [NS, PAST, DKV * DHD])
        I["c_dv"] = self.din("c_dv", [NS, PAST, DKV * DHD])
        I["c_ik"] = self.din("c_ik", [NS, PAST, IDIM])
        I["c_conv"] = self.din("c_conv", [NS, 2, D_FF])
        I["attn_norm"] = self.din("attn_norm", [1, D_MODEL])
        I["w_in"] = self.din("w_in", [D_MODEL, IN_COLS])
        I["q_a_norm"] = self.din("q_a_norm", [1, Q_LORA])
        I["w_q_up"] = self.din("w_q_up", [Q_LORA, MH * (NOPE + ROPE)])
        I["kv_a_norm"] = self.din("kv_a_norm", [1, KV_LORA])
        I["w_kv_up"] = self.din("w_kv_up", [KV_LORA, MH * (NOPE + MV)])
        for nm, d in (("mla_q_nope_norm", NOPE), ("mla_q_rope_norm", ROPE), ("mla_k_nope_norm", NOPE),
                      ("mla_k_rope_norm", ROPE), ("dsa_q_norm", DHD), ("dsa_k_norm", DHD)):
            I[nm] = self.din(nm, [1, d])
        I["w_o_mla"] = self.din("w_o_mla", [MH * MV, D_MODEL])
        I["w_o_dsa"] = self.din("w_o_dsa", [DH * DHD, D_MODEL])
        I["w_out"] = self.din("w_out", [D_MODEL, D_MODEL])
        I["ffn_normT"] = self.din("ffn_normT", [128, D_MODEL // 128])
        I["w_ffn_up"] = self.din("w_ffn_up", [D_MODEL, 2 * D_FF])
        I["conv_wT"] = self.din("conv_wT", [128, 3, D_FF // 128])
        I["conv_bT"] = self.din("conv_bT", [128, D_FF // 128])
        I["w_ffn_down"] = self.din("w_ffn_down", [D_FF, D_MODEL])
        I["ident"] = self.din("ident", [128, 128])
        I["ropet"] = self.din("ropet", [max(T, PAST + c.DEC), 2, 56])
        self.I = I
        O = {}
        O["y_p"] = self.dout("y_p", [T, D_MODEL])
        O["y_s"] = self.dout("y_s", [TS, D_MODEL])
        O["p_ckv"] = self.dout("p_ckv", [T, KV_LORA])
        O["p_kpe"] = self.dout("p_kpe", [T, ROPE])
        O["p_dk"] = self.dout("p_dk", [T, DKV * DHD])
        O["p_dv"] = self.dout("p_dv", [T, DKV * DHD])
        O["p_ik"] = self.dout("p_ik", [T, IDIM])
        O["p_conv"] = self.dout("p_conv", [2, D_FF])
        O["s_ckv"] = self.dout("s_ckv", [TS, KV_LORA])
        O["s_kpe"] = self.dout("s_kpe", [TS, ROPE])
        O["s_dk"] = self.dout("s_dk", [TS, DKV * DHD])
        O["s_dv"] = self.dout("s_dv", [TS, DKV * DHD])
        O["s_ik"] = self.dout("s_ik", [TS, IDIM])
        O["s_conv"] = self.dout("s_conv", [NS, 2, D_FF])
        self.O = O
        W = {}
        for nm in ("w_in", "w_q_up", "w_kv_up", "w_o_mla", "w_o_dsa", "w_out", "w_ffn_up", "w_ffn_down"):
            W[nm] = self.dscr(nm + "_b", I[nm].shape)
        self.W = W
        X = {}
        X["hT"] = self.dscr("hT_s", [D_MODEL, TT])
        X["qnT"] = self.dscr("qnT_s", [MH * NOPE, TT])
        X["qrT"] = self.dscr("qrT_s", [MH * ROPE, TT])
        X["knT"] = self.dscr("knT_s", [MH * NOPE, KT])
        X["kpeT"] = self.dscr("kpeT_s", [ROPE, KT])
        X["v"] = self.dscr("v_s", [KT, MH * MV])
        X["qdT"] = self.dscr("qdT_s", [DH * DHD, TT])
        X["kdT"] = self.dscr("kdT_s", [DKV * DHD, KT])
        X["vd"] = self.dscr("vd_s", [KT, DKV * DHD])
        X["qiT"] = self.dscr("qiT_s", [IH * IDIM, TT])
        X["kiT"] = self.dscr("kiT_s", [IDIM, KT])
        X["wi"] = self.dscr("wi_s", [TT, 2, IH], F32)
        X["omT"] = self.dscr("omT_s", [MH * MV, TT])
        X["odT"] = self.dscr("odT_s", [DH * DHD, TT])
        self.X = X
        self.XB = {k: self.B("x_" + k) for k in X}
        self.WB = {k: self.B("w_" + k) for k in W}
        if self.debug:
            self.DBG = {}

    def token_tiles(self):
        c = self.cfg
        tiles = [(i * 128, 128, False) for i in range(c.NT)]
        tiles.append((c.T, c.TS, True))
        return tiles

    def setup_consts(self):
        S, I, c = self.S, self.I, self.cfg
        self.psum = [self.nc.alloc_psum_tensor(f"ps{i}", [128, 512], F32) for i in range(8)]
        self.PB = [self.B(f"psb{i}") for i in range(8)]
        self.identf = S.sb("identf", [128, 128], F32)
        self.identb = S.sb("identb", [128, 128], BF16)
        self.onesb = S.sb("onesb", [128, 128], BF16)
        self.CB = self.B("consts")
        self.dma(self.identf[:], I["ident"][:, :], w=[self.CB])
        v = self.nc.vector
        self.dve(lambda: v.tensor_copy(out=self.identb[:], in_=self.identf[:]), r=[self.CB], w=[self.CB])
        self.dve(lambda: v.memset(self.onesb[:], 1.0), w=[self.CB])
        self.NI = 16
        self.ctab = S.sb("ctab", [128, self.NI], F32)
        for i in range(self.NI):
            self.dve(lambda i=i: v.memset(self.ctab[:, i:i + 1], float(2.0 ** -(i + 1))), w=[self.CB])
        self.epsc = S.sb("epsc", [128, 1], F32)
        self.dve(lambda: v.memset(self.epsc[:], EPS), w=[self.CB])

    def rep_gain(self, name, d):
        t = self.S.sb("g_" + name, [128, d], F32)
        self.dma(t[:], self.I[name][0:1, :].partition_broadcast(128), w=[self.CB])
        return t

    EARLY_W = ("w_in", "w_q_up", "w_kv_up")

    def cast_gen(self, names):
        for nm in names:
            w, src = self.W[nm], self.I[nm]
            rows = src.shape[0]
            for r0 in range(0, rows, 128):
                self.dma(w[r0:r0 + 128, :], src[r0:r0 + 128, :], w=[self.WB[nm]], q="pool")
                yield

    def cast_weights(self, names):
        for _ in self.cast_gen(names):
            pass

    def phase0(self):
        self.cast_weights(self.EARLY_W)

    def rms_gen(self, nr, src, H, D, gain, dst, sq, ss, rB, wB, tmpB):
        v = self.nc.vector
        a = self.nc.scalar
        sqv = sq[:nr, 0:H * D].rearrange("p (h d) -> p h d", d=D)
        self.dve(lambda: v.tensor_tensor(out=sqv, in0=src, in1=src, op=ALU.mult), r=rB, w=[tmpB])
        yield
        ssv = ss[:nr, 0:H]
        self.dve(lambda: v.tensor_reduce(out=ssv, in_=sqv, axis=AX.X, op=ALU.add), r=[tmpB], w=[tmpB])
        yield
        self.act(lambda: a.activation(out=ssv, in_=ssv, func=AF.Sqrt, scale=1.0 / D, bias=self.epsc[:nr, 0:1]),
                 r=[tmpB, self.CB], w=[tmpB])
        yield
        self.dve(lambda: v.reciprocal(out=ssv, in_=ssv), r=[tmpB], w=[tmpB])
        yield
        rsb = ssv.unsqueeze(2).to_broadcast([nr, H, D])
        self.dve(lambda: v.tensor_tensor(out=sqv, in0=src, in1=rsb, op=ALU.mult), r=list(rB) + [tmpB], w=[tmpB])
        yield
        gb = gain[:nr, :].unsqueeze(1).to_broadcast([nr, H, D])
        self.dve(lambda: v.tensor_tensor(out=dst, in0=sqv, in1=gb, op=ALU.mult), r=[tmpB, self.CB], w=wB)
        yield

    @staticmethod
    def run(*gens):
        gens = list(gens)
        while gens:
            for g in list(gens):
                try:
                    next(g)
                except StopIteration:
                    gens.remove(g)

    def rms_heads(self, nr, src, H, D, gain, dst, sq, ss, rB, wB, tmpB):
        self.run(self.rms_gen(nr, src, H, D, gain, dst, sq, ss, rB, wB, tmpB))

    def rope(self, nr, src, dst, H, half, cos, sin, tmp, rB, wB, tmpB, eng="pool"):
        g = self.nc.gpsimd if eng == "pool" else self.nc.vector
        opf = self.pool if eng == "pool" else self.dve
        x1, x2 = src[:, :, 0:half], src[:, :, half:2 * half]
        cb = cos.unsqueeze(1).to_broadcast([nr, H, half])
        sn = sin.unsqueeze(1).to_broadcast([nr, H, half])
        t = [tmp[:nr, k * H * half:(k + 1) * H * half].rearrange("p (h d) -> p h d", d=half) for k in range(4)]
        opf(lambda: g.tensor_tensor(out=t[0], in0=x1, in1=cb, op=ALU.mult), r=rB, w=[tmpB])
        opf(lambda: g.tensor_tensor(out=t[1], in0=x2, in1=sn, op=ALU.mult), r=rB, w=[tmpB])
        opf(lambda: g.tensor_tensor(out=t[2], in0=x2, in1=cb, op=ALU.mult), r=rB, w=[tmpB])
        opf(lambda: g.tensor_tensor(out=t[3], in0=x1, in1=sn, op=ALU.mult), r=rB, w=[tmpB])
        opf(lambda: g.tensor_tensor(out=dst[:, :, 0:half], in0=t[0], in1=t[1], op=ALU.subtract), r=[tmpB], w=wB)
        opf(lambda: g.tensor_tensor(out=dst[:, :, half:2 * half], in0=t[2], in1=t[3], op=ALU.add), r=[tmpB], w=wB)

    def bank(self):
        self._bk = (getattr(self, "_bk", -1) + 1) % 8
        return self.psum[self._bk], self.PB[self._bk]

    def key_dsts(self, t0, nr, is_sample):
        c = self.cfg
        if not is_sample:
            return [(t0, 0, nr)]
        return [(c.T + s * c.SS + c.PAST, s * c.DEC, c.DEC) for s in range(c.NS)]

    def evac(self, k, out, in_, r, w):
        if k % 2 == 0:
            a = self.nc.scalar
            return self.act(lambda: a.activation(out=out, in_=in_, func=AF.Copy), r, w)
        v = self.nc.vector
        return self.dve(lambda: v.tensor_copy(out=out, in_=in_), r, w)

    def transpose_blocks(self, nr, srcs, rB, dst, wB, k0=0):
        n = len(srcs)
        for g0 in range(0, n, 8):
            g = srcs[g0:g0 + 8]
            bk, BK = self.bank()
            bb = bk[:].bitcast(BF16)
            for j, s_ap in enumerate(g):
                wd = s_ap.shape[1]
                self.tr(bb[:wd, j * 128:j * 128 + nr], s_ap, self.identb[:nr, :nr], r=list(rB) + [self.CB], w=[BK])
            src = bb[:, 0:len(g) * 128].rearrange("p (j t) -> p j t", t=128)[:, :, :nr]
            self.evac(k0 + g0 // 8, dst[:, g0:g0 + len(g), :nr], src, r=[], w=[BK] + list(wB))

    def kside_mla(self, nr, ckvb, kpeb, inB, dsts, R, split=False):
        X, v, a = self.X, self.nc.vector, self.nc.scalar
        self.transpose_blocks(nr, [ckvb[:nr, 0:128], ckvb[:nr, 128:256]], inB, R["ckvT"], [R["CKVT"]])
        for n0 in range(4):
            bk, BK = self.bank()
            for k in range(2):
                self.mm(bk[:nr, :], R["ckvT"][:, k, :nr], R["wkv"][:, k, n0 * 512:(n0 + 1) * 512], k == 0, k == 1,
                        r=[R["CKVT"], R["WKV"]], w=[BK])
            self.evac(n0, R["kvf"][:nr, n0 * 512:(n0 + 1) * 512], bk[:nr, :], r=[], w=[BK, R["KVF"][n0]])
        if split:
            return
        self.run(self.kside_kn_gen(nr, R))
        self.kside_b(nr, kpeb, inB, dsts, R)

    def kside_kn_gen(self, nr, R):
        kv3 = R["kvf"][:nr, :].rearrange("p (h d) -> p h d", d=256)
        knb3 = R["knb"][:nr, :].rearrange("p (h d) -> p h d", d=128)
        return self.rms_gen(nr, kv3[:, :, 0:128], 8, 128, R["g_kn"], knb3, R["sq2"][:, 1024:2048], R["ss"][:, 16:24], R["KVF"],
                            [R["KNB"]], R["TMPK"])

    def kside_b(self, nr, kpeb, inB, dsts, R):
        X, v, a = self.X, self.nc.vector, self.nc.scalar
        kv3 = R["kvf"][:nr, :].rearrange("p (h d) -> p h d", d=256)
        vb3 = R["vb"][:nr, :].rearrange("p (h d) -> p h d", d=128)
        self.act(lambda: a.activation(out=vb3, in_=kv3[:, :, 128:256], func=AF.Copy), r=R["KVF"], w=[R["VB"]])
        blocks = [R["knb"][:nr, h * 128:(h + 1) * 128] for h in range(8)]
        self.transpose_blocks(nr, blocks, [R["KNB"]], R["tstk"], [R["TSTK"]], k0=1)
        self.transpose_blocks(nr, [kpeb[:nr, 0:64]], inB, R["tstk"][:, 8:9, :], [R["TSTK"]])
        for (koff, c0, n) in dsts:
            self.dma(X["knT"][:, koff:koff + n].rearrange("(h p) t -> p h t", p=128), R["tstk"][:, 0:8, c0:c0 + n],
                     r=[R["TSTK"]], w=[self.XB["knT"]])
            self.dma(X["kpeT"][:, koff:koff + n], R["tstk"][0:64, 8, c0:c0 + n], r=[R["TSTK"]], w=[self.XB["kpeT"]])
            self.dma(X["v"][koff:koff + n, :], R["vb"][c0:c0 + n, :], r=[R["VB"]], w=[self.XB["v"]])

    def phaseA1(self):
        S, I, O, X, W, c = self.S, self.I, self.O, self.X, self.W, self.cfg
        nc = self.nc
        v, a, gp = nc.vector, nc.scalar, nc.gpsimd
        m = S.mark()
        R = {}
        wA = S.sb("wA1", [128, 16, NA1], BF16)
        WA = self.B()
        for k in range(16):
            self.dma(wA[:, k, :], W["w_in"][k * 128:(k + 1) * 128, 0:NA1], r=[self.WB["w_in"]], w=[WA])
        wq = S.sb("wq", [128, 4, 1536], BF16)
        WQ = self.B()
        for k in range(4):
            self.dma(wq[:, k, :], W["w_q_up"][k * 128:(k + 1) * 128, :], r=[self.WB["w_q_up"]], w=[WQ])
        R["wkv"] = S.sb("wkv", [128, 2, 2048], BF16)
        R["WKV"] = self.B()
        for k in range(2):
            self.dma(R["wkv"][:, k, :], W["w_kv_up"][k * 128:(k + 1) * 128, :], r=[self.WB["w_kv_up"]], w=[R["WKV"]])
        g_attn = self.rep_gain("attn_norm", D_MODEL)
        g_qa = self.rep_gain("q_a_norm", Q_LORA)
        g_kva = self.rep_gain("kv_a_norm", KV_LORA)
        g_qn = self.rep_gain("mla_q_nope_norm", NOPE)
        g_qr = self.rep_gain("mla_q_rope_norm", ROPE)
        R["g_kn"] = self.rep_gain("mla_k_nope_norm", NOPE)
        g_kr = self.rep_gain("mla_k_rope_norm", ROPE)
        g_dq = self.rep_gain("dsa_q_norm", DHD)
        xbuf = [S.sb("x", [128, D_MODEL], F32) for _ in range(2)]
        XBF = [self.B() for _ in range(2)]
        R["sq"] = S.sb("sq", [128, 2048], F32)
        R["sq2"] = S.sb("sq2", [128, 2048], F32)
        R["ss"] = S.sb("ss", [128, 32], F32)
        R["TMP"] = self.B()
        R["TMPK"] = self.B()
        TM = [self.B() for _ in range(4)]
        ss0 = S.sb("ss0", [128, 2], F32)
        SS0 = self.B()
        sqj = S.sb("sqj", [128, 2048], BF16)
        SQJ = self.B()
        hb = S.sb("hb", [128, D_MODEL], BF16)
        HB = self.B()
        hT = [S.sb("hT", [128, 16, 128], BF16) for _ in range(2)]
        HT = [self.B() for _ in range(2)]
        pjs = [S.sb("pj", [128, NA1], F32) for _ in range(2)]
        PJS = [[self.B() for _ in range(4)] for _ in range(2)]
        cqb = S.sb("cqb", [128, 512], BF16)
        CQB = self.B()
        ckvf = S.sb("ckvf", [128, 256], F32)
        CKVF = self.B()
        ckvb = S.sb("ckvb", [128, 256], BF16)
        CKVB = self.B()
        kpef = S.sb("kpef", [128, 64], F32)
        KPEF = self.B()
        kpeb = S.sb("kpeb", [128, 64], BF16)
        KPEB = self.B()
        cqT = S.sb("cqT", [128, 4, 128], BF16)
        CQT = self.B()
        R["ckvT"] = S.sb("ckvT", [128, 2, 128], BF16)
        R["CKVT"] = self.B()
        qf = S.sb("qf", [128, 1536], F32)
        QF = [self.B() for _ in range(3)]
        qnb = S.sb("qnb", [128, 1024], BF16)
        QNB = self.B()
        qrf = S.sb("qrf", [128, 512], F32)
        QRF = self.B()
        qrb = S.sb("qrb", [128, 512], BF16)
        QRB = self.B()
        R["kvf"] = S.sb("kvf", [128, 2048], F32)
        R["KVF"] = [self.B() for _ in range(4)]
        R["knb"] = S.sb("knb", [128, 1024], BF16)
        R["KNB"] = self.B()
        R["vb"] = S.sb("vb", [128, 1024], BF16)
        R["VB"] = self.B()
        R["tstk"] = S.sb("tstk", [128, 9, 128], BF16)
        R["TSTK"] = self.B()
        tstq = S.sb("tstq", [128, 12, 128], BF16)
        TSTQ = self.B()
        tstd = S.sb("tstd", [128, 8, 128], BF16)
        TSTD = self.B()
        qdb = S.sb("qdb", [128, 1024], BF16)
        QDB = self.B()
        rtmp = S.sb("rtmp", [128, 1024], F32)
        RTMP = self.B()
        ropes = [S.sb("ropes", [128, 2, 56], F32) for _ in range(2)]
        RP = [self.B() for _ in range(2)]
        cin = S.sb("cin", [128, 320], F32)
        CIN = self.B()

        tiles_ = self.token_tiles()

        def front(idx):
            t0, nr, smp = tiles_[idx]
            pb = idx % 2
            xt, XT = xbuf[pb], XBF[pb]
            rp, RPB = ropes[pb], RP[pb]
            pj, PJ = pjs[pb], PJS[pb]
            if smp:
                self.dma(xt[:nr, :], I["xs"][:, :], w=[XT])
                for s in range(c.NS):
                    self.dma(rp[s * c.DEC:(s + 1) * c.DEC, :, :], I["ropet"][c.PAST:c.PAST + c.DEC, :, :], w=[RPB])
            else:
                self.dma(xt[:nr, :], I["xp"][t0:t0 + nr, :], w=[XT])
                self.dma(rp[:nr, :, :], I["ropet"][t0:t0 + nr, :, :], w=[RPB])
            self.act(lambda xt=xt, nr=nr: a.activation(out=sqj[:nr, :], in_=xt[:nr, :], func=AF.Square,
                                                         accum_out=ss0[:nr, 0:1]), r=[XT], w=[SQJ, SS0])
            self.act(lambda nr=nr: a.activation(out=ss0[:nr, 0:1], in_=ss0[:nr, 0:1], func=AF.Sqrt, scale=1.0 / D_MODEL,
                                                 bias=self.epsc[:nr, 0:1]), r=[self.CB], w=[SS0])
            self.dve(lambda nr=nr: v.reciprocal(out=ss0[:nr, 0:1], in_=ss0[:nr, 0:1]), w=[SS0])
            self.dve(lambda xt=xt, nr=nr: v.scalar_tensor_tensor(out=hb[:nr, :], in0=xt[:nr, :], scalar=ss0[:nr, 0:1],
                                                                  in1=g_attn[:nr, :], op0=ALU.mult, op1=ALU.mult),
                     r=[XT, SS0, self.CB], w=[HB])
            self.transpose_blocks(nr, [hb[:nr, k * 128:(k + 1) * 128] for k in range(16)], [HB], hT[pb], [HT[pb]])
            self.dma(X["hT"][:, t0:t0 + nr].rearrange("(k p) t -> p k t", p=128), hT[pb][:, :, :nr], r=[HT[pb]],
                     w=[self.XB["hT"]])
            for gi, (c0, cw) in enumerate(((0, 512), (512, 512), (1024, 512), (1536, 320))):
                bk, BK = self.bank()
                for k in range(16):
                    self.mm(bk[:nr, :cw], hT[pb][:, k, :nr], wA[:, k, c0:c0 + cw], k == 0, k == 15, r=[HT[pb], WA], w=[BK])
                self.evac(gi, pj[:nr, c0:c0 + cw], bk[:nr, :cw], r=[], w=[BK, PJ[gi]])
        def back(idx):
            t0, nr, smp = tiles_[idx]
            pb = idx % 2
            rp, RPB = ropes[pb], RP[pb]
            pj, PJ = pjs[pb], PJS[pb]
            sq, sq2, ss = R["sq"], R["sq2"], R["ss"]
            kp3 = kpef[:nr, :].rearrange("p (h d) -> p h d", d=64)
            qd3 = pj[:nr, 832:1856].rearrange("p (h d) -> p h d", d=128)
            qdf3 = sq2[:nr, 1024:2048].rearrange("p (h d) -> p h d", d=128)
            self.run(
                self.rms_gen(nr, pj[:nr, 0:512].rearrange("p (h d) -> p h d", d=512), 1, 512, g_qa,
                             cqb[:nr, :].rearrange("p (h d) -> p h d", d=512), sq[:, 0:512], ss[:, 0:1], [PJ[0]], [CQB], TM[0]),
                self.rms_gen(nr, pj[:nr, 512:768].rearrange("p (h d) -> p h d", d=256), 1, 256, g_kva,
                             ckvf[:nr, :].rearrange("p (h d) -> p h d", d=256), sq[:, 512:768], ss[:, 1:2], [PJ[1]], [CKVF], TM[1]),
                self.rms_gen(nr, pj[:nr, 768:832].rearrange("p (h d) -> p h d", d=64), 1, 64, g_kr, kp3, sq[:, 768:832], ss[:, 2:3],
                             [PJ[1]], [KPEF], TM[2]),
                self.rms_gen(nr, qd3, 8, 128, g_dq, qdf3, sq2[:, 0:1024], ss[:, 3:11], [PJ[1], PJ[2], PJ[3]], [TM[3]], TM[3]),
            )
            self.transpose_blocks(nr, [cqb[:nr, k * 128:(k + 1) * 128] for k in range(4)], [CQB], cqT, [CQT])
            for n0 in range(3):
                bk, BK = self.bank()
                for k in range(4):
                    self.mm(bk[:nr, :], cqT[:, k, :nr], wq[:, k, n0 * 512:(n0 + 1) * 512], k == 0, k == 3, r=[CQT, WQ], w=[BK])
                self.evac(n0 + 1, qf[:nr, n0 * 512:(n0 + 1) * 512], bk[:nr, :], r=[], w=[BK, QF[n0]])
            self.act(lambda: a.activation(out=ckvb[:nr, :], in_=ckvf[:nr, :], func=AF.Copy), r=[CKVF], w=[CKVB])
            self.rope(nr, kp3, kp3, 1, 32, rp[:nr, 0, 0:32], rp[:nr, 1, 0:32], rtmp, [KPEF, RPB], [KPEF], RTMP)
            self.act(lambda: a.activation(out=kpeb[:nr, :], in_=kpef[:nr, :], func=AF.Copy), r=[KPEF], w=[KPEB])
            if smp:
                self.dma(O["s_ckv"][:, :], ckvf[:nr, :], r=[CKVF])
                self.dma(O["s_kpe"][:, :], kpef[:nr, :], r=[KPEF])
            else:
                self.dma(O["p_ckv"][t0:t0 + nr, :], ckvf[:nr, :], r=[CKVF])
                self.dma(O["p_kpe"][t0:t0 + nr, :], kpef[:nr, :], r=[KPEF])
            self.kside_mla(nr, ckvb, kpeb, [CKVB, KPEB], None, R, split=True)
            self.rope(nr, qdf3[:, :, 0:32], qdf3[:, :, 0:32], 8, 16, rp[:nr, 0, 32:48], rp[:nr, 1, 32:48], rtmp,
                      [TM[3], RPB], [TM[3]], RTMP)
            self.act(lambda: a.activation(out=qdb[:nr, :].rearrange("p (h d) -> p h d", d=128), in_=qdf3, func=AF.Copy),
                     r=[TM[3]], w=[QDB])
            q3 = qf[:nr, :].rearrange("p (h d) -> p h d", d=192)
            qr3 = qrf[:nr, :].rearrange("p (h d) -> p h d", d=64)
            self.run(
                self.rms_gen(nr, q3[:, :, 0:128], 8, 128, g_qn, qnb[:nr, :].rearrange("p (h d) -> p h d", d=128),
                             sq[:, 0:1024], ss[:, 0:8], QF, [QNB], TM[0]),
                self.rms_gen(nr, q3[:, :, 128:192], 8, 64, g_qr, qr3, sq[:, 1024:1536], ss[:, 8:16], QF, [QRF], TM[1]),
                self.kside_kn_gen(nr, R),
            )
            self.transpose_blocks(nr, [qdb[:nr, h * 128:(h + 1) * 128] for h in range(8)], [QDB], tstd, [TSTD], k0=1)
            self.dma(X["qdT"][:, t0:t0 + nr].rearrange("(h p) t -> p h t", p=128), tstd[:, :, :nr], r=[TSTD],
                     w=[self.XB["qdT"]])
            self.rope(nr, qr3, qrb[:nr, :].rearrange("p (h d) -> p h d", d=64), 8, 32, rp[:nr, 0, 0:32], rp[:nr, 1, 0:32],
                      rtmp, [QRF, RPB], [QRB], RTMP)
            self.kside_b(nr, kpeb, [CKVB, KPEB], self.key_dsts(t0, nr, smp), R)
            blocks = [qnb[:nr, h * 128:(h + 1) * 128] for h in range(8)] + [qrb[:nr, j * 128:(j + 1) * 128] for j in range(4)]
            self.transpose_blocks(nr, blocks, [QNB, QRB], tstq, [TSTQ])
            self.dma(X["qnT"][:, t0:t0 + nr].rearrange("(h p) t -> p h t", p=128), tstq[:, 0:8, :nr], r=[TSTQ],
                     w=[self.XB["qnT"]])
            self.dma(X["qrT"][:, t0:t0 + nr].rearrange("(h p) t -> p h t", p=128), tstq[:, 8:12, :nr], r=[TSTQ],
                     w=[self.XB["qrT"]])

        late = self.cast_gen([k for k in self.W if k not in self.EARLY_W])
        front(0)
        for idx in range(len(tiles_)):
            if idx + 1 < len(tiles_):
                front(idx + 1)
            for _ in range(3):
                next(late, None)
            back(idx)
        for _ in late:
            pass
        for s in range(c.NS):
            for j in range(c.PAST // 128):
                self.dma(cin[:, 0:256], I["c_ckv"][s, j * 128:(j + 1) * 128, :], w=[CIN])
                self.dma(cin[:, 256:320], I["c_kpe"][s, j * 128:(j + 1) * 128, :], w=[CIN])
                self.dve(lambda: v.tensor_copy(out=ckvb[:, :], in_=cin[:, 0:256]), r=[CIN], w=[CKVB])
                self.dve(lambda: v.tensor_copy(out=kpeb[:, :], in_=cin[:, 256:320]), r=[CIN], w=[KPEB])
                self.kside_mla(128, ckvb, kpeb, [CKVB, KPEB], [(c.T + s * c.SS + j * 128, 0, 128)], R)
        S.release(m)

    def phaseA2(self):
        S, I, O, X, W, c = self.S, self.I, self.O, self.X, self.W, self.cfg
        nc = self.nc
        v, a, gp = nc.vector, nc.scalar, nc.gpsimd
        m = S.mark()
        wA = S.sb("wA2", [128, 16, NA2], BF16)
        WA = self.B()
        for k in range(16):
            self.dma(wA[:, k, :], W["w_in"][k * 128:(k + 1) * 128, NA1:NA], r=[self.WB["w_in"]], w=[WA])
        g_dk = self.rep_gain("dsa_k_norm", DHD)
        hT = [S.sb("hT2", [128, 16, 128], BF16) for _ in range(2)]
        HT = [self.B() for _ in range(2)]
        ropes = [S.sb("ropes2", [128, 2, 56], F32) for _ in range(2)]
        RP = [self.B() for _ in range(2)]
        pjs = [S.sb("pj2", [128, NA2], F32) for _ in range(2)]
        PJS = [[self.B() for _ in range(4)] for _ in range(2)]
        sq = S.sb("sq2", [128, 1024], F32)
        ss = S.sb("ss2", [128, 16], F32)
        TMP = self.B()
        rtmp = S.sb("rtmp2", [128, 1024], F32)
        RTMP = self.B()
        kdf = S.sb("kdf", [128, 256], F32)
        KDF = self.B()
        kdb = S.sb("kdb", [128, 256], BF16)
        KDB = self.B()
        vdb = S.sb("vdb", [128, 256], BF16)
        VDB = self.B()
        qib = S.sb("qib", [128, 1024], BF16)
        QIB = self.B()
        kib = S.sb("kib", [128, 64], BF16)
        KIB = self.B()
        tst = S.sb("tst2", [128, 11, 128], BF16)
        TST = self.B()
        wtmp = S.sb("wtmp", [128, 2 * IH], F32)
        WT = self.B()
        wout = S.sb("wout", [128, 2, IH], F32)
        WO = self.B()
        cin = S.sb("cin2", [128, 576], F32)
        CIN = self.B()

        def kside(nr, inB, dsts):
            self.transpose_blocks(nr, [kdb[:nr, 0:128], kdb[:nr, 128:256], kib[:nr, 0:64]], inB, tst[:, 8:11, :], [TST])
            for (koff, c0, n) in dsts:
                self.dma(X["kdT"][:, koff:koff + n].rearrange("(g p) t -> p g t", p=128), tst[:, 8:10, c0:c0 + n], r=[TST],
                         w=[self.XB["kdT"]])
                self.dma(X["kiT"][:, koff:koff + n], tst[0:64, 10, c0:c0 + n], r=[TST], w=[self.XB["kiT"]])
                self.dma(X["vd"][koff:koff + n, :], vdb[c0:c0 + n, :], r=[VDB], w=[self.XB["vd"]])

        tiles_ = self.token_tiles()

        def front(idx):
            t0, nr, smp = tiles_[idx]
            pb = idx % 2
            rp, RPB = ropes[pb], RP[pb]
            pj, PJ = pjs[pb], PJS[pb]
            self.dma(hT[pb][:, :, :nr], X["hT"][:, t0:t0 + nr].rearrange("(k p) t -> p k t", p=128), r=[self.XB["hT"]],
                     w=[HT[pb]])
            if smp:
                for s in range(c.NS):
                    self.dma(rp[s * c.DEC:(s + 1) * c.DEC, :, :], I["ropet"][c.PAST:c.PAST + c.DEC, :, :], w=[RPB])
            else:
                self.dma(rp[:nr, :, :], I["ropet"][t0:t0 + nr, :, :], w=[RPB])
            for gi, (c0, cw) in enumerate(((0, 512), (512, 512), (1024, 512), (1536, 80))):
                bk, BK = self.bank()
                for k in range(16):
                    self.mm(bk[:nr, :cw], hT[pb][:, k, :nr], wA[:, k, c0:c0 + cw], k == 0, k == 15, r=[HT[pb], WA], w=[BK])
                self.evac(gi, pj[:nr, c0:c0 + cw], bk[:nr, :cw], r=[], w=[BK, PJ[gi]])
        def back(idx):
            t0, nr, smp = tiles_[idx]
            pb = idx % 2
            rp, RPB = ropes[pb], RP[pb]
            pj, PJ = pjs[pb], PJS[pb]
            kv3 = pj[:nr, 0:512].rearrange("p (h d) -> p h d", d=128)
            kd3 = kdf[:nr, :].rearrange("p (h d) -> p h d", d=128)
            self.rms_heads(nr, kv3[:, 0:2, :], 2, 128, g_dk, kd3, sq, ss, [PJ[0]], [KDF], TMP)
            self.rope(nr, kd3[:, :, 0:32], kd3[:, :, 0:32], 2, 16, rp[:nr, 0, 32:48], rp[:nr, 1, 32:48], rtmp, [KDF, RPB],
                      [KDF], RTMP)
            self.act(lambda nr=nr: a.activation(out=kdb[:nr, :], in_=kdf[:nr, :], func=AF.Copy), r=[KDF], w=[KDB])
            self.act(lambda nr=nr: a.activation(out=vdb[:nr, :], in_=pj[:nr, 256:512], func=AF.Copy), r=[PJ[0]], w=[VDB])
            qi3 = pj[:nr, 512:1536].rearrange("p (h d) -> p h d", d=64)
            self.rope(nr, qi3[:, :, 0:16], qi3[:, :, 0:16], 16, 8, rp[:nr, 0, 48:56], rp[:nr, 1, 48:56], rtmp,
                      [PJ[1], PJ[2], RPB], [PJ[1], PJ[2]], RTMP)
            self.act(lambda nr=nr: a.activation(out=qib[:nr, :], in_=pj[:nr, 512:1536], func=AF.Copy), r=[PJ[1], PJ[2]], w=[QIB])
            ki3 = pj[:nr, 1536:1600].rearrange("p (h d) -> p h d", d=64)
            self.rope(nr, ki3[:, :, 0:16], ki3[:, :, 0:16], 1, 8, rp[:nr, 0, 48:56], rp[:nr, 1, 48:56], rtmp, [PJ[3], RPB],
                      [PJ[3]], RTMP)
            self.act(lambda nr=nr: a.activation(out=kib[:nr, :], in_=pj[:nr, 1536:1600], func=AF.Copy), r=[PJ[3]], w=[KIB])
            if smp:
                self.dma(O["s_dk"][:, :], kdf[:nr, :], r=[KDF])
                self.dma(O["s_dv"][:, :], pj[:nr, 256:512], r=[PJ[0]])
                self.dma(O["s_ik"][:, :], pj[:nr, 1536:1600], r=[PJ[3]])
            else:
                self.dma(O["p_dk"][t0:t0 + nr, :], kdf[:nr, :], r=[KDF])
                self.dma(O["p_dv"][t0:t0 + nr, :], pj[:nr, 256:512], r=[PJ[0]])
                self.dma(O["p_ik"][t0:t0 + nr, :], pj[:nr, 1536:1600], r=[PJ[3]])
            self.transpose_blocks(nr, [qib[:nr, j * 128:(j + 1) * 128] for j in range(8)], [QIB], tst[:, 0:8, :], [TST])
            self.dma(X["qiT"][:, t0:t0 + nr].rearrange("(j p) t -> p j t", p=128), tst[:, 0:8, :nr], r=[TST],
                     w=[self.XB["qiT"]])
            kside(nr, [KDB, KIB], self.key_dsts(t0, nr, smp))
            wi = pj[:nr, 1600:1616]
            self.dve(lambda nr=nr, wi=wi: v.tensor_scalar(out=wtmp[:nr, 0:IH], in0=wi, scalar1=-1.0, scalar2=None, op0=ALU.mult),
                     r=[PJ[3]], w=[WT])
            self.dve(lambda nr=nr, wi=wi: v.tensor_tensor(out=wtmp[:nr, 0:IH], in0=wtmp[:nr, 0:IH], in1=wi, op=ALU.max),
                     r=[PJ[3]], w=[WT])
            self.dve(lambda nr=nr: v.tensor_scalar(out=wout[:nr, 0, :], in0=wtmp[:nr, 0:IH], scalar1=0.03125,
                                                   scalar2=None, op0=ALU.mult), r=[WT], w=[WO])
            self.dve(lambda nr=nr, wi=wi: v.tensor_scalar(out=wtmp[:nr, IH:2 * IH], in0=wi, scalar1=0.0, scalar2=2.0,
                                                          op0=ALU.is_gt, op1=ALU.mult), r=[PJ[3]], w=[WT])
            self.dve(lambda nr=nr: v.tensor_scalar(out=wout[:nr, 1, :], in0=wtmp[:nr, IH:2 * IH], scalar1=-1.0,
                                                   scalar2=None, op0=ALU.add), r=[WT], w=[WO])
            self.dma(X["wi"][t0:t0 + nr, :, :], wout[:nr, :, :], r=[WO], w=[self.XB["wi"]])

        front(0)
        for idx in range(len(tiles_)):
            if idx + 1 < len(tiles_):
                front(idx + 1)
            back(idx)
        for s in range(c.NS):
            for j in range(c.PAST // 128):
                rows = slice(j * 128, (j + 1) * 128)
                self.dma(cin[:, 0:256], I["c_dk"][s, rows, :], w=[CIN])
                self.dma(cin[:, 256:512], I["c_dv"][s, rows, :], w=[CIN])
                self.dma(cin[:, 512:576], I["c_ik"][s, rows, :], w=[CIN])
                self.dve(lambda: v.tensor_copy(out=kdb[:, :], in_=cin[:, 0:256]), r=[CIN], w=[KDB])
                self.act(lambda: a.activation(out=vdb[:, :], in_=cin[:, 256:512], func=AF.Copy), r=[CIN], w=[VDB])
                self.dve(lambda: v.tensor_copy(out=kib[:, :], in_=cin[:, 512:576]), r=[CIN], w=[KIB])
                kside(128, [KDB, KIB], [(c.T + s * c.SS + j * 128, 0, 128)])
        S.release(m)

    def bank_of(self, pool, key):
        cnt = getattr(self, "_bkc", {})
        self._bkc = cnt
        i = cnt.get(key, 0)
        cnt[key] = i + 1
        b = pool[i % len(pool)]
        return self.psum[b], self.PB[b]

    def seqs(self):
        c = self.cfg
        out = [dict(koff=0, S=c.T, qoff=0, nq=c.T, causal=True)]
        for s in range(c.NS):
            out.append(dict(koff=c.T + s * c.SS, S=c.SS, qoff=c.T + s * c.DEC, nq=c.DEC, causal=False))
        return out

    def phaseB(self):
        S, X, c = self.S, self.X, self.cfg
        nc = self.nc
        v, a, gp = nc.vector, nc.scalar, nc.gpsimd
        m = S.mark()
        Smax = max(c.T, c.SS)
        nktmax = (Smax + 127) // 128
        nqmax = max(c.T, c.DEC)
        kn = [S.sb("kn", [128, Smax], BF16) for _ in range(2)]
        vt = [S.sb("vt", [128, nktmax, 128], BF16) for _ in range(2)]
        qn = [S.sb("qn", [128, nqmax], BF16) for _ in range(2)]
        qr = [S.sb("qr", [64, nqmax], BF16) for _ in range(2)]
        HB = [self.B() for _ in range(2)]
        kpe = S.sb("kpe", [64, Smax], BF16)
        KPE = self.B()
        pT = [S.sb("pT", [128, 512], BF16) for _ in range(5)]
        PT = [self.B() for _ in range(5)]
        rden = S.sb("rden", [128, 512], F32)
        RD = self.B()
        oT = [S.sb("oT", [128, 512], BF16) for _ in range(2)]
        OT = [self.B() for _ in range(2)]
        scale = float((NOPE + ROPE) ** -0.5)
        rdeps = [self.XB[k] for k in ("knT", "kpeT", "v", "qnT", "qrT")]

        def load(sq, h, pb):
            koff, Sk, qoff, nq = sq["koff"], sq["S"], sq["qoff"], sq["nq"]
            self.dma(kn[pb][:, :Sk], X["knT"][h * 128:(h + 1) * 128, koff:koff + Sk], r=rdeps, w=[HB[pb]])
            nfull = Sk // 128
            if nfull:
                self.dma(vt[pb][:, 0:nfull, :],
                         X["v"][koff:koff + nfull * 128, h * 128:(h + 1) * 128].rearrange("(k p) d -> p k d", p=128),
                         r=rdeps, w=[HB[pb]])
            rem = Sk - nfull * 128
            if rem:
                self.dma(vt[pb][:rem, nfull, :], X["v"][koff + nfull * 128:koff + Sk, h * 128:(h + 1) * 128], r=rdeps, w=[HB[pb]])
            self.dma(qn[pb][:, :nq], X["qnT"][h * 128:(h + 1) * 128, qoff:qoff + nq], r=rdeps, w=[HB[pb]])
            self.dma(qr[pb][:, :nq], X["qrT"][h * 64:(h + 1) * 64, qoff:qoff + nq], r=rdeps, w=[HB[pb]])

        work = [(sq, h) for sq in self.seqs() for h in range(MH)]
        load(work[0][0], work[0][1], 0)
        pc = [0]
        ocount = 0
        for wi_, (sq, h) in enumerate(work):
            pb = wi_ % 2
            if wi_ + 1 < len(work):
                load(work[wi_ + 1][0], work[wi_ + 1][1], (wi_ + 1) % 2)
            koff, Sk, qoff, nq, causal = sq["koff"], sq["S"], sq["qoff"], sq["nq"], sq["causal"]
            if h == 0:
                self.dma(kpe[:, :Sk], X["kpeT"][:, koff:koff + Sk], r=rdeps, w=[KPE])
            for qb0 in range(0, nq, 512):
                nqb = min(512, nq - qb0)
                nkt = (qb0 + nqb) // 128 if causal else (Sk + 127) // 128
                ao, AO = self.bank_of([0, 1], "bo")
                ad, AD = self.bank_of([2, 3], "bd")
                pend = {}

                def front(kt):
                    kr = min(128, Sk - kt * 128)
                    qa = max(0, kt * 128 - qb0) if causal else 0
                    n = nqb - qa
                    sb_, SB_ = self.bank_of([4, 5, 6, 7], "bs")
                    self.mm(sb_[:kr, :n], kn[pb][:, kt * 128:kt * 128 + kr], qn[pb][:, qb0 + qa:qb0 + nqb], True, False,
                            r=[HB[pb]], w=[SB_])
                    self.mm(sb_[:kr, :n], kpe[:, kt * 128:kt * 128 + kr], qr[pb][:, qb0 + qa:qb0 + nqb], False, True,
                            r=[HB[pb], KPE], w=[SB_])
                    p, P = pT[pc[0] % 5], PT[pc[0] % 5]
                    pc[0] += 1
                    self.act(lambda: a.activation(out=p[:kr, :n], in_=sb_[:kr, :n], func=AF.Exp, scale=scale), r=[], w=[SB_, P])
                    if causal and kt * 128 >= qb0:
                        self.pool(lambda: gp.memset(p[64:128, 0:64], 0.0), r=[], w=[P])
                    pend[kt] = (p, P, kr, qa, n)

                def back(kt):
                    p, P, kr, qa, n = pend.pop(kt)
                    first, last = kt == 0, kt == nkt - 1
                    self.mm(ao[:, qa:nqb], vt[pb][:kr, kt, :], p[:kr, :n], first, last, r=[HB[pb], P], w=[AO])
                    self.mm(ad[:, qa:nqb], self.onesb[:kr, :], p[:kr, :n], first, last, r=[self.CB, P], w=[AD])

                LA = 2
                for k_ in range(nkt + LA):
                    if k_ < nkt:
                        front(k_)
                    if k_ >= LA:
                        back(k_ - LA)
                o, OB = oT[ocount % 2], OT[ocount % 2]
                ocount += 1
                self.dve(lambda ad=ad, nqb=nqb: v.reciprocal(out=rden[:, :nqb], in_=ad[:, :nqb]), r=[], w=[AD, RD])
                self.dve(lambda o=o, ao=ao, nqb=nqb: v.tensor_tensor(out=o[:, :nqb], in0=ao[:, :nqb], in1=rden[:, :nqb],
                                                                      op=ALU.mult), r=[RD], w=[AO, OB])
                self.dma(X["omT"][h * 128:(h + 1) * 128, qoff + qb0:qoff + qb0 + nqb], o[:, :nqb], r=[OB], w=[self.XB["omT"]])
        S.release(m)

    def phaseC(self):
        S, X, c = self.S, self.X, self.cfg
        nc = self.nc
        v, a, gp = nc.vector, nc.scalar, nc.gpsimd
        m = S.mark()
        Smax = max(c.T, c.SS)
        nktmax = (Smax + 127) // 128
        ki2 = S.sb("ki2", [128, Smax], BF16)
        kd = S.sb("kd", [128, 2, Smax], BF16)
        vd = S.sb("vd", [128, nktmax, 256], BF16)
        SEQB = self.B()
        qi = [S.sb("qi", [128, 8, 128], BF16) for _ in range(2)]
        qd = [S.sb("qd", [128, 8, 128], BF16) for _ in range(2)]
        wv = [S.sb("wv", [128, 2, IH], F32) for _ in range(2)]
        QB = [self.B() for _ in range(2)]
        scores = [S.sb("score", [128, Smax], F32) for _ in range(2)]
        SCB = [self.B() for _ in range(2)]
        m8 = S.sb("m8", [128, 8], F32)
        M8 = self.B()
        thr = S.sb("thr", [128, 1], F32)
        bs_ = S.sb("bsct", [128, 4], F32)
        rtab = S.sb("rtab", [128, self.NI], F32)
        cntb = S.sb("cntb", [128, self.NI], F32)
        mask = S.sb("mask", [128, Smax], BF16)
        MK = self.B()
        maskTs = [S.sb("maskT", [128, nktmax, 128], BF16) for _ in range(2)]
        MTB = [self.B() for _ in range(2)]
        tmp = [S.sb("itmp", [128, 512], F32) for _ in range(3)]
        TM = [self.B() for _ in range(3)]
        NEB = 4
        eT = [S.sb("eT", [128, 512], BF16) for _ in range(NEB)]
        ET = [self.B() for _ in range(NEB)]
        pT = [S.sb("pTd", [128, 512], BF16) for _ in range(NEB)]
        PT = [self.B() for _ in range(NEB)]
        rden = S.sb("rdend", [128, 512], F32)
        RD = self.B()
        oT = [S.sb("oTd", [128, 512], BF16) for _ in range(2)]
        OT = [self.B() for _ in range(2)]
        scale = float(DHD ** -0.5)
        kdeps = [self.XB[k] for k in ("kiT", "kdT", "vd")]
        qdeps = [self.XB[k] for k in ("qiT", "qdT", "wi")]
        tiles = []
        for si, sq in enumerate(self.seqs()):
            if sq["causal"]:
                for qt in range(sq["nq"] // 128):
                    tiles.append((si, sq, sq["qoff"] + qt * 128, 128, (qt + 1) * 128, True))
            else:
                tiles.append((si, sq, sq["qoff"], sq["nq"], sq["S"], False))

        def loadq(tl, pb):
            tok0, nq = tl[2], tl[3]
            self.dma(qi[pb][:, :, :nq], X["qiT"][:, tok0:tok0 + nq].rearrange("(j p) t -> p j t", p=128), r=qdeps, w=[QB[pb]])
            self.dma(qd[pb][:, :, :nq], X["qdT"][:, tok0:tok0 + nq].rearrange("(j p) t -> p j t", p=128), r=qdeps, w=[QB[pb]])
            self.dma(wv[pb][:nq, :, :], X["wi"][tok0:tok0 + nq, :, :], r=qdeps, w=[QB[pb]])

        st = dict(tcount=0, ecount=0, ocount=0, cur_seq=-1)
        acc = {}

        def seq_load(tl):
            si, sq = tl[0], tl[1]
            if si == st["cur_seq"]:
                return
            st["cur_seq"] = si
            koff, Sk = sq["koff"], sq["S"]
            for half in range(2):
                self.dma(ki2[half * 64:(half + 1) * 64, :Sk], X["kiT"][:, koff:koff + Sk], r=kdeps, w=[SEQB])
            self.dma(kd[:, :, :Sk], X["kdT"][:, koff:koff + Sk].rearrange("(g p) t -> p g t", p=128), r=kdeps, w=[SEQB])
            nfull = Sk // 128
            if nfull:
                self.dma(vd[:, 0:nfull, :], X["vd"][koff:koff + nfull * 128, :].rearrange("(k p) d -> p k d", p=128),
                         r=kdeps, w=[SEQB])
            if Sk - nfull * 128:
                self.dma(vd[:Sk - nfull * 128, nfull, :], X["vd"][koff + nfull * 128:koff + Sk, :], r=kdeps, w=[SEQB])

        def idx(ti):
            si, sq, tok0, nq, W, causal = tiles[ti]
            pb = ti % 2
            score, SC = scores[pb], SCB[pb]
            for kb in range(0, W, 512):
                wb = min(512, W - kb)
                for j in range(8):
                    banks = [self.bank_of([4, 5, 6, 7], "bs") for _ in range(2)]
                    for hh in range(2):
                        bk, BK = banks[hh]
                        self.mm(bk[:nq, :wb], qi[pb][hh * 64:(hh + 1) * 64, j, :nq], ki2[hh * 64:(hh + 1) * 64, kb:kb + wb],
                                True, True, r=[QB[pb], SEQB], w=[BK])
                    for hh in range(2):
                        bk, BK = banks[hh]
                        h = 2 * j + hh
                        t_, T_ = tmp[st["tcount"] % 3], TM[st["tcount"] % 3]
                        st["tcount"] += 1
                        self.act(lambda t_=t_, bk=bk, nq=nq, wb=wb, pb=pb, h=h: a.activation(
                            out=t_[:nq, :wb], in_=bk[:nq, :wb], func=AF.Relu, scale=wv[pb][:nq, 0, h:h + 1]),
                            r=[QB[pb]], w=[BK, T_])
                        if h == 0:
                            self.dve(lambda t_=t_, nq=nq, wb=wb, kb=kb, pb=pb, h=h, score=score: v.tensor_scalar(
                                out=score[:nq, kb:kb + wb], in0=t_[:nq, :wb], scalar1=wv[pb][:nq, 1, h:h + 1], scalar2=None,
                                op0=ALU.mult), r=[T_, QB[pb]], w=[SC])
                        else:
                            self.dve(lambda t_=t_, nq=nq, wb=wb, kb=kb, pb=pb, h=h, score=score: v.scalar_tensor_tensor(
                                out=score[:nq, kb:kb + wb], in0=t_[:nq, :wb], scalar=wv[pb][:nq, 1, h:h + 1],
                                in1=score[:nq, kb:kb + wb], op0=ALU.mult, op1=ALU.add), r=[T_, QB[pb]], w=[SC])

        def bisect(ti):
            si, sq, tok0, nq, W, causal = tiles[ti]
            pb = ti % 2
            score, SC = scores[pb], SCB[pb]
            topk = min(TOPK_MAX, sq["S"] // 4)
            if W > topk:
                NI = self.NI
                self.dve(lambda: v.tensor_reduce(out=bs_[:nq, 0:1], in_=score[:nq, :W], axis=AX.X, op=ALU.max), r=[SC], w=[M8])
                self.dve(lambda: v.tensor_reduce(out=thr[:nq, 0:1], in_=score[:nq, :W], axis=AX.X, op=ALU.min), r=[SC], w=[M8])
                if causal:
                    self.dve(lambda: v.memset(score[0:64, W - 64:W], NEG), r=[M8], w=[SC])
                self.dve(lambda: v.tensor_tensor(out=bs_[:nq, 0:1], in0=bs_[:nq, 0:1], in1=thr[:nq, 0:1], op=ALU.subtract),
                         r=[], w=[M8])
                self.dve(lambda: v.tensor_scalar(out=bs_[:nq, 0:1], in0=bs_[:nq, 0:1], scalar1=1.0001, scalar2=1e-6,
                                                 op0=ALU.mult, op1=ALU.add), r=[], w=[M8])
                self.dve(lambda: v.tensor_scalar(out=rtab[:nq, :], in0=self.ctab[:nq, :], scalar1=bs_[:nq, 0:1], scalar2=None,
                                                 op0=ALU.mult), r=[self.CB], w=[M8])
                self.dve(lambda: v.memset(cntb[:nq, :], 0.0), r=[], w=[M8])
                for it in range(NI):
                    self.dve(lambda it=it: v.tensor_tensor(out=bs_[:nq, 1:2], in0=thr[:nq, 0:1], in1=rtab[:nq, it:it + 1],
                                                           op=ALU.add), r=[], w=[M8])
                    self.dve(lambda it=it: v.tensor_scalar(out=mask[:nq, :W], in0=score[:nq, :W], scalar1=bs_[:nq, 1:2],
                                                           scalar2=0.0, op0=ALU.is_ge, op1=ALU.add,
                                                           accum_out=cntb[:nq, it:it + 1]), r=[SC], w=[M8, MK])
                    self.dve(lambda it=it: v.tensor_scalar(out=bs_[:nq, 2:3], in0=cntb[:nq, it:it + 1],
                                                           scalar1=float(topk) - 0.5, scalar2=rtab[:nq, it:it + 1],
                                                           op0=ALU.is_ge, op1=ALU.mult), r=[], w=[M8])
                    self.dve(lambda: v.tensor_tensor(out=thr[:nq, 0:1], in0=thr[:nq, 0:1], in1=bs_[:nq, 2:3], op=ALU.add),
                             r=[], w=[M8])
            else:
                if causal:
                    self.dve(lambda: v.memset(score[0:64, W - 64:W], NEG), r=[], w=[SC])
                self.dve(lambda: v.memset(thr[:nq, :], -1e29), r=[], w=[M8])

        def mask_tr(ti):
            si, sq, tok0, nq, W, causal = tiles[ti]
            pb = ti % 2
            score, SC = scores[pb], SCB[pb]
            nkt = (W + 127) // 128
            self.dve(lambda: v.tensor_scalar(out=mask[:nq, :W], in0=score[:nq, :W], scalar1=thr[:nq, 0:1],
                                             scalar2=None, op0=ALU.is_ge), r=[SC, M8], w=[MK])
            srcs = [mask[:nq, kt * 128:min(W, (kt + 1) * 128)] for kt in range(nkt)]
            for g0 in range(0, nkt, 8):
                g = srcs[g0:g0 + 8]
                bk, BK = self.bank_of([4, 5, 6, 7], "bs")
                bb = bk[:].bitcast(BF16)
                for j, s_ap in enumerate(g):
                    wd = s_ap.shape[1]
                    self.tr(bb[:wd, j * 128:j * 128 + nq], s_ap, self.identb[:nq, :nq], r=[MK, self.CB], w=[BK])
                src = bb[:, 0:len(g) * 128].rearrange("p (j t) -> p j t", t=128)[:, :, :nq]
                self.evac(g0 // 8, maskTs[pb][:, g0:g0 + len(g), :nq], src, r=[], w=[BK, MTB[pb]])

        def att(ti):
            si, sq, tok0, nq, W, causal = tiles[ti]
            pb = ti % 2
            nkt = (W + 127) // 128
            maskT, MT = maskTs[pb], MTB[pb]
            for g in range(DKV):
                ao, AO = self.bank_of([0, 1], "bo")
                ad, AD = self.bank_of([2, 3], "bd")
                acc[(ti, g)] = (ao, AO, ad, AD)
            steps = [(g, kt) for g in range(DKV) for kt in range(nkt)]
            pend = {}

            def front(g, kt):
                kr = min(128, W - kt * 128)
                sb_, SB_ = self.bank_of([4, 5, 6, 7], "bs")
                s3 = sb_[:kr, 0:4 * nq].rearrange("p (r q) -> p r q", q=nq)
                self.mm(s3, kd[:, g, kt * 128:kt * 128 + kr], qd[pb][:, 4 * g:4 * g + 4, :nq], True, True,
                        r=[SEQB, QB[pb]], w=[SB_])
                e_, E_ = eT[st["ecount"] % NEB], ET[st["ecount"] % NEB]
                p_, P_ = pT[st["ecount"] % NEB], PT[st["ecount"] % NEB]
                st["ecount"] += 1
                self.act(lambda: a.activation(out=e_[:kr, :4 * nq], in_=sb_[:kr, :4 * nq], func=AF.Exp, scale=scale),
                         r=[], w=[SB_, E_])
                e3 = e_[:kr, 0:4 * nq].rearrange("p (r q) -> p r q", q=nq)
                p3 = p_[:kr, 0:4 * nq].rearrange("p (r q) -> p r q", q=nq)
                mb = maskT[:kr, kt, :nq].unsqueeze(1).to_broadcast([kr, 4, nq])
                self.pool(lambda: gp.tensor_tensor(out=p3, in0=e3, in1=mb, op=ALU.mult), r=[E_, MT], w=[P_])
                pend[(g, kt)] = (p_, P_, kr)

            def back(g, kt):
                p_, P_, kr = pend.pop((g, kt))
                ao, AO, ad, AD = acc[(ti, g)]
                first, last = kt == 0, kt == nkt - 1
                self.mm(ao[:, :4 * nq], vd[:kr, kt, g * 128:(g + 1) * 128], p_[:kr, :4 * nq], first, last, r=[SEQB, P_], w=[AO])
                self.mm(ad[:, :4 * nq], self.onesb[:kr, :], p_[:kr, :4 * nq], first, last, r=[self.CB, P_], w=[AD])

            LA = 2
            for k_ in range(len(steps) + LA):
                if k_ < len(steps):
                    front(*steps[k_])
                if k_ >= LA:
                    back(*steps[k_ - LA])

        def norm(ti):
            si, sq, tok0, nq, W, causal = tiles[ti]
            for g in range(DKV):
                ao, AO, ad, AD = acc.pop((ti, g))
                o, OB = oT[st["ocount"] % 2], OT[st["ocount"] % 2]
                st["ocount"] += 1
                self.dve(lambda ad=ad: v.reciprocal(out=rden[:, :4 * nq], in_=ad[:, :4 * nq]), r=[], w=[AD, RD])
                self.dve(lambda o=o, ao=ao: v.tensor_tensor(out=o[:, :4 * nq], in0=ao[:, :4 * nq], in1=rden[:, :4 * nq],
                                                             op=ALU.mult), r=[RD], w=[AO, OB])
                self.dma(X["odT"][g * 512:(g + 1) * 512, tok0:tok0 + nq].rearrange("(r p) t -> p r t", p=128),
                         o[:, :4 * nq].rearrange("p (r q) -> p r q", q=nq), r=[OB], w=[self.XB["odT"]])

        n = len(tiles)
        loadq(tiles[0], 0)
        seq_load(tiles[0])
        if n > 1:
            loadq(tiles[1], 1)
        idx(0)
        bisect(0)
        mask_tr(0)
        for ti in range(n):
            nxt = ti + 1 if ti + 1 < n else None
            same = nxt is not None and tiles[nxt][0] == tiles[ti][0]
            if same:
                idx(nxt)
            att(ti)
            if same:
                bisect(nxt)
            norm(ti)
            if nxt is not None and not same:
                seq_load(tiles[nxt])
                idx(nxt)
                bisect(nxt)
            if nxt is not None:
                mask_tr(nxt)
                if ti + 2 < n:
                    loadq(tiles[ti + 2], ti % 2)
        S.release(m)

    def phaseD(self):
        S, I, O, X, W, c = self.S, self.I, self.O, self.X, self.W, self.cfg
        nc = self.nc
        v, a, gp = nc.vector, nc.scalar, nc.gpsimd
        m = S.mark()
        NG = 512
        NJ = D_FF // 128
        ring = [S.sb("wring", [128, 8192], BF16) for _ in range(3)]
        RG = [self.B() for _ in range(3)]
        io = [S.sb("ioD", [128, D_MODEL], F32) for _ in range(2)]
        IO = [self.B() for _ in range(2)]
        gff = S.sb("gff", [128, 16], F32)
        cw = S.sb("cw", [128, 3, NJ], F32)
        cb = S.sb("cb", [128, NJ], F32)
        self.dma(gff[:], I["ffn_normT"][:, :], w=[self.CB])
        self.dma(cw[:], I["conv_wT"][:, :, :], w=[self.CB])
        self.dma(cb[:], I["conv_bT"][:, :], w=[self.CB])
        rc = [0]
        ioc = [0]

        def slot():
            i = rc[0] % 3
            rc[0] += 1
            return ring[i], RG[i]

        class Ctx:
            pass

        def mk(ngmax, nseqmax, tag):
            x = Ctx()
            x.xT = S.sb("xT" + tag, [128, 16, ngmax], F32)
            x.XT = [self.B() for _ in range(16)]
            x.r2 = S.sb("r2" + tag, [128, 16, ngmax], BF16)
            x.R2 = [self.B() for _ in range(16)]
            x.r1 = S.sb("r1" + tag, [128, NJ, ngmax], BF16)
            x.R1 = [self.B() for _ in range(NJ)]
            x.omT, x.odT, x.mT = x.r1[:, 0:8, :], x.r1[:, 8:16, :], x.r1[:, 16:32, :]
            x.sg = [S.sb("sg" + tag, [128, ngmax], F32) for _ in range(2)]
            x.SG = [self.B() for _ in range(2)]
            x.t12 = [S.sb("t12" + tag, [128, ngmax], F32) for _ in range(2)]
            x.T12 = [self.B() for _ in range(2)]
            x.rstd = S.sb("rstdD" + tag, [128, ngmax], F32)
            x.RS = self.B()
            x.sqb = [S.sb("sqbD" + tag, [128, ngmax], BF16) for _ in range(2)]
            x.SQ = [self.B() for _ in range(2)]
            x.gpx = [S.sb("gpx" + tag, [128, ngmax + 2 * nseqmax], F32) for _ in range(2)]
            x.GP = [self.B() for _ in range(2)]
            x.tt = [S.sb("ttD" + tag, [128, ngmax], F32) for _ in range(2)]
            x.TT_ = [self.B() for _ in range(2)]
            x.sl = [S.sb("slD" + tag, [128, ngmax], F32) for _ in range(2)]
            x.SL = [self.B() for _ in range(2)]
            x.carry = S.sb("carry" + tag, [128, NJ, 2 * nseqmax], F32)
            x.CY = [self.B() for _ in range(NJ)]
            return x

        P = mk(NG, 1, "p")
        Q = mk(c.TS, c.NS, "s")
        for j in range(NJ):
            self.pool(lambda j=j: gp.memset(P.carry[:, j, :], 0.0), w=[P.CY[j]])

        def setg(x, tok0, ng, nseq, L, smp):
            x.tok0, x.ng, x.nseq, x.L, x.smp = tok0, ng, nseq, L, smp
            x.last = smp or tok0 + ng == c.T

        def load_group(x):
            tok0, ng = x.tok0, x.ng
            if x.smp:
                for j in range(NJ):
                    self.dma(x.carry[:, j, 0:2 * x.nseq].rearrange("p (s r) -> p s r", r=2),
                             I["c_conv"][:, :, j * 128:(j + 1) * 128].rearrange("s r p -> p s r"), w=[x.CY[j]], slow=True)
            self.dma(x.r2[:, :, :ng], X["hT"][:, tok0:tok0 + ng].rearrange("(k p) t -> p k t", p=128), r=[self.XB["hT"]], w=x.R2)
            self.dma(x.omT[:, :, :ng], X["omT"][:, tok0:tok0 + ng].rearrange("(k p) t -> p k t", p=128), r=[self.XB["omT"]],
                     w=x.R1[0:8])
            self.dma(x.odT[:, :, :ng], X["odT"][:, tok0:tok0 + ng].rearrange("(k p) t -> p k t", p=128), r=[self.XB["odT"]],
                     w=x.R1[8:16])
            for q0 in range(0, ng, 128):
                nr = min(128, ng - q0)
                xi, XI = io[ioc[0] % 2], IO[ioc[0] % 2]
                ioc[0] += 1
                src = I["xs"][:, :] if x.smp else I["xp"][tok0 + q0:tok0 + q0 + nr, :]
                self.dma(xi[:nr, :], src, w=[XI])
                for k4 in range(4):
                    bk, BK = self.bank()
                    for kk in range(4):
                        k = 4 * k4 + kk
                        self.tr(bk[:, kk * 128:kk * 128 + nr], xi[:nr, k * 128:(k + 1) * 128], self.identf[:nr, :nr],
                                r=[XI, self.CB], w=[BK])
                    self.evac(k4, x.xT[:, 4 * k4:4 * k4 + 4, q0:q0 + nr],
                              bk[:, 0:512].rearrange("p (j t) -> p j t", t=128)[:, :, :nr], r=[], w=[BK] + x.XT[4 * k4:4 * k4 + 4])

        def merge_chunk(x, cc, SLB, wom, wod, wgm, wgd):
            ng = x.ng
            sg, SG, t12, T12 = x.sg, x.SG, x.t12, x.T12
            ba, BA = self.bank()
            for k in range(8):
                self.mm(ba[:, :ng], wom[:, k, :], x.omT[:, k, :ng], k == 0, k == 7, r=[SLB] + x.R1[0:8], w=[BA])
            bb, BB = self.bank()
            for k in range(8):
                self.mm(bb[:, :ng], wod[:, k, :], x.odT[:, k, :ng], k == 0, k == 7, r=[SLB] + x.R1[8:16], w=[BB])
            bgm, BGM = self.bank()
            for k in range(16):
                self.mm(bgm[:, :ng], wgm[:, k, :], x.r2[:, k, :ng], k == 0, k == 15, r=[SLB] + x.R2, w=[BGM])
            bgd, BGD = self.bank()
            for k in range(16):
                self.mm(bgd[:, :ng], wgd[:, k, :], x.r2[:, k, :ng], k == 0, k == 15, r=[SLB] + x.R2, w=[BGD])
            yield
            self.act(lambda: a.activation(out=sg[0][:, :ng], in_=bgm[:, :ng], func=AF.Sigmoid), w=[BGM, SG[0]])
            yield
            self.act(lambda: a.activation(out=sg[1][:, :ng], in_=bgd[:, :ng], func=AF.Sigmoid), w=[BGD, SG[1]])
            yield
            self.dve(lambda: v.tensor_tensor(out=t12[0][:, :ng], in0=sg[0][:, :ng], in1=ba[:, :ng], op=ALU.mult),
                     r=[SG[0]], w=[BA, T12[0]])
            yield
            self.dve(lambda: v.tensor_tensor(out=t12[1][:, :ng], in0=sg[1][:, :ng], in1=bb[:, :ng], op=ALU.mult),
                     r=[SG[1]], w=[BB, T12[1]])
            yield
            self.pool(lambda: gp.tensor_tensor(out=x.mT[:, cc, :ng], in0=t12[0][:, :ng], in1=t12[1][:, :ng], op=ALU.add),
                      r=T12, w=[x.R1[16 + cc]])
            yield

        def wout_chunk(x, cc, SLB, wo, ci):
            ng = x.ng
            bk, BK = self.bank()
            for k in range(16):
                self.mm(bk[:, :ng], wo[:, k, ci * 128:(ci + 1) * 128], x.mT[:, k, :ng], k == 0, k == 15,
                        r=[SLB] + x.R1[16:32], w=[BK])
            self.dve(lambda: v.tensor_tensor(out=x.xT[:, cc, :ng], in0=x.xT[:, cc, :ng], in1=bk[:, :ng], op=ALU.add),
                     r=[], w=[BK, x.XT[cc]])

        def rms_stage(x):
            ng = x.ng
            bs, BS = self.bank()
            for cc in range(16):
                q_, Q_ = x.sqb[cc % 2], x.SQ[cc % 2]
                self.act(lambda q_=q_, cc=cc: a.activation(out=q_[:, :ng], in_=x.xT[:, cc, :ng], func=AF.Square),
                         r=[x.XT[cc]], w=[Q_])
                self.mm(bs[:, :ng], self.onesb[:, :], q_[:, :ng], cc == 0, cc == 15, r=[Q_, self.CB], w=[BS])
            self.act(lambda: a.activation(out=x.rstd[:, :ng], in_=bs[:, :ng], func=AF.Sqrt, scale=1.0 / D_MODEL,
                                          bias=self.epsc[:, 0:1]), r=[self.CB], w=[BS, x.RS])
            self.dve(lambda: v.reciprocal(out=x.rstd[:, :ng], in_=x.rstd[:, :ng]), w=[x.RS])
            for cc in range(16):
                self.dve(lambda cc=cc: v.scalar_tensor_tensor(out=x.r2[:, cc, :ng], in0=x.xT[:, cc, :ng], scalar=gff[:, cc:cc + 1],
                                                              in1=x.rstd[:, :ng], op0=ALU.mult, op1=ALU.mult),
                         r=[x.XT[cc], x.RS, self.CB], w=[x.R2[cc]])

        def up_chunk(x, j, SLB, wg, wu):
            ng, nseq, L = x.ng, x.nseq, x.L
            bg, BG = self.bank()
            for k in range(16):
                self.mm(bg[:, :ng], wg[:, k, :], x.r2[:, k, :ng], k == 0, k == 15, r=[SLB] + x.R2, w=[BG])
            bu, BU = self.bank()
            for k in range(16):
                self.mm(bu[:, :ng], wu[:, k, :], x.r2[:, k, :ng], k == 0, k == 15, r=[SLB] + x.R2, w=[BU])
            pb = j % 2
            GPB, TTB, SLB_ = x.GP[pb], x.TT_[pb], x.SL[pb]
            g3 = x.gpx[pb][:, 0:nseq * (L + 2)].rearrange("p (s l) -> p s l", l=L + 2)
            bg3 = bg[:, 0:ng].rearrange("p (s l) -> p s l", l=L)
            t3 = x.tt[pb][:, 0:ng].rearrange("p (s l) -> p s l", l=L)
            cyv = x.carry[:, j, 0:2 * nseq].rearrange("p (s r) -> p s r", r=2)
            yield
            self.pool(lambda: gp.tensor_copy(out=g3[:, :, 0:2], in_=cyv), r=[x.CY[j]], w=[GPB])
            self.act(lambda: a.activation(out=g3[:, :, 2:L + 2], in_=bg3, func=AF.Copy), r=[], w=[BG, GPB])
            yield
            self.act(lambda: a.activation(out=t3, in_=bg3, func=AF.Identity, scale=cw[:, 2, j:j + 1], bias=cb[:, j:j + 1]),
                     r=[self.CB], w=[BG, TTB])
            yield
            self.dve(lambda: v.scalar_tensor_tensor(out=t3, in0=g3[:, :, 1:L + 1], scalar=cw[:, 1, j:j + 1], in1=t3,
                                                    op0=ALU.mult, op1=ALU.add), r=[GPB, self.CB], w=[TTB])
            yield
            self.dve(lambda: v.scalar_tensor_tensor(out=t3, in0=g3[:, :, 0:L], scalar=cw[:, 0, j:j + 1], in1=t3,
                                                    op0=ALU.mult, op1=ALU.add), r=[GPB, self.CB], w=[TTB])
            yield
            self.act(lambda: a.activation(out=x.sl[pb][:, :ng], in_=x.tt[pb][:, :ng], func=AF.Silu), r=[TTB], w=[SLB_])
            yield
            self.dve(lambda: v.tensor_tensor(out=x.r1[:, j, :ng], in0=x.sl[pb][:, :ng], in1=bu[:, :ng], op=ALU.mult),
                     r=[SLB_], w=[BU, x.R1[j]])
            yield
            self.pool(lambda: gp.tensor_copy(out=cyv, in_=g3[:, :, L:L + 2]), r=[GPB], w=[x.CY[j]])

        def conv_state_out(x):
            for j in range(NJ):
                if x.smp:
                    for s_ in range(x.nseq):
                        self.dma(O["s_conv"][s_, :, j * 128:(j + 1) * 128].rearrange("r p -> p r"), x.carry[:, j, 2 * s_:2 * s_ + 2],
                                 r=[x.CY[j]], slow=True, q="pool")
                else:
                    self.dma(O["p_conv"][:, j * 128:(j + 1) * 128].rearrange("r p -> p r"), x.carry[:, j, 0:2], r=[x.CY[j]],
                             slow=True, q="pool")

        def down_chunk(x, cc, SLB, wd):
            ng = x.ng
            bk, BK = self.bank()
            for k in range(NJ):
                self.mm(bk[:, :ng], wd[:, k, :], x.r1[:, k, :ng], k == 0, k == NJ - 1, r=[SLB] + x.R1, w=[BK])
            self.dve(lambda: v.tensor_tensor(out=x.xT[:, cc, :ng], in0=x.xT[:, cc, :ng], in1=bk[:, :ng], op=ALU.add),
                     r=[], w=[BK, x.XT[cc]])

        def out_stage(x):
            tok0, ng = x.tok0, x.ng
            for q0 in range(0, ng, 128):
                nr = min(128, ng - q0)
                yo, YO = io[ioc[0] % 2], IO[ioc[0] % 2]
                ioc[0] += 1
                for k4 in range(4):
                    bk, BK = self.bank()
                    for kk in range(4):
                        k = 4 * k4 + kk
                        self.tr(bk[:nr, kk * 128:(kk + 1) * 128], x.xT[:, k, q0:q0 + nr], self.identf[:, :], r=[x.XT[k], self.CB], w=[BK])
                    self.evac(k4, yo[:nr, k4 * 512:(k4 + 1) * 512], bk[:nr, :], r=[], w=[BK, YO])
                dst = O["y_s"][:, :] if x.smp else O["y_p"][tok0 + q0:tok0 + q0 + nr, :]
                self.dma(dst, yo[:nr, :], r=[YO])

        pg = [(g0, min(NG, c.T - g0)) for g0 in range(0, c.T, NG)]
        for gi, (tok0, ng) in enumerate(pg):
            setg(P, tok0, ng, 1, ng, False)
            unit = [P]
            if gi == len(pg) - 1:
                setg(Q, c.T, c.TS, c.NS, c.DEC, True)
                unit.append(Q)
            for x in unit:
                load_group(x)
            for cc in range(16):
                sl_, SLB = slot()
                wom = sl_[:, 0:1024].rearrange("p (k n) -> p k n", n=128)
                wod = sl_[:, 1024:2048].rearrange("p (k n) -> p k n", n=128)
                wgm = sl_[:, 2048:4096].rearrange("p (k n) -> p k n", n=128)
                wgd = sl_[:, 4096:6144].rearrange("p (k n) -> p k n", n=128)
                cs = slice(cc * 128, (cc + 1) * 128)
                self.dma(wom, W["w_o_mla"][:, cs].rearrange("(k p) n -> p k n", p=128), r=[self.WB["w_o_mla"]], w=[SLB])
                self.dma(wod, W["w_o_dsa"][:, cs].rearrange("(k p) n -> p k n", p=128), r=[self.WB["w_o_dsa"]], w=[SLB])
                self.dma(wgm, W["w_in"][:, NA + cc * 128:NA + (cc + 1) * 128].rearrange("(k p) n -> p k n", p=128),
                         r=[self.WB["w_in"]], w=[SLB])
                self.dma(wgd, W["w_in"][:, NA + D_MODEL + cc * 128:NA + D_MODEL + (cc + 1) * 128]
                         .rearrange("(k p) n -> p k n", p=128), r=[self.WB["w_in"]], w=[SLB])
                self.run(*[merge_chunk(x, cc, SLB, wom, wod, wgm, wgd) for x in unit])
            for c4 in range(4):
                sl_, SLB = slot()
                wo = sl_[:, 0:8192].rearrange("p (k n) -> p k n", n=512)
                self.dma(wo, W["w_out"][:, c4 * 512:(c4 + 1) * 512].rearrange("(k p) n -> p k n", p=128),
                         r=[self.WB["w_out"]], w=[SLB])
                for ci in range(4):
                    for x in unit:
                        wout_chunk(x, 4 * c4 + ci, SLB, wo, ci)
            for x in unit:
                rms_stage(x)
            for j in range(NJ):
                sl_, SLB = slot()
                wg = sl_[:, 0:2048].rearrange("p (k n) -> p k n", n=128)
                wu = sl_[:, 2048:4096].rearrange("p (k n) -> p k n", n=128)
                self.dma(wg, W["w_ffn_up"][:, j * 128:(j + 1) * 128].rearrange("(k p) n -> p k n", p=128),
                         r=[self.WB["w_ffn_up"]], w=[SLB])
                self.dma(wu, W["w_ffn_up"][:, D_FF + j * 128:D_FF + (j + 1) * 128].rearrange("(k p) n -> p k n", p=128),
                         r=[self.WB["w_ffn_up"]], w=[SLB])
                self.run(*[up_chunk(x, j, SLB, wg, wu) for x in unit])
            for cc in range(16):
                sl_, SLB = slot()
                wd = sl_[:, 0:NJ * 128].rearrange("p (k n) -> p k n", n=128)
                self.dma(wd, W["w_ffn_down"][:, cc * 128:(cc + 1) * 128].rearrange("(k p) n -> p k n", p=128),
                         r=[self.WB["w_ffn_down"]], w=[SLB])
                for x in unit:
                    down_chunk(x, cc, SLB, wd)
            for x in unit:
                out_stage(x)
            for x in unit:
                if x.last:
                    conv_state_out(x)
        S.release(m)

    def finish(self):
        self.S.emit()
        return self.nc


PHASES = ["P0", "A1", "A2", "B", "C", "D"]


def build(cfg, upto="D", debug=False):
    b = Builder(cfg, debug=debug)
    b.declare()
    b.setup_consts()
    b.phase0()
    for ph in PHASES[1:PHASES.index(upto) + 1]:
        getattr(b, "phase" + ph)()
    nc = b.finish()
    return b, nc


def rope_table(n):
    out = np.zeros((n, 2, 56), np.float32)
    pos = np.arange(n, dtype=np.float32)
    c0 = 0
    for half in (32, 16, 8):
        inv = (np.float32(500000.0) ** (-(np.arange(half, dtype=np.float32) / np.float32(half)))).astype(np.float32)
        ang = (pos[:, None] * inv[None, :]).astype(np.float32)
        out[:, 0, c0:c0 + half] = np.cos(ang)
        out[:, 1, c0:c0 + half] = np.sin(ang)
        c0 += half
    return out


def core_inputs(cfg, inp, core):
    NS = cfg.NS
    sl = slice(core * NS, (core + 1) * NS)
    f = lambda a: np.ascontiguousarray(a, dtype=np.float32)
    d = {
        "xp": f(inp["x_prompt"][core]),
        "xs": f(inp["x_sample"][sl].reshape(cfg.TS, D_MODEL)),
        "c_ckv": f(inp["cache_mla_ckv"][0, sl]),
        "c_kpe": f(inp["cache_mla_kpe"][0, sl]),
        "c_dk": f(inp["cache_dsa_k"][0, sl].reshape(NS, cfg.PAST, DKV * DHD)),
        "c_dv": f(inp["cache_dsa_v"][0, sl].reshape(NS, cfg.PAST, DKV * DHD)),
        "c_ik": f(inp["cache_idx_k"][0, sl]),
        "c_conv": f(inp["state_ffn_conv"][0, sl]),
        "ffn_normT": f(inp["ffn_norm"][0].reshape(D_MODEL // 128, 128).T),
        "conv_wT": f(inp["conv_w"][0].reshape(3, D_FF // 128, 128).transpose(2, 0, 1)),
        "conv_bT": f(inp["conv_b"][0].reshape(D_FF // 128, 128).T),
        "ident": np.eye(128, dtype=np.float32),
        "ropet": rope_table(max(cfg.T, cfg.PAST + cfg.DEC)),
    }
    for nm in ("attn_norm", "q_a_norm", "kv_a_norm", "mla_q_nope_norm", "mla_q_rope_norm", "mla_k_nope_norm",
               "mla_k_rope_norm", "dsa_q_norm", "dsa_k_norm"):
        d[nm] = f(inp[nm][0:1])
    for nm in ("w_in", "w_q_up", "w_kv_up", "w_o_mla", "w_o_dsa", "w_out", "w_ffn_up", "w_ffn_down"):
        d[nm] = f(inp[nm][0])
    return d


def run_core_debug(cfg, inp, upto):
    b, nc = build(cfg, upto, debug=True)
    print("stats", b.S.stats, flush=True)
    res = run_bass_kernel_spmd(nc, [core_inputs(cfg, inp, 0)], core_ids=[0]).results[0]
    outs = {k: res[k] for k in b.O}
    dbg = {k: res[v.tensor.name] if hasattr(v, "tensor") else None for k, v in b.X.items()}
    return outs, dbg


def kernel(**inputs):
    cfg = Cfg()
    inp = {k: np.asarray(v) for k, v in inputs.items()}
    b, nc = build(cfg, "D", debug=False)
    in_maps = [core_inputs(cfg, inp, core) for core in range(8)]
    res = run_bass_kernel_spmd(nc, in_maps, core_ids=list(range(8))).results
    g = lambda k: [np.asarray(r[k], dtype=np.float32) for r in res]
    NS, DEC, T = cfg.NS, cfg.DEC, cfg.T
    y_p = np.stack(g("y_p"), 0)
    y_s = np.concatenate([a.reshape(NS, DEC, D_MODEL) for a in g("y_s")], 0)
    p_ckv = np.stack(g("p_ckv"), 0)[None]
    p_kpe = np.stack(g("p_kpe"), 0)[None]
    p_dk = np.stack(g("p_dk"), 0).reshape(1, 8, T, DKV, DHD)
    p_dv = np.stack(g("p_dv"), 0).reshape(1, 8, T, DKV, DHD)
    p_ik = np.stack(g("p_ik"), 0)[None]
    p_conv = np.stack(g("p_conv"), 0)[None]
    cat = lambda k, *shp: np.concatenate([a.reshape((NS, DEC) + shp) for a in g(k)], 0)[None]
    s_ckv = cat("s_ckv", KV_LORA)
    s_kpe = cat("s_kpe", ROPE)
    s_dk = cat("s_dk", DKV, DHD)
    s_dv = cat("s_dv", DKV, DHD)
    s_ik = cat("s_ik", IDIM)
    s_conv = np.concatenate(g("s_conv"), 0)[None]
    return (y_p, y_s, p_ckv, p_kpe, p_dk, p_dv, p_ik, p_conv, s_ckv, s_kpe, s_dk, s_dv, s_ik, s_conv)
```

```python
import numpy as np
import concourse.bass as bass
import concourse.mybir as mybir
from concourse.bass_utils import run_bass_kernel_spmd

F32 = mybir.dt.float32
BF16 = mybir.dt.bfloat16
AF = mybir.ActivationFunctionType
ALU = mybir.AluOpType
AX = mybir.AxisListType

D_MODEL = 2048
CHUNK = 64
EPS = 1e-6
NEG = -1e30
Q_LORA, KV_LORA = 512, 256
NOPE, ROPE, MV, MH = 128, 64, 128, 8
DH, DKV, DHD, DROT = 8, 2, 128, 32
IH, IDIM, IROT = 16, 64, 16
TOPK_MAX = 256
D_FF = 5632
NA1 = 1856
NA2 = 1616
NA = NA1 + NA2
IN_COLS = NA + 2 * D_MODEL


class Cfg:
    def __init__(self, T=4096, NS=4, PAST=1024, DEC=16):
        self.T, self.NS, self.PAST, self.DEC = T, NS, PAST, DEC
        self.NT = T // 128
        self.TS = NS * DEC
        self.TT = T + self.TS
        self.SS = PAST + DEC
        self.KT = T + NS * self.SS


class Buf:
    __slots__ = ("name", "w", "r")

    def __init__(self, name):
        self.name, self.w, self.r = name, None, []


class Op:
    __slots__ = ("eng", "fn", "deps", "dma")

    def __init__(self, eng, fn, deps, dma):
        self.eng, self.fn, self.deps, self.dma = eng, fn, deps, dma


class Sched:
    ND = {"sp": 8, "pool": 8, "act": 4}

    def __init__(self, nc):
        self.nc = nc
        self.ops = []
        self.eng = {"pe": nc.tensor, "act": nc.scalar, "dve": nc.vector, "pool": nc.gpsimd, "sp": nc.sync}
        self.sb_off = 16512
        self.sb_end = 229376

    def mark(self):
        return self.sb_off

    def release(self, m):
        self.sb_off = m

    def sb(self, name, shape, dtype):
        nbytes = int(np.prod(shape[1:])) * (4 if dtype == F32 else 2)
        off = (self.sb_off + 63) // 64 * 64
        assert off + nbytes <= self.sb_end, f"SBUF overflow at {name}: {off}+{nbytes}"
        self.sb_off = off + nbytes
        self._n = getattr(self, "_n", 0) + 1
        return self.nc.alloc_sbuf_tensor_at(f"{name}_{self._n}", list(shape), dtype, offset=off)

    def op(self, eng, fn, reads=(), writes=(), dma=False):
        deps = set()
        for b in reads:
            if b.w is not None:
                deps.add(b.w)
        for b in writes:
            if b.w is not None:
                deps.add(b.w)
            deps.update(b.r)
        i = len(self.ops)
        self.ops.append(Op(eng, fn, deps, dma))
        for b in reads:
            b.r.append(i)
        for b in writes:
            b.w = i
            b.r = []
        return i

    def dma(self, q, out, in_, reads=(), writes=(), slow=False):
        e = self.eng[q]
        if slow:
            return self.op(q, lambda: e.dma_start(out=out, in_=in_, allow_slow_non_contiguous=True), reads, writes, dma=True)
        return self.op(q, lambda: e.dma_start(out=out, in_=in_), reads, writes, dma=True)

    def emit(self):
        nc, ops = self.nc, self.ops
        n = len(ops)
        need = [False] * n
        for o in ops:
            for d in o.deps:
                p = ops[d]
                if p.dma:
                    continue
                if p.eng == "pe" and o.eng == "pe" and not o.dma:
                    continue
                need[d] = True
        sems = {}

        def sem(key):
            if key not in sems:
                sems[key] = nc.alloc_semaphore("s_" + "_".join(str(k) for k in key))
            return sems[key]

        cnt, dcnt, tok = {}, {}, [None] * n
        for i, o in enumerate(ops):
            if o.dma:
                k = dcnt.get(o.eng, 0)
                dcnt[o.eng] = k + 1
                nd = self.ND[o.eng]
                tok[i] = (("d", o.eng, k % nd), 16 * (k // nd + 1))
            elif need[i]:
                cnt[o.eng] = cnt.get(o.eng, 0) + 1
                tok[i] = (("c", o.eng), cnt[o.eng])
        seen = {e: {} for e in self.eng}
        nwaits = 0
        for i, o in enumerate(ops):
            E = o.eng
            waits = {}
            for d in o.deps:
                p = ops[d]
                if (not p.dma) and p.eng == "pe" and E == "pe" and not o.dma:
                    continue
                s, v = tok[d]
                if waits.get(s, 0) < v:
                    waits[s] = v
            if o.dma:
                s, v = tok[i]
                if v > 16 and waits.get(s, 0) < v - 16:
                    waits[s] = v - 16
            for s, v in waits.items():
                if seen[E].get(s, 0) >= v:
                    continue
                seen[E][s] = v
                self.eng[E].wait_ge(sem(s), v)
                nwaits += 1
            ins = o.fn()
            if tok[i] is not None:
                ins.then_inc(sem(tok[i][0]), 16 if o.dma else 1)
        for q, k in dcnt.items():
            nd = self.ND[q]
            for slot in range(min(nd, k)):
                last = (k - 1 - slot) // nd * nd + slot
                v = 16 * (last // nd + 1)
                if seen["sp"].get(("d", q, slot), 0) < v:
                    nc.sync.wait_ge(sem(("d", q, slot)), v)
        for e, c in cnt.items():
            nc.sync.wait_ge(sem(("c", e)), c)
        self.stats = dict(n_ops=n, n_waits=nwaits, n_sems=len(sems))


class Builder:
    def __init__(self, cfg, debug=False):
        self.cfg = cfg
        self.debug = debug
        self.nc = bass.Bass("TRN2", target_bir_lowering=False)
        self.S = Sched(self.nc)
        self._bufn = 0

    def B(self, name="b"):
        self._bufn += 1
        return Buf(f"{name}{self._bufn}")

    def din(self, name, shape, dt=F32):
        return self.nc.dram_tensor(name, list(shape), dt, kind="ExternalInput").ap()

    def dout(self, name, shape, dt=F32):
        return self.nc.dram_tensor(name, list(shape), dt, kind="ExternalOutput").ap()

    def dscr(self, name, shape, dt=BF16):
        kind = "ExternalOutput" if (self.debug and not name.startswith("w_")) else "Internal"
        return self.nc.dram_tensor(name, list(shape), dt, kind=kind).ap()

    def pe(self, fn, r=(), w=()):
        return self.S.op("pe", fn, r, w)

    def act(self, fn, r=(), w=()):
        return self.S.op("act", fn, r, w)

    def dve(self, fn, r=(), w=()):
        return self.S.op("dve", fn, r, w)

    def pool(self, fn, r=(), w=()):
        return self.S.op("pool", fn, r, w)

    def dma(self, out, in_, r=(), w=(), q="sp", slow=False):
        return self.S.dma(q, out, in_, r, w, slow=slow)

    def mm(self, out, lhsT, rhs, start, stop, r=(), w=()):
        t = self.nc.tensor
        return self.pe(lambda: t.matmul(out, lhsT=lhsT, rhs=rhs, start=start, stop=stop), r, w)

    def tr(self, out, in_, ident, r=(), w=()):
        t = self.nc.tensor
        return self.pe(lambda: t.transpose(out, in_, ident), r, w)

    def declare(self):
        c = self.cfg
        T, NS, PAST, TS, TT, KT = c.T, c.NS, c.PAST, c.TS, c.TT, c.KT
        I = {}
        I["xp"] = self.din("xp", [T, D_MODEL])
        I["xs"] = self.din("xs", [TS, D_MODEL])
        I["c_ckv"] = self.din("c_ckv", [NS, PAST, KV_LORA])
        I["c_kpe"] = self.din("c_kpe", [NS, PAST, ROPE])
        I["c_dk"] = self.din("c_dk", [NS, PAST, DKV * DHD])
        I["c_dv"] = self.din("c_dv", [NS, PAST, DKV * DHD])
        I["c_ik"] = self.din("c_ik", [NS, PAST, IDIM])
        I["c_conv"] = self.din("c_conv", [NS, 2, D_FF])
        I["attn_norm"] = self.din("attn_norm", [1, D_MODEL])
        I["w_in"] = self.din("w_in", [D_MODEL, IN_COLS])
        I["q_a_norm"] = self.din("q_a_norm", [1, Q_LORA])
        I["w_q_up"] = self.din("w_q_up", [Q_LORA, MH * (NOPE + ROPE)])
        I["kv_a_norm"] = self.din("kv_a_norm", [1, KV_LORA])
        I["w_kv_up"] = self.din("w_kv_up", [KV_LORA, MH * (NOPE + MV)])
        for nm, d in (("mla_q_nope_norm", NOPE), ("mla_q_rope_norm", ROPE), ("mla_k_nope_norm", NOPE),
                      ("mla_k_rope_norm", ROPE), ("dsa_q_norm", DHD), ("dsa_k_norm", DHD)):
            I[nm] = self.din(nm, [1, d])
        I["w_o_mla"] = self.din("w_o_mla", [MH * MV, D_MODEL])
        I["w_o_dsa"] = self.din("w_o_dsa", [DH * DHD, D_MODEL])
        I["w_out"] = self.din("w_out", [D_MODEL, D_MODEL])
        I["ffn_normT"] = self.din("ffn_normT", [128, D_MODEL // 128])
        I["w_ffn_up"] = self.din("w_ffn_up", [D_MODEL, 2 * D_FF])
        I["conv_wT"] = self.din("conv_wT", [128, 3, D_FF // 128])
        I["conv_bT"] = self.din("conv_bT", [128, D_FF // 128])
        I["w_ffn_down"] = self.din("w_ffn_down", [D_FF, D_MODEL])
        I["ident"] = self.din("ident", [128, 128])
        I["ropet"] = self.din("ropet", [max(T, PAST + c.DEC), 2, 56])
        self.I = I
        O = {}
        O["y_p"] = self.dout("y_p", [T, D_MODEL])
        O["y_s"] = self.dout("y_s", [TS, D_MODEL])
        O["p_ckv"] = self.dout("p_ckv", [T, KV_LORA])
        O["p_kpe"] = self.dout("p_kpe", [T, ROPE])
        O["p_dk"] = self.dout("p_dk", [T, DKV * DHD])
        O["p_dv"] = self.dout("p_dv", [T, DKV * DHD])
        O["p_ik"] = self.dout("p_ik", [T, IDIM])
        O["p_conv"] = self.dout("p_conv", [2, D_FF])
        O["s_ckv"] = self.dout("s_ckv", [TS, KV_LORA])
        O["s_kpe"] = self.dout("s_kpe", [TS, ROPE])
        O["s_dk"] = self.dout("s_dk", [TS, DKV * DHD])
        O["s_dv"] = self.dout("s_dv", [TS, DKV * DHD])
        O["s_ik"] = self.dout("s_ik", [TS, IDIM])
        O["s_conv"] = self.dout("s_conv", [NS, 2, D_FF])
        self.O = O
        W = {}
        for nm in ("w_in", "w_q_up", "w_kv_up", "w_o_mla", "w_o_dsa", "w_out", "w_ffn_up", "w_ffn_down"):
            W[nm] = self.dscr(nm + "_b", I[nm].shape)
        self.W = W
        X = {}
        X["hT"] = self.dscr("hT_s", [D_MODEL, TT])
        X["qnT"] = self.dscr("qnT_s", [MH * NOPE, TT])
        X["qrT"] = self.dscr("qrT_s", [MH * ROPE, TT])
        X["knT"] = self.dscr("knT_s", [MH * NOPE, KT])
        X["kpeT"] = self.dscr("kpeT_s", [ROPE, KT])
        X["v"] = self.dscr("v_s", [KT, MH * MV])
        X["qdT"] = self.dscr("qdT_s", [DH * DHD, TT])
        X["kdT"] = self.dscr("kdT_s", [DKV * DHD, KT])
        X["vd"] = self.dscr("vd_s", [KT, DKV * DHD])
        X["qiT"] = self.dscr("qiT_s", [IH * IDIM, TT])
        X["kiT"] = self.dscr("kiT_s", [IDIM, KT])
        X["wi"] = self.dscr("wi_s", [TT, 2, IH], F32)
        X["omT"] = self.dscr("omT_s", [MH * MV, TT])
        X["odT"] = self.dscr("odT_s", [DH * DHD, TT])
        self.X = X
        self.XB = {k: self.B("x_" + k) for k in X}
        self.WB = {k: self.B("w_" + k) for k in W}
        if self.debug:
            self.DBG = {}

    def token_tiles(self):
        c = self.cfg
        tiles = [(i * 128, 128, False) for i in range(c.NT)]
        tiles.append((c.T, c.TS, True))
        return tiles

    def setup_consts(self):
        S, I, c = self.S, self.I, self.cfg
        self.psum = [self.nc.alloc_psum_tensor(f"ps{i}", [128, 512], F32) for i in range(8)]
        self.PB = [self.B(f"psb{i}") for i in range(8)]
        self.identf = S.sb("identf", [128, 128], F32)
        self.identb = S.sb("identb", [128, 128], BF16)
        self.onesb = S.sb("onesb", [128, 128], BF16)
        self.CB = self.B("consts")
        self.dma(self.identf[:], I["ident"][:, :], w=[self.CB])
        v = self.nc.vector
        self.dve(lambda: v.tensor_copy(out=self.identb[:], in_=self.identf[:]), r=[self.CB], w=[self.CB])
        self.dve(lambda: v.memset(self.onesb[:], 1.0), w=[self.CB])
        self.NI = 16
        self.ctab = S.sb("ctab", [128, self.NI], F32)
        for i in range(self.NI):
            self.dve(lambda i=i: v.memset(self.ctab[:, i:i + 1], float(2.0 ** -(i + 1))), w=[self.CB])
        self.epsc = S.sb("epsc", [128, 1], F32)
        self.dve(lambda: v.memset(self.epsc[:], EPS), w=[self.CB])

    def rep_gain(self, name, d):
        t = self.S.sb("g_" + name, [128, d], F32)
        self.dma(t[:], self.I[name][0:1, :].partition_broadcast(128), w=[self.CB])
        return t

    EARLY_W = ("w_in", "w_q_up", "w_kv_up")

    def cast_weights(self, names):
        for nm in names:
            w, src = self.W[nm], self.I[nm]
            rows = src.shape[0]
            for r0 in range(0, rows, 128):
                self.dma(w[r0:r0 + 128, :], src[r0:r0 + 128, :], w=[self.WB[nm]], q="pool")

    def phase0(self):
        self.cast_weights(self.EARLY_W)

    def rms_gen(self, nr, src, H, D, gain, dst, sq, ss, rB, wB, tmpB):
        v = self.nc.vector
        a = self.nc.scalar
        sqv = sq[:nr, 0:H * D].rearrange("p (h d) -> p h d", d=D)
        self.dve(lambda: v.tensor_tensor(out=sqv, in0=src, in1=src, op=ALU.mult), r=rB, w=[tmpB])
        yield
        ssv = ss[:nr, 0:H]
        self.dve(lambda: v.tensor_reduce(out=ssv, in_=sqv, axis=AX.X, op=ALU.add), r=[tmpB], w=[tmpB])
        yield
        self.act(lambda: a.activation(out=ssv, in_=ssv, func=AF.Sqrt, scale=1.0 / D, bias=self.epsc[:nr, 0:1]),
                 r=[tmpB, self.CB], w=[tmpB])
        yield
        self.dve(lambda: v.reciprocal(out=ssv, in_=ssv), r=[tmpB], w=[tmpB])
        yield
        rsb = ssv.unsqueeze(2).to_broadcast([nr, H, D])
        self.dve(lambda: v.tensor_tensor(out=sqv, in0=src, in1=rsb, op=ALU.mult), r=list(rB) + [tmpB], w=[tmpB])
        yield
        gb = gain[:nr, :].unsqueeze(1).to_broadcast([nr, H, D])
        self.dve(lambda: v.tensor_tensor(out=dst, in0=sqv, in1=gb, op=ALU.mult), r=[tmpB, self.CB], w=wB)
        yield

    @staticmethod
    def run(*gens):
        gens = list(gens)
        while gens:
            for g in list(gens):
                try:
                    next(g)
                except StopIteration:
                    gens.remove(g)

    def rms_heads(self, nr, src, H, D, gain, dst, sq, ss, rB, wB, tmpB):
        self.run(self.rms_gen(nr, src, H, D, gain, dst, sq, ss, rB, wB, tmpB))

    def rope(self, nr, src, dst, H, half, cos, sin, tmp, rB, wB, tmpB, eng="pool"):
        g = self.nc.gpsimd if eng == "pool" else self.nc.vector
        opf = self.pool if eng == "pool" else self.dve
        x1, x2 = src[:, :, 0:half], src[:, :, half:2 * half]
        cb = cos.unsqueeze(1).to_broadcast([nr, H, half])
        sn = sin.unsqueeze(1).to_broadcast([nr, H, half])
        t = [tmp[:nr, k * H * half:(k + 1) * H * half].rearrange("p (h d) -> p h d", d=half) for k in range(4)]
        opf(lambda: g.tensor_tensor(out=t[0], in0=x1, in1=cb, op=ALU.mult), r=rB, w=[tmpB])
        opf(lambda: g.tensor_tensor(out=t[1], in0=x2, in1=sn, op=ALU.mult), r=rB, w=[tmpB])
        opf(lambda: g.tensor_tensor(out=t[2], in0=x2, in1=cb, op=ALU.mult), r=rB, w=[tmpB])
        opf(lambda: g.tensor_tensor(out=t[3], in0=x1, in1=sn, op=ALU.mult), r=rB, w=[tmpB])
        opf(lambda: g.tensor_tensor(out=dst[:, :, 0:half], in0=t[0], in1=t[1], op=ALU.subtract), r=[tmpB], w=wB)
        opf(lambda: g.tensor_tensor(out=dst[:, :, half:2 * half], in0=t[2], in1=t[3], op=ALU.add), r=[tmpB], w=wB)

    def bank(self):
        self._bk = (getattr(self, "_bk", -1) + 1) % 8
        return self.psum[self._bk], self.PB[self._bk]

    def key_dsts(self, t0, nr, is_sample):
        c = self.cfg
        if not is_sample:
            return [(t0, 0, nr)]
        return [(c.T + s * c.SS + c.PAST, s * c.DEC, c.DEC) for s in range(c.NS)]

    def evac(self, k, out, in_, r, w):
        if k % 2 == 0:
            a = self.nc.scalar
            return self.act(lambda: a.activation(out=out, in_=in_, func=AF.Copy), r, w)
        v = self.nc.vector
        return self.dve(lambda: v.tensor_copy(out=out, in_=in_), r, w)

    def transpose_blocks(self, nr, srcs, rB, dst, wB, k0=0):
        n = len(srcs)
        for g0 in range(0, n, 8):
            g = srcs[g0:g0 + 8]
            bk, BK = self.bank()
            bb = bk[:].bitcast(BF16)
            for j, s_ap in enumerate(g):
                wd = s_ap.shape[1]
                self.tr(bb[:wd, j * 128:j * 128 + nr], s_ap, self.identb[:nr, :nr], r=list(rB) + [self.CB], w=[BK])
            src = bb[:, 0:len(g) * 128].rearrange("p (j t) -> p j t", t=128)[:, :, :nr]
            self.evac(k0 + g0 // 8, dst[:, g0:g0 + len(g), :nr], src, r=[], w=[BK] + list(wB))

    def kside_mla(self, nr, ckvb, kpeb, inB, dsts, R, split=False):
        X, v, a = self.X, self.nc.vector, self.nc.scalar
        self.transpose_blocks(nr, [ckvb[:nr, 0:128], ckvb[:nr, 128:256]], inB, R["ckvT"], [R["CKVT"]])
        for n0 in range(4):
            bk, BK = self.bank()
            for k in range(2):
                self.mm(bk[:nr, :], R["ckvT"][:, k, :nr], R["wkv"][:, k, n0 * 512:(n0 + 1) * 512], k == 0, k == 1,
                        r=[R["CKVT"], R["WKV"]], w=[BK])
            self.evac(n0, R["kvf"][:nr, n0 * 512:(n0 + 1) * 512], bk[:nr, :], r=[], w=[BK, R["KVF"][n0]])
        if split:
            return
        self.run(self.kside_kn_gen(nr, R))
        self.kside_b(nr, kpeb, inB, dsts, R)

    def kside_kn_gen(self, nr, R):
        kv3 = R["kvf"][:nr, :].rearrange("p (h d) -> p h d", d=256)
        knb3 = R["knb"][:nr, :].rearrange("p (h d) -> p h d", d=128)
        return self.rms_gen(nr, kv3[:, :, 0:128], 8, 128, R["g_kn"], knb3, R["sq2"][:, 1024:2048], R["ss"][:, 16:24], R["KVF"],
                            [R["KNB"]], R["TMPK"])

    def kside_b(self, nr, kpeb, inB, dsts, R):
        X, v, a = self.X, self.nc.vector, self.nc.scalar
        kv3 = R["kvf"][:nr, :].rearrange("p (h d) -> p h d", d=256)
        vb3 = R["vb"][:nr, :].rearrange("p (h d) -> p h d", d=128)
        self.act(lambda: a.activation(out=vb3, in_=kv3[:, :, 128:256], func=AF.Copy), r=R["KVF"], w=[R["VB"]])
        blocks = [R["knb"][:nr, h * 128:(h + 1) * 128] for h in range(8)]
        self.transpose_blocks(nr, blocks, [R["KNB"]], R["tstk"], [R["TSTK"]], k0=1)
        self.transpose_blocks(nr, [kpeb[:nr, 0:64]], inB, R["tstk"][:, 8:9, :], [R["TSTK"]])
        for (koff, c0, n) in dsts:
            self.dma(X["knT"][:, koff:koff + n].rearrange("(h p) t -> p h t", p=128), R["tstk"][:, 0:8, c0:c0 + n],
                     r=[R["TSTK"]], w=[self.XB["knT"]])
            self.dma(X["kpeT"][:, koff:koff + n], R["tstk"][0:64, 8, c0:c0 + n], r=[R["TSTK"]], w=[self.XB["kpeT"]])
            self.dma(X["v"][koff:koff + n, :], R["vb"][c0:c0 + n, :], r=[R["VB"]], w=[self.XB["v"]])

    def phaseA1(self):
        S, I, O, X, W, c = self.S, self.I, self.O, self.X, self.W, self.cfg
        nc = self.nc
        v, a, gp = nc.vector, nc.scalar, nc.gpsimd
        m = S.mark()
        R = {}
        wA = S.sb("wA1", [128, 16, NA1], BF16)
        WA = self.B()
        for k in range(16):
            self.dma(wA[:, k, :], W["w_in"][k * 128:(k + 1) * 128, 0:NA1], r=[self.WB["w_in"]], w=[WA])
        wq = S.sb("wq", [128, 4, 1536], BF16)
        WQ = self.B()
        for k in range(4):
            self.dma(wq[:, k, :], W["w_q_up"][k * 128:(k + 1) * 128, :], r=[self.WB["w_q_up"]], w=[WQ])
        R["wkv"] = S.sb("wkv", [128, 2, 2048], BF16)
        R["WKV"] = self.B()
        for k in range(2):
            self.dma(R["wkv"][:, k, :], W["w_kv_up"][k * 128:(k + 1) * 128, :], r=[self.WB["w_kv_up"]], w=[R["WKV"]])
        g_attn = self.rep_gain("attn_norm", D_MODEL)
        g_qa = self.rep_gain("q_a_norm", Q_LORA)
        g_kva = self.rep_gain("kv_a_norm", KV_LORA)
        g_qn = self.rep_gain("mla_q_nope_norm", NOPE)
        g_qr = self.rep_gain("mla_q_rope_norm", ROPE)
        R["g_kn"] = self.rep_gain("mla_k_nope_norm", NOPE)
        g_kr = self.rep_gain("mla_k_rope_norm", ROPE)
        g_dq = self.rep_gain("dsa_q_norm", DHD)
        xbuf = [S.sb("x", [128, D_MODEL], F32) for _ in range(2)]
        XBF = [self.B() for _ in range(2)]
        R["sq"] = S.sb("sq", [128, 2048], F32)
        R["sq2"] = S.sb("sq2", [128, 2048], F32)
        R["ss"] = S.sb("ss", [128, 32], F32)
        R["TMP"] = self.B()
        R["TMPK"] = self.B()
        TM = [self.B() for _ in range(4)]
        ss0 = S.sb("ss0", [128, 2], F32)
        SS0 = self.B()
        sqj = S.sb("sqj", [128, 2048], BF16)
        SQJ = self.B()
        hb = S.sb("hb", [128, D_MODEL], BF16)
        HB = self.B()
        hT = [S.sb("hT", [128, 16, 128], BF16) for _ in range(2)]
        HT = [self.B() for _ in range(2)]
        pjs = [S.sb("pj", [128, NA1], F32) for _ in range(2)]
        PJS = [[self.B() for _ in range(4)] for _ in range(2)]
        cqb = S.sb("cqb", [128, 512], BF16)
        CQB = self.B()
        ckvf = S.sb("ckvf", [128, 256], F32)
        CKVF = self.B()
        ckvb = S.sb("ckvb", [128, 256], BF16)
        CKVB = self.B()
        kpef = S.sb("kpef", [128, 64], F32)
        KPEF = self.B()
        kpeb = S.sb("kpeb", [128, 64], BF16)
        KPEB = self.B()
        cqT = S.sb("cqT", [128, 4, 128], BF16)
        CQT = self.B()
        R["ckvT"] = S.sb("ckvT", [128, 2, 128], BF16)
        R["CKVT"] = self.B()
        qf = S.sb("qf", [128, 1536], F32)
        QF = [self.B() for _ in range(3)]
        qnb = S.sb("qnb", [128, 1024], BF16)
        QNB = self.B()
        qrf = S.sb("qrf", [128, 512], F32)
        QRF = self.B()
        qrb = S.sb("qrb", [128, 512], BF16)
        QRB = self.B()
        R["kvf"] = S.sb("kvf", [128, 2048], F32)
        R["KVF"] = [self.B() for _ in range(4)]
        R["knb"] = S.sb("knb", [128, 1024], BF16)
        R["KNB"] = self.B()
        R["vb"] = S.sb("vb", [128, 1024], BF16)
        R["VB"] = self.B()
        R["tstk"] = S.sb("tstk", [128, 9, 128], BF16)
        R["TSTK"] = self.B()
        tstq = S.sb("tstq", [128, 12, 128], BF16)
        TSTQ = self.B()
        tstd = S.sb("tstd", [128, 8, 128], BF16)
        TSTD = self.B()
        qdb = S.sb("qdb", [128, 1024], BF16)
        QDB = self.B()
        rtmp = S.sb("rtmp", [128, 1024], F32)
        RTMP = self.B()
        ropes = [S.sb("ropes", [128, 2, 56], F32) for _ in range(2)]
        RP = [self.B() for _ in range(2)]
        cin = S.sb("cin", [128, 320], F32)
        CIN = self.B()

        tiles_ = self.token_tiles()

        def front(idx):
            t0, nr, smp = tiles_[idx]
            pb = idx % 2
            xt, XT = xbuf[pb], XBF[pb]
            rp, RPB = ropes[pb], RP[pb]
            pj, PJ = pjs[pb], PJS[pb]
            if smp:
                self.dma(xt[:nr, :], I["xs"][:, :], w=[XT])
                for s in range(c.NS):
                    self.dma(rp[s * c.DEC:(s + 1) * c.DEC, :, :], I["ropet"][c.PAST:c.PAST + c.DEC, :, :], w=[RPB])
            else:
                self.dma(xt[:nr, :], I["xp"][t0:t0 + nr, :], w=[XT])
                self.dma(rp[:nr, :, :], I["ropet"][t0:t0 + nr, :, :], w=[RPB])
            self.act(lambda xt=xt, nr=nr: a.activation(out=sqj[:nr, :], in_=xt[:nr, :], func=AF.Square,
                                                         accum_out=ss0[:nr, 0:1]), r=[XT], w=[SQJ, SS0])
            self.act(lambda nr=nr: a.activation(out=ss0[:nr, 0:1], in_=ss0[:nr, 0:1], func=AF.Sqrt, scale=1.0 / D_MODEL,
                                                 bias=self.epsc[:nr, 0:1]), r=[self.CB], w=[SS0])
            self.dve(lambda nr=nr: v.reciprocal(out=ss0[:nr, 0:1], in_=ss0[:nr, 0:1]), w=[SS0])
            self.dve(lambda xt=xt, nr=nr: v.scalar_tensor_tensor(out=hb[:nr, :], in0=xt[:nr, :], scalar=ss0[:nr, 0:1],
                                                                  in1=g_attn[:nr, :], op0=ALU.mult, op1=ALU.mult),
                     r=[XT, SS0, self.CB], w=[HB])
            self.transpose_blocks(nr, [hb[:nr, k * 128:(k + 1) * 128] for k in range(16)], [HB], hT[pb], [HT[pb]])
            self.dma(X["hT"][:, t0:t0 + nr].rearrange("(k p) t -> p k t", p=128), hT[pb][:, :, :nr], r=[HT[pb]],
                     w=[self.XB["hT"]])
            for gi, (c0, cw) in enumerate(((0, 512), (512, 512), (1024, 512), (1536, 320))):
                bk, BK = self.bank()
                for k in range(16):
                    self.mm(bk[:nr, :cw], hT[pb][:, k, :nr], wA[:, k, c0:c0 + cw], k == 0, k == 15, r=[HT[pb], WA], w=[BK])
                self.evac(gi, pj[:nr, c0:c0 + cw], bk[:nr, :cw], r=[], w=[BK, PJ[gi]])
        def back(idx):
            t0, nr, smp = tiles_[idx]
            pb = idx % 2
            rp, RPB = ropes[pb], RP[pb]
            pj, PJ = pjs[pb], PJS[pb]
            sq, sq2, ss = R["sq"], R["sq2"], R["ss"]
            kp3 = kpef[:nr, :].rearrange("p (h d) -> p h d", d=64)
            qd3 = pj[:nr, 832:1856].rearrange("p (h d) -> p h d", d=128)
            qdf3 = sq2[:nr, 1024:2048].rearrange("p (h d) -> p h d", d=128)
            self.run(
                self.rms_gen(nr, pj[:nr, 0:512].rearrange("p (h d) -> p h d", d=512), 1, 512, g_qa,
                             cqb[:nr, :].rearrange("p (h d) -> p h d", d=512), sq[:, 0:512], ss[:, 0:1], [PJ[0]], [CQB], TM[0]),
                self.rms_gen(nr, pj[:nr, 512:768].rearrange("p (h d) -> p h d", d=256), 1, 256, g_kva,
                             ckvf[:nr, :].rearrange("p (h d) -> p h d", d=256), sq[:, 512:768], ss[:, 1:2], [PJ[1]], [CKVF], TM[1]),
                self.rms_gen(nr, pj[:nr, 768:832].rearrange("p (h d) -> p h d", d=64), 1, 64, g_kr, kp3, sq[:, 768:832], ss[:, 2:3],
                             [PJ[1]], [KPEF], TM[2]),
                self.rms_gen(nr, qd3, 8, 128, g_dq, qdf3, sq2[:, 0:1024], ss[:, 3:11], [PJ[1], PJ[2], PJ[3]], [TM[3]], TM[3]),
            )
            self.transpose_blocks(nr, [cqb[:nr, k * 128:(k + 1) * 128] for k in range(4)], [CQB], cqT, [CQT])
            for n0 in range(3):
                bk, BK = self.bank()
                for k in range(4):
                    self.mm(bk[:nr, :], cqT[:, k, :nr], wq[:, k, n0 * 512:(n0 + 1) * 512], k == 0, k == 3, r=[CQT, WQ], w=[BK])
                self.evac(n0 + 1, qf[:nr, n0 * 512:(n0 + 1) * 512], bk[:nr, :], r=[], w=[BK, QF[n0]])
            self.act(lambda: a.activation(out=ckvb[:nr, :], in_=ckvf[:nr, :], func=AF.Copy), r=[CKVF], w=[CKVB])
            self.rope(nr, kp3, kp3, 1, 32, rp[:nr, 0, 0:32], rp[:nr, 1, 0:32], rtmp, [KPEF, RPB], [KPEF], RTMP)
            self.act(lambda: a.activation(out=kpeb[:nr, :], in_=kpef[:nr, :], func=AF.Copy), r=[KPEF], w=[KPEB])
            if smp:
                self.dma(O["s_ckv"][:, :], ckvf[:nr, :], r=[CKVF])
                self.dma(O["s_kpe"][:, :], kpef[:nr, :], r=[KPEF])
            else:
                self.dma(O["p_ckv"][t0:t0 + nr, :], ckvf[:nr, :], r=[CKVF])
                self.dma(O["p_kpe"][t0:t0 + nr, :], kpef[:nr, :], r=[KPEF])
            self.kside_mla(nr, ckvb, kpeb, [CKVB, KPEB], None, R, split=True)
            self.rope(nr, qdf3[:, :, 0:32], qdf3[:, :, 0:32], 8, 16, rp[:nr, 0, 32:48], rp[:nr, 1, 32:48], rtmp,
                      [TM[3], RPB], [TM[3]], RTMP)
            self.act(lambda: a.activation(out=qdb[:nr, :].rearrange("p (h d) -> p h d", d=128), in_=qdf3, func=AF.Copy),
                     r=[TM[3]], w=[QDB])
            q3 = qf[:nr, :].rearrange("p (h d) -> p h d", d=192)
            qr3 = qrf[:nr, :].rearrange("p (h d) -> p h d", d=64)
            self.run(
                self.rms_gen(nr, q3[:, :, 0:128], 8, 128, g_qn, qnb[:nr, :].rearrange("p (h d) -> p h d", d=128),
                             sq[:, 0:1024], ss[:, 0:8], QF, [QNB], TM[0]),
                self.rms_gen(nr, q3[:, :, 128:192], 8, 64, g_qr, qr3, sq[:, 1024:1536], ss[:, 8:16], QF, [QRF], TM[1]),
                self.kside_kn_gen(nr, R),
            )
            self.transpose_blocks(nr, [qdb[:nr, h * 128:(h + 1) * 128] for h in range(8)], [QDB], tstd, [TSTD], k0=1)
            self.dma(X["qdT"][:, t0:t0 + nr].rearrange("(h p) t -> p h t", p=128), tstd[:, :, :nr], r=[TSTD],
                     w=[self.XB["qdT"]])
            self.rope(nr, qr3, qrb[:nr, :].rearrange("p (h d) -> p h d", d=64), 8, 32, rp[:nr, 0, 0:32], rp[:nr, 1, 0:32],
                      rtmp, [QRF, RPB], [QRB], RTMP)
            self.kside_b(nr, kpeb, [CKVB, KPEB], self.key_dsts(t0, nr, smp), R)
            blocks = [qnb[:nr, h * 128:(h + 1) * 128] for h in range(8)] + [qrb[:nr, j * 128:(j + 1) * 128] for j in range(4)]
            self.transpose_blocks(nr, blocks, [QNB, QRB], tstq, [TSTQ])
            self.dma(X["qnT"][:, t0:t0 + nr].rearrange("(h p) t -> p h t", p=128), tstq[:, 0:8, :nr], r=[TSTQ],
                     w=[self.XB["qnT"]])
            self.dma(X["qrT"][:, t0:t0 + nr].rearrange("(h p) t -> p h t", p=128), tstq[:, 8:12, :nr], r=[TSTQ],
                     w=[self.XB["qrT"]])

        front(0)
        for idx in range(len(tiles_)):
            if idx + 1 < len(tiles_):
                front(idx + 1)
            back(idx)
        self.cast_weights([k for k in self.W if k not in self.EARLY_W])
        for s in range(c.NS):
            for j in range(c.PAST // 128):
                self.dma(cin[:, 0:256], I["c_ckv"][s, j * 128:(j + 1) * 128, :], w=[CIN])
                self.dma(cin[:, 256:320], I["c_kpe"][s, j * 128:(j + 1) * 128, :], w=[CIN])
                self.dve(lambda: v.tensor_copy(out=ckvb[:, :], in_=cin[:, 0:256]), r=[CIN], w=[CKVB])
                self.dve(lambda: v.tensor_copy(out=kpeb[:, :], in_=cin[:, 256:320]), r=[CIN], w=[KPEB])
                self.kside_mla(128, ckvb, kpeb, [CKVB, KPEB], [(c.T + s * c.SS + j * 128, 0, 128)], R)
        S.release(m)

    def phaseA2(self):
        S, I, O, X, W, c = self.S, self.I, self.O, self.X, self.W, self.cfg
        nc = self.nc
        v, a, gp = nc.vector, nc.scalar, nc.gpsimd
        m = S.mark()
        wA = S.sb("wA2", [128, 16, NA2], BF16)
        WA = self.B()
        for k in range(16):
            self.dma(wA[:, k, :], W["w_in"][k * 128:(k + 1) * 128, NA1:NA], r=[self.WB["w_in"]], w=[WA])
        g_dk = self.rep_gain("dsa_k_norm", DHD)
        hT = [S.sb("hT2", [128, 16, 128], BF16) for _ in range(2)]
        HT = [self.B() for _ in range(2)]
        ropes = [S.sb("ropes2", [128, 2, 56], F32) for _ in range(2)]
        RP = [self.B() for _ in range(2)]
        pjs = [S.sb("pj2", [128, NA2], F32) for _ in range(2)]
        PJS = [[self.B() for _ in range(4)] for _ in range(2)]
        sq = S.sb("sq2", [128, 1024], F32)
        ss = S.sb("ss2", [128, 16], F32)
        TMP = self.B()
        rtmp = S.sb("rtmp2", [128, 1024], F32)
        RTMP = self.B()
        kdf = S.sb("kdf", [128, 256], F32)
        KDF = self.B()
        kdb = S.sb("kdb", [128, 256], BF16)
        KDB = self.B()
        vdb = S.sb("vdb", [128, 256], BF16)
        VDB = self.B()
        qib = S.sb("qib", [128, 1024], BF16)
        QIB = self.B()
        kib = S.sb("kib", [128, 64], BF16)
        KIB = self.B()
        tst = S.sb("tst2", [128, 11, 128], BF16)
        TST = self.B()
        wtmp = S.sb("wtmp", [128, 2 * IH], F32)
        WT = self.B()
        wout = S.sb("wout", [128, 2, IH], F32)
        WO = self.B()
        cin = S.sb("cin2", [128, 576], F32)
        CIN = self.B()

        def kside(nr, inB, dsts):
            self.transpose_blocks(nr, [kdb[:nr, 0:128], kdb[:nr, 128:256], kib[:nr, 0:64]], inB, tst[:, 8:11, :], [TST])
            for (koff, c0, n) in dsts:
                self.dma(X["kdT"][:, koff:koff + n].rearrange("(g p) t -> p g t", p=128), tst[:, 8:10, c0:c0 + n], r=[TST],
                         w=[self.XB["kdT"]])
                self.dma(X["kiT"][:, koff:koff + n], tst[0:64, 10, c0:c0 + n], r=[TST], w=[self.XB["kiT"]])
                self.dma(X["vd"][koff:koff + n, :], vdb[c0:c0 + n, :], r=[VDB], w=[self.XB["vd"]])

        tiles_ = self.token_tiles()

        def front(idx):
            t0, nr, smp = tiles_[idx]
            pb = idx % 2
            rp, RPB = ropes[pb], RP[pb]
            pj, PJ = pjs[pb], PJS[pb]
            self.dma(hT[pb][:, :, :nr], X["hT"][:, t0:t0 + nr].rearrange("(k p) t -> p k t", p=128), r=[self.XB["hT"]],
                     w=[HT[pb]])
            if smp:
                for s in range(c.NS):
                    self.dma(rp[s * c.DEC:(s + 1) * c.DEC, :, :], I["ropet"][c.PAST:c.PAST + c.DEC, :, :], w=[RPB])
            else:
                self.dma(rp[:nr, :, :], I["ropet"][t0:t0 + nr, :, :], w=[RPB])
            for gi, (c0, cw) in enumerate(((0, 512), (512, 512), (1024, 512), (1536, 80))):
                bk, BK = self.bank()
                for k in range(16):
                    self.mm(bk[:nr, :cw], hT[pb][:, k, :nr], wA[:, k, c0:c0 + cw], k == 0, k == 15, r=[HT[pb], WA], w=[BK])
                self.evac(gi, pj[:nr, c0:c0 + cw], bk[:nr, :cw], r=[], w=[BK, PJ[gi]])
        def back(idx):
            t0, nr, smp = tiles_[idx]
            pb = idx % 2
            rp, RPB = ropes[pb], RP[pb]
            pj, PJ = pjs[pb], PJS[pb]
            kv3 = pj[:nr, 0:512].rearrange("p (h d) -> p h d", d=128)
            kd3 = kdf[:nr, :].rearrange("p (h d) -> p h d", d=128)
            self.rms_heads(nr, kv3[:, 0:2, :], 2, 128, g_dk, kd3, sq, ss, [PJ[0]], [KDF], TMP)
            self.act(lambda nr=nr: a.activation(out=vdb[:nr, :], in_=pj[:nr, 256:512], func=AF.Copy), r=[PJ[0]], w=[VDB])
            qi3 = pj[:nr, 512:1536].rearrange("p (h d) -> p h d", d=64)
            self.rope(nr, qi3[:, :, 0:16], qi3[:, :, 0:16], 16, 8, rp[:nr, 0, 48:56], rp[:nr, 1, 48:56], rtmp,
                      [PJ[1], PJ[2], RPB], [PJ[1], PJ[2]], RTMP)
            self.act(lambda nr=nr: a.activation(out=qib[:nr, :], in_=pj[:nr, 512:1536], func=AF.Copy), r=[PJ[1], PJ[2]], w=[QIB])
            ki3 = pj[:nr, 1536:1600].rearrange("p (h d) -> p h d", d=64)
            self.rope(nr, ki3[:, :, 0:16], ki3[:, :, 0:16], 1, 8, rp[:nr, 0, 48:56], rp[:nr, 1, 48:56], rtmp, [PJ[3], RPB],
                      [PJ[3]], RTMP)
            self.act(lambda nr=nr: a.activation(out=kib[:nr, :], in_=pj[:nr, 1536:1600], func=AF.Copy), r=[PJ[3]], w=[KIB])
            self.rope(nr, kd3[:, :, 0:32], kd3[:, :, 0:32], 2, 16, rp[:nr, 0, 32:48], rp[:nr, 1, 32:48], rtmp, [KDF, RPB],
                      [KDF], RTMP)
            self.act(lambda nr=nr: a.activation(out=kdb[:nr, :], in_=kdf[:nr, :], func=AF.Copy), r=[KDF], w=[KDB])
            if smp:
                self.dma(O["s_dk"][:, :], kdf[:nr, :], r=[KDF])
                self.dma(O["s_dv"][:, :], pj[:nr, 256:512], r=[PJ[0]])
                self.dma(O["s_ik"][:, :], pj[:nr, 1536:1600], r=[PJ[3]])
            else:
                self.dma(O["p_dk"][t0:t0 + nr, :], kdf[:nr, :], r=[KDF])
                self.dma(O["p_dv"][t0:t0 + nr, :], pj[:nr, 256:512], r=[PJ[0]])
                self.dma(O["p_ik"][t0:t0 + nr, :], pj[:nr, 1536:1600], r=[PJ[3]])
            self.transpose_blocks(nr, [qib[:nr, j * 128:(j + 1) * 128] for j in range(8)], [QIB], tst[:, 0:8, :], [TST])
            self.dma(X["qiT"][:, t0:t0 + nr].rearrange("(j p) t -> p j t", p=128), tst[:, 0:8, :nr], r=[TST],
                     w=[self.XB["qiT"]])
            kside(nr, [KDB, KIB], self.key_dsts(t0, nr, smp))
            wi = pj[:nr, 1600:1616]
            self.dve(lambda nr=nr, wi=wi: v.tensor_scalar(out=wtmp[:nr, 0:IH], in0=wi, scalar1=-1.0, scalar2=None, op0=ALU.mult),
                     r=[PJ[3]], w=[WT])
            self.dve(lambda nr=nr, wi=wi: v.tensor_tensor(out=wtmp[:nr, 0:IH], in0=wtmp[:nr, 0:IH], in1=wi, op=ALU.max),
                     r=[PJ[3]], w=[WT])
            self.dve(lambda nr=nr: v.tensor_scalar(out=wout[:nr, 0, :], in0=wtmp[:nr, 0:IH], scalar1=0.03125,
                                                   scalar2=None, op0=ALU.mult), r=[WT], w=[WO])
            self.dve(lambda nr=nr, wi=wi: v.tensor_scalar(out=wtmp[:nr, IH:2 * IH], in0=wi, scalar1=0.0, scalar2=2.0,
                                                          op0=ALU.is_gt, op1=ALU.mult), r=[PJ[3]], w=[WT])
            self.dve(lambda nr=nr: v.tensor_scalar(out=wout[:nr, 1, :], in0=wtmp[:nr, IH:2 * IH], scalar1=-1.0,
                                                   scalar2=None, op0=ALU.add), r=[WT], w=[WO])
            self.dma(X["wi"][t0:t0 + nr, :, :], wout[:nr, :, :], r=[WO], w=[self.XB["wi"]])

        front(0)
        for idx in range(len(tiles_)):
            if idx + 1 < len(tiles_):
                front(idx + 1)
            back(idx)
        for s in range(c.NS):
            for j in range(c.PAST // 128):
                rows = slice(j * 128, (j + 1) * 128)
                self.dma(cin[:, 0:256], I["c_dk"][s, rows, :], w=[CIN])
                self.dma(cin[:, 256:512], I["c_dv"][s, rows, :], w=[CIN])
                self.dma(cin[:, 512:576], I["c_ik"][s, rows, :], w=[CIN])
                self.dve(lambda: v.tensor_copy(out=kdb[:, :], in_=cin[:, 0:256]), r=[CIN], w=[KDB])
                self.act(lambda: a.activation(out=vdb[:, :], in_=cin[:, 256:512], func=AF.Copy), r=[CIN], w=[VDB])
                self.dve(lambda: v.tensor_copy(out=kib[:, :], in_=cin[:, 512:576]), r=[CIN], w=[KIB])
                kside(128, [KDB, KIB], [(c.T + s * c.SS + j * 128, 0, 128)])
        S.release(m)

    def bank_of(self, pool, key):
        cnt = getattr(self, "_bkc", {})
        self._bkc = cnt
        i = cnt.get(key, 0)
        cnt[key] = i + 1
        b = pool[i % len(pool)]
        return self.psum[b], self.PB[b]

    def seqs(self):
        c = self.cfg
        out = [dict(koff=0, S=c.T, qoff=0, nq=c.T, causal=True)]
        for s in range(c.NS):
            out.append(dict(koff=c.T + s * c.SS, S=c.SS, qoff=c.T + s * c.DEC, nq=c.DEC, causal=False))
        return out

    def phaseB(self):
        S, X, c = self.S, self.X, self.cfg
        nc = self.nc
        v, a, gp = nc.vector, nc.scalar, nc.gpsimd
        m = S.mark()
        Smax = max(c.T, c.SS)
        nktmax = (Smax + 127) // 128
        nqmax = max(c.T, c.DEC)
        kn = [S.sb("kn", [128, Smax], BF16) for _ in range(2)]
        vt = [S.sb("vt", [128, nktmax, 128], BF16) for _ in range(2)]
        qn = [S.sb("qn", [128, nqmax], BF16) for _ in range(2)]
        qr = [S.sb("qr", [64, nqmax], BF16) for _ in range(2)]
        HB = [self.B() for _ in range(2)]
        kpe = S.sb("kpe", [64, Smax], BF16)
        KPE = self.B()
        pT = [S.sb("pT", [128, 512], BF16) for _ in range(5)]
        PT = [self.B() for _ in range(5)]
        rden = S.sb("rden", [128, 512], F32)
        RD = self.B()
        oT = [S.sb("oT", [128, 512], BF16) for _ in range(2)]
        OT = [self.B() for _ in range(2)]
        scale = float((NOPE + ROPE) ** -0.5)
        rdeps = [self.XB[k] for k in ("knT", "kpeT", "v", "qnT", "qrT")]

        def load(sq, h, pb):
            koff, Sk, qoff, nq = sq["koff"], sq["S"], sq["qoff"], sq["nq"]
            self.dma(kn[pb][:, :Sk], X["knT"][h * 128:(h + 1) * 128, koff:koff + Sk], r=rdeps, w=[HB[pb]])
            nfull = Sk // 128
            if nfull:
                self.dma(vt[pb][:, 0:nfull, :],
                         X["v"][koff:koff + nfull * 128, h * 128:(h + 1) * 128].rearrange("(k p) d -> p k d", p=128),
                         r=rdeps, w=[HB[pb]])
            rem = Sk - nfull * 128
            if rem:
                self.dma(vt[pb][:rem, nfull, :], X["v"][koff + nfull * 128:koff + Sk, h * 128:(h + 1) * 128], r=rdeps, w=[HB[pb]])
            self.dma(qn[pb][:, :nq], X["qnT"][h * 128:(h + 1) * 128, qoff:qoff + nq], r=rdeps, w=[HB[pb]])
            self.dma(qr[pb][:, :nq], X["qrT"][h * 64:(h + 1) * 64, qoff:qoff + nq], r=rdeps, w=[HB[pb]])

        work = [(sq, h) for sq in self.seqs() for h in range(MH)]
        load(work[0][0], work[0][1], 0)
        pc = [0]
        ocount = 0
        for wi_, (sq, h) in enumerate(work):
            pb = wi_ % 2
            if wi_ + 1 < len(work):
                load(work[wi_ + 1][0], work[wi_ + 1][1], (wi_ + 1) % 2)
            koff, Sk, qoff, nq, causal = sq["koff"], sq["S"], sq["qoff"], sq["nq"], sq["causal"]
            if h == 0:
                self.dma(kpe[:, :Sk], X["kpeT"][:, koff:koff + Sk], r=rdeps, w=[KPE])
            for qb0 in range(0, nq, 512):
                nqb = min(512, nq - qb0)
                nkt = (qb0 + nqb) // 128 if causal else (Sk + 127) // 128
                ao, AO = self.bank_of([0, 1], "bo")
                ad, AD = self.bank_of([2, 3], "bd")
                pend = {}

                def front(kt):
                    kr = min(128, Sk - kt * 128)
                    qa = max(0, kt * 128 - qb0) if causal else 0
                    n = nqb - qa
                    sb_, SB_ = self.bank_of([4, 5, 6, 7], "bs")
                    self.mm(sb_[:kr, :n], kn[pb][:, kt * 128:kt * 128 + kr], qn[pb][:, qb0 + qa:qb0 + nqb], True, False,
                            r=[HB[pb]], w=[SB_])
                    self.mm(sb_[:kr, :n], kpe[:, kt * 128:kt * 128 + kr], qr[pb][:, qb0 + qa:qb0 + nqb], False, True,
                            r=[HB[pb], KPE], w=[SB_])
                    p, P = pT[pc[0] % 5], PT[pc[0] % 5]
                    pc[0] += 1
                    self.act(lambda: a.activation(out=p[:kr, :n], in_=sb_[:kr, :n], func=AF.Exp, scale=scale), r=[], w=[SB_, P])
                    if causal and kt * 128 >= qb0:
                        self.pool(lambda: gp.memset(p[64:128, 0:64], 0.0), r=[], w=[P])
                    pend[kt] = (p, P, kr, qa, n)

                def back(kt):
                    p, P, kr, qa, n = pend.pop(kt)
                    first, last = kt == 0, kt == nkt - 1
                    self.mm(ao[:, qa:nqb], vt[pb][:kr, kt, :], p[:kr, :n], first, last, r=[HB[pb], P], w=[AO])
                    self.mm(ad[:, qa:nqb], self.onesb[:kr, :], p[:kr, :n], first, last, r=[self.CB, P], w=[AD])

                LA = 2
                for k_ in range(nkt + LA):
                    if k_ < nkt:
                        front(k_)
                    if k_ >= LA:
                        back(k_ - LA)
                o, OB = oT[ocount % 2], OT[ocount % 2]
                ocount += 1
                self.dve(lambda ad=ad, nqb=nqb: v.reciprocal(out=rden[:, :nqb], in_=ad[:, :nqb]), r=[], w=[AD, RD])
                self.dve(lambda o=o, ao=ao, nqb=nqb: v.tensor_tensor(out=o[:, :nqb], in0=ao[:, :nqb], in1=rden[:, :nqb],
                                                                      op=ALU.mult), r=[RD], w=[AO, OB])
                self.dma(X["omT"][h * 128:(h + 1) * 128, qoff + qb0:qoff + qb0 + nqb], o[:, :nqb], r=[OB], w=[self.XB["omT"]])
        S.release(m)

    def phaseC(self):
        S, X, c = self.S, self.X, self.cfg
        nc = self.nc
        v, a, gp = nc.vector, nc.scalar, nc.gpsimd
        m = S.mark()
        Smax = max(c.T, c.SS)
        nktmax = (Smax + 127) // 128
        ki2 = S.sb("ki2", [128, Smax], BF16)
        kd = S.sb("kd", [128, 2, Smax], BF16)
        vd = S.sb("vd", [128, nktmax, 256], BF16)
        SEQB = self.B()
        qi = [S.sb("qi", [128, 8, 128], BF16) for _ in range(2)]
        qd = [S.sb("qd", [128, 8, 128], BF16) for _ in range(2)]
        wv = [S.sb("wv", [128, 2, IH], F32) for _ in range(2)]
        QB = [self.B() for _ in range(2)]
        scores = [S.sb("score", [128, Smax], F32) for _ in range(2)]
        SCB = [self.B() for _ in range(2)]
        m8 = S.sb("m8", [128, 8], F32)
        M8 = self.B()
        thr = S.sb("thr", [128, 1], F32)
        bs_ = S.sb("bsct", [128, 4], F32)
        rtab = S.sb("rtab", [128, self.NI], F32)
        cntb = S.sb("cntb", [128, self.NI], F32)
        mask = S.sb("mask", [128, Smax], BF16)
        MK = self.B()
        maskTs = [S.sb("maskT", [128, nktmax, 128], BF16) for _ in range(2)]
        MTB = [self.B() for _ in range(2)]
        tmp = [S.sb("itmp", [128, 512], F32) for _ in range(3)]
        TM = [self.B() for _ in range(3)]
        NEB = 4
        eT = [S.sb("eT", [128, 512], BF16) for _ in range(NEB)]
        ET = [self.B() for _ in range(NEB)]
        pT = [S.sb("pTd", [128, 512], BF16) for _ in range(NEB)]
        PT = [self.B() for _ in range(NEB)]
        rden = S.sb("rdend", [128, 512], F32)
        RD = self.B()
        oT = [S.sb("oTd", [128, 512], BF16) for _ in range(2)]
        OT = [self.B() for _ in range(2)]
        scale = float(DHD ** -0.5)
        kdeps = [self.XB[k] for k in ("kiT", "kdT", "vd")]
        qdeps = [self.XB[k] for k in ("qiT", "qdT", "wi")]
        tiles = []
        for si, sq in enumerate(self.seqs()):
            if sq["causal"]:
                for qt in range(sq["nq"] // 128):
                    tiles.append((si, sq, sq["qoff"] + qt * 128, 128, (qt + 1) * 128, True))
            else:
                tiles.append((si, sq, sq["qoff"], sq["nq"], sq["S"], False))

        def loadq(tl, pb):
            tok0, nq = tl[2], tl[3]
            self.dma(qi[pb][:, :, :nq], X["qiT"][:, tok0:tok0 + nq].rearrange("(j p) t -> p j t", p=128), r=qdeps, w=[QB[pb]])
            self.dma(qd[pb][:, :, :nq], X["qdT"][:, tok0:tok0 + nq].rearrange("(j p) t -> p j t", p=128), r=qdeps, w=[QB[pb]])
            self.dma(wv[pb][:nq, :, :], X["wi"][tok0:tok0 + nq, :, :], r=qdeps, w=[QB[pb]])

        st = dict(tcount=0, ecount=0, ocount=0, cur_seq=-1)
        acc = {}

        def seq_load(tl):
            si, sq = tl[0], tl[1]
            if si == st["cur_seq"]:
                return
            st["cur_seq"] = si
            koff, Sk = sq["koff"], sq["S"]
            for half in range(2):
                self.dma(ki2[half * 64:(half + 1) * 64, :Sk], X["kiT"][:, koff:koff + Sk], r=kdeps, w=[SEQB])
            self.dma(kd[:, :, :Sk], X["kdT"][:, koff:koff + Sk].rearrange("(g p) t -> p g t", p=128), r=kdeps, w=[SEQB])
            nfull = Sk // 128
            if nfull:
                self.dma(vd[:, 0:nfull, :], X["vd"][koff:koff + nfull * 128, :].rearrange("(k p) d -> p k d", p=128),
                         r=kdeps, w=[SEQB])
            if Sk - nfull * 128:
                self.dma(vd[:Sk - nfull * 128, nfull, :], X["vd"][koff + nfull * 128:koff + Sk, :], r=kdeps, w=[SEQB])

        def idx(ti):
            si, sq, tok0, nq, W, causal = tiles[ti]
            pb = ti % 2
            score, SC = scores[pb], SCB[pb]
            for kb in range(0, W, 512):
                wb = min(512, W - kb)
                for j in range(8):
                    banks = [self.bank_of([4, 5, 6, 7], "bs") for _ in range(2)]
                    for hh in range(2):
                        bk, BK = banks[hh]
                        self.mm(bk[:nq, :wb], qi[pb][hh * 64:(hh + 1) * 64, j, :nq], ki2[hh * 64:(hh + 1) * 64, kb:kb + wb],
                                True, True, r=[QB[pb], SEQB], w=[BK])
                    for hh in range(2):
                        bk, BK = banks[hh]
                        h = 2 * j + hh
                        t_, T_ = tmp[st["tcount"] % 3], TM[st["tcount"] % 3]
                        st["tcount"] += 1
                        self.act(lambda t_=t_, bk=bk, nq=nq, wb=wb, pb=pb, h=h: a.activation(
                            out=t_[:nq, :wb], in_=bk[:nq, :wb], func=AF.Relu, scale=wv[pb][:nq, 0, h:h + 1]),
                            r=[QB[pb]], w=[BK, T_])
                        if h == 0:
                            self.dve(lambda t_=t_, nq=nq, wb=wb, kb=kb, pb=pb, h=h, score=score: v.tensor_scalar(
                                out=score[:nq, kb:kb + wb], in0=t_[:nq, :wb], scalar1=wv[pb][:nq, 1, h:h + 1], scalar2=None,
                                op0=ALU.mult), r=[T_, QB[pb]], w=[SC])
                        else:
                            self.dve(lambda t_=t_, nq=nq, wb=wb, kb=kb, pb=pb, h=h, score=score: v.scalar_tensor_tensor(
                                out=score[:nq, kb:kb + wb], in0=t_[:nq, :wb], scalar=wv[pb][:nq, 1, h:h + 1],
                                in1=score[:nq, kb:kb + wb], op0=ALU.mult, op1=ALU.add), r=[T_, QB[pb]], w=[SC])

        def bisect(ti):
            si, sq, tok0, nq, W, causal = tiles[ti]
            pb = ti % 2
            score, SC = scores[pb], SCB[pb]
            topk = min(TOPK_MAX, sq["S"] // 4)
            if W > topk:
                NI = self.NI
                self.dve(lambda: v.tensor_reduce(out=bs_[:nq, 0:1], in_=score[:nq, :W], axis=AX.X, op=ALU.max), r=[SC], w=[M8])
                self.dve(lambda: v.tensor_reduce(out=thr[:nq, 0:1], in_=score[:nq, :W], axis=AX.X, op=ALU.min), r=[SC], w=[M8])
                if causal:
                    self.dve(lambda: v.memset(score[0:64, W - 64:W], NEG), r=[M8], w=[SC])
                self.dve(lambda: v.tensor_tensor(out=bs_[:nq, 0:1], in0=bs_[:nq, 0:1], in1=thr[:nq, 0:1], op=ALU.subtract),
                         r=[], w=[M8])
                self.dve(lambda: v.tensor_scalar(out=bs_[:nq, 0:1], in0=bs_[:nq, 0:1], scalar1=1.0001, scalar2=1e-6,
                                                 op0=ALU.mult, op1=ALU.add), r=[], w=[M8])
                self.dve(lambda: v.tensor_scalar(out=rtab[:nq, :], in0=self.ctab[:nq, :], scalar1=bs_[:nq, 0:1], scalar2=None,
                                                 op0=ALU.mult), r=[self.CB], w=[M8])
                self.dve(lambda: v.memset(cntb[:nq, :], 0.0), r=[], w=[M8])
                for it in range(NI):
                    self.dve(lambda it=it: v.tensor_tensor(out=bs_[:nq, 1:2], in0=thr[:nq, 0:1], in1=rtab[:nq, it:it + 1],
                                                           op=ALU.add), r=[], w=[M8])
                    self.dve(lambda it=it: v.tensor_scalar(out=mask[:nq, :W], in0=score[:nq, :W], scalar1=bs_[:nq, 1:2],
                                                           scalar2=0.0, op0=ALU.is_ge, op1=ALU.add,
                                                           accum_out=cntb[:nq, it:it + 1]), r=[SC], w=[M8, MK])
                    self.dve(lambda it=it: v.tensor_scalar(out=bs_[:nq, 2:3], in0=cntb[:nq, it:it + 1],
                                                           scalar1=float(topk) - 0.5, scalar2=rtab[:nq, it:it + 1],
                                                           op0=ALU.is_ge, op1=ALU.mult), r=[], w=[M8])
                    self.dve(lambda: v.tensor_tensor(out=thr[:nq, 0:1], in0=thr[:nq, 0:1], in1=bs_[:nq, 2:3], op=ALU.add),
                             r=[], w=[M8])
            else:
                if causal:
                    self.dve(lambda: v.memset(score[0:64, W - 64:W], NEG), r=[], w=[SC])
                self.dve(lambda: v.memset(thr[:nq, :], -1e29), r=[], w=[M8])

        def mask_tr(ti):
            si, sq, tok0, nq, W, causal = tiles[ti]
            pb = ti % 2
            score, SC = scores[pb], SCB[pb]
            nkt = (W + 127) // 128
            self.dve(lambda: v.tensor_scalar(out=mask[:nq, :W], in0=score[:nq, :W], scalar1=thr[:nq, 0:1],
                                             scalar2=None, op0=ALU.is_ge), r=[SC, M8], w=[MK])
            srcs = [mask[:nq, kt * 128:min(W, (kt + 1) * 128)] for kt in range(nkt)]
            for g0 in range(0, nkt, 8):
                g = srcs[g0:g0 + 8]
                bk, BK = self.bank_of([4, 5, 6, 7], "bs")
                bb = bk[:].bitcast(BF16)
                for j, s_ap in enumerate(g):
                    wd = s_ap.shape[1]
                    self.tr(bb[:wd, j * 128:j * 128 + nq], s_ap, self.identb[:nq, :nq], r=[MK, self.CB], w=[BK])
                src = bb[:, 0:len(g) * 128].rearrange("p (j t) -> p j t", t=128)[:, :, :nq]
                self.evac(g0 // 8, maskTs[pb][:, g0:g0 + len(g), :nq], src, r=[], w=[BK, MTB[pb]])

        def att(ti):
            si, sq, tok0, nq, W, causal = tiles[ti]
            pb = ti % 2
            nkt = (W + 127) // 128
            maskT, MT = maskTs[pb], MTB[pb]
            for g in range(DKV):
                ao, AO = self.bank_of([0, 1], "bo")
                ad, AD = self.bank_of([2, 3], "bd")
                acc[(ti, g)] = (ao, AO, ad, AD)
            steps = [(g, kt) for g in range(DKV) for kt in range(nkt)]
            pend = {}

            def front(g, kt):
                kr = min(128, W - kt * 128)
                sb_, SB_ = self.bank_of([4, 5, 6, 7], "bs")
                s3 = sb_[:kr, 0:4 * nq].rearrange("p (r q) -> p r q", q=nq)
                self.mm(s3, kd[:, g, kt * 128:kt * 128 + kr], qd[pb][:, 4 * g:4 * g + 4, :nq], True, True,
                        r=[SEQB, QB[pb]], w=[SB_])
                e_, E_ = eT[st["ecount"] % NEB], ET[st["ecount"] % NEB]
                p_, P_ = pT[st["ecount"] % NEB], PT[st["ecount"] % NEB]
                st["ecount"] += 1
                self.act(lambda: a.activation(out=e_[:kr, :4 * nq], in_=sb_[:kr, :4 * nq], func=AF.Exp, scale=scale),
                         r=[], w=[SB_, E_])
                e3 = e_[:kr, 0:4 * nq].rearrange("p (r q) -> p r q", q=nq)
                p3 = p_[:kr, 0:4 * nq].rearrange("p (r q) -> p r q", q=nq)
                mb = maskT[:kr, kt, :nq].unsqueeze(1).to_broadcast([kr, 4, nq])
                self.pool(lambda: gp.tensor_tensor(out=p3, in0=e3, in1=mb, op=ALU.mult), r=[E_, MT], w=[P_])
                pend[(g, kt)] = (p_, P_, kr)

            def back(g, kt):
                p_, P_, kr = pend.pop((g, kt))
                ao, AO, ad, AD = acc[(ti, g)]
                first, last = kt == 0, kt == nkt - 1
                self.mm(ao[:, :4 * nq], vd[:kr, kt, g * 128:(g + 1) * 128], p_[:kr, :4 * nq], first, last, r=[SEQB, P_], w=[AO])
                self.mm(ad[:, :4 * nq], self.onesb[:kr, :], p_[:kr, :4 * nq], first, last, r=[self.CB, P_], w=[AD])

            LA = 2
            for k_ in range(len(steps) + LA):
                if k_ < len(steps):
                    front(*steps[k_])
                if k_ >= LA:
                    back(*steps[k_ - LA])

        def norm(ti):
            si, sq, tok0, nq, W, causal = tiles[ti]
            for g in range(DKV):
                ao, AO, ad, AD = acc.pop((ti, g))
                o, OB = oT[st["ocount"] % 2], OT[st["ocount"] % 2]
                st["ocount"] += 1
                self.dve(lambda ad=ad: v.reciprocal(out=rden[:, :4 * nq], in_=ad[:, :4 * nq]), r=[], w=[AD, RD])
                self.dve(lambda o=o, ao=ao: v.tensor_tensor(out=o[:, :4 * nq], in0=ao[:, :4 * nq], in1=rden[:, :4 * nq],
                                                             op=ALU.mult), r=[RD], w=[AO, OB])
                self.dma(X["odT"][g * 512:(g + 1) * 512, tok0:tok0 + nq].rearrange("(r p) t -> p r t", p=128),
                         o[:, :4 * nq].rearrange("p (r q) -> p r q", q=nq), r=[OB], w=[self.XB["odT"]])

        n = len(tiles)
        loadq(tiles[0], 0)
        seq_load(tiles[0])
        if n > 1:
            loadq(tiles[1], 1)
        idx(0)
        bisect(0)
        mask_tr(0)
        for ti in range(n):
            nxt = ti + 1 if ti + 1 < n else None
            same = nxt is not None and tiles[nxt][0] == tiles[ti][0]
            if same:
                idx(nxt)
            att(ti)
            if same:
                bisect(nxt)
            norm(ti)
            if nxt is not None and not same:
                seq_load(tiles[nxt])
                idx(nxt)
                bisect(nxt)
            if nxt is not None:
                mask_tr(nxt)
                if ti + 2 < n:
                    loadq(tiles[ti + 2], ti % 2)
        S.release(m)

    def phaseD(self):
        S, I, O, X, W, c = self.S, self.I, self.O, self.X, self.W, self.cfg
        nc = self.nc
        v, a, gp = nc.vector, nc.scalar, nc.gpsimd
        m = S.mark()
        NG = 512
        NJ = D_FF // 128
        ring = [S.sb("wring", [128, 8192], BF16) for _ in range(3)]
        RG = [self.B() for _ in range(3)]
        io = [S.sb("ioD", [128, D_MODEL], F32) for _ in range(2)]
        IO = [self.B() for _ in range(2)]
        gff = S.sb("gff", [128, 16], F32)
        cw = S.sb("cw", [128, 3, NJ], F32)
        cb = S.sb("cb", [128, NJ], F32)
        self.dma(gff[:], I["ffn_normT"][:, :], w=[self.CB])
        self.dma(cw[:], I["conv_wT"][:, :, :], w=[self.CB])
        self.dma(cb[:], I["conv_bT"][:, :], w=[self.CB])
        rc = [0]
        ioc = [0]

        def slot():
            i = rc[0] % 3
            rc[0] += 1
            return ring[i], RG[i]

        class Ctx:
            pass

        def mk(ngmax, nseqmax, tag):
            x = Ctx()
            x.xT = S.sb("xT" + tag, [128, 16, ngmax], F32)
            x.XT = [self.B() for _ in range(16)]
            x.r2 = S.sb("r2" + tag, [128, 16, ngmax], BF16)
            x.R2 = [self.B() for _ in range(16)]
            x.r1 = S.sb("r1" + tag, [128, NJ, ngmax], BF16)
            x.R1 = [self.B() for _ in range(NJ)]
            x.omT, x.odT, x.mT = x.r1[:, 0:8, :], x.r1[:, 8:16, :], x.r1[:, 16:32, :]
            x.sg = [S.sb("sg" + tag, [128, ngmax], F32) for _ in range(2)]
            x.SG = [self.B() for _ in range(2)]
            x.t12 = [S.sb("t12" + tag, [128, ngmax], F32) for _ in range(2)]
            x.T12 = [self.B() for _ in range(2)]
            x.rstd = S.sb("rstdD" + tag, [128, ngmax], F32)
            x.RS = self.B()
            x.sqb = [S.sb("sqbD" + tag, [128, ngmax], BF16) for _ in range(2)]
            x.SQ = [self.B() for _ in range(2)]
            x.gpx = [S.sb("gpx" + tag, [128, ngmax + 2 * nseqmax], F32) for _ in range(2)]
            x.GP = [self.B() for _ in range(2)]
            x.tt = [S.sb("ttD" + tag, [128, ngmax], F32) for _ in range(2)]
            x.TT_ = [self.B() for _ in range(2)]
            x.sl = [S.sb("slD" + tag, [128, ngmax], F32) for _ in range(2)]
            x.SL = [self.B() for _ in range(2)]
            x.carry = S.sb("carry" + tag, [128, NJ, 2 * nseqmax], F32)
            x.CY = [self.B() for _ in range(NJ)]
            return x

        P = mk(NG, 1, "p")
        Q = mk(c.TS, c.NS, "s")
        for j in range(NJ):
            self.pool(lambda j=j: gp.memset(P.carry[:, j, :], 0.0), w=[P.CY[j]])

        def setg(x, tok0, ng, nseq, L, smp):
            x.tok0, x.ng, x.nseq, x.L, x.smp = tok0, ng, nseq, L, smp
            x.last = smp or tok0 + ng == c.T

        def load_group(x):
            tok0, ng = x.tok0, x.ng
            if x.smp:
                for j in range(NJ):
                    self.dma(x.carry[:, j, 0:2 * x.nseq].rearrange("p (s r) -> p s r", r=2),
                             I["c_conv"][:, :, j * 128:(j + 1) * 128].rearrange("s r p -> p s r"), w=[x.CY[j]], slow=True)
            self.dma(x.r2[:, :, :ng], X["hT"][:, tok0:tok0 + ng].rearrange("(k p) t -> p k t", p=128), r=[self.XB["hT"]], w=x.R2)
            self.dma(x.omT[:, :, :ng], X["omT"][:, tok0:tok0 + ng].rearrange("(k p) t -> p k t", p=128), r=[self.XB["omT"]],
                     w=x.R1[0:8])
            self.dma(x.odT[:, :, :ng], X["odT"][:, tok0:tok0 + ng].rearrange("(k p) t -> p k t", p=128), r=[self.XB["odT"]],
                     w=x.R1[8:16])
            for q0 in range(0, ng, 128):
                nr = min(128, ng - q0)
                xi, XI = io[ioc[0] % 2], IO[ioc[0] % 2]
                ioc[0] += 1
                src = I["xs"][:, :] if x.smp else I["xp"][tok0 + q0:tok0 + q0 + nr, :]
                self.dma(xi[:nr, :], src, w=[XI])
                for k4 in range(4):
                    bk, BK = self.bank()
                    for kk in range(4):
                        k = 4 * k4 + kk
                        self.tr(bk[:, kk * 128:kk * 128 + nr], xi[:nr, k * 128:(k + 1) * 128], self.identf[:nr, :nr],
                                r=[XI, self.CB], w=[BK])
                    self.evac(k4, x.xT[:, 4 * k4:4 * k4 + 4, q0:q0 + nr],
                              bk[:, 0:512].rearrange("p (j t) -> p j t", t=128)[:, :, :nr], r=[], w=[BK] + x.XT[4 * k4:4 * k4 + 4])

        def merge_chunk(x, cc, SLB, wom, wod, wgm, wgd):
            ng = x.ng
            sg, SG, t12, T12 = x.sg, x.SG, x.t12, x.T12
            ba, BA = self.bank()
            for k in range(8):
                self.mm(ba[:, :ng], wom[:, k, :], x.omT[:, k, :ng], k == 0, k == 7, r=[SLB] + x.R1[0:8], w=[BA])
            bb, BB = self.bank()
            for k in range(8):
                self.mm(bb[:, :ng], wod[:, k, :], x.odT[:, k, :ng], k == 0, k == 7, r=[SLB] + x.R1[8:16], w=[BB])
            bgm, BGM = self.bank()
            for k in range(16):
                self.mm(bgm[:, :ng], wgm[:, k, :], x.r2[:, k, :ng], k == 0, k == 15, r=[SLB] + x.R2, w=[BGM])
            bgd, BGD = self.bank()
            for k in range(16):
                self.mm(bgd[:, :ng], wgd[:, k, :], x.r2[:, k, :ng], k == 0, k == 15, r=[SLB] + x.R2, w=[BGD])
            yield
            self.act(lambda: a.activation(out=sg[0][:, :ng], in_=bgm[:, :ng], func=AF.Sigmoid), w=[BGM, SG[0]])
            yield
            self.act(lambda: a.activation(out=sg[1][:, :ng], in_=bgd[:, :ng], func=AF.Sigmoid), w=[BGD, SG[1]])
            yield
            self.dve(lambda: v.tensor_tensor(out=t12[0][:, :ng], in0=sg[0][:, :ng], in1=ba[:, :ng], op=ALU.mult),
                     r=[SG[0]], w=[BA, T12[0]])
            yield
            self.dve(lambda: v.tensor_tensor(out=t12[1][:, :ng], in0=sg[1][:, :ng], in1=bb[:, :ng], op=ALU.mult),
                     r=[SG[1]], w=[BB, T12[1]])
            yield
            self.pool(lambda: gp.tensor_tensor(out=x.mT[:, cc, :ng], in0=t12[0][:, :ng], in1=t12[1][:, :ng], op=ALU.add),
                      r=T12, w=[x.R1[16 + cc]])
            yield

        def wout_chunk(x, cc, SLB, wo, ci):
            ng = x.ng
            bk, BK = self.bank()
            for k in range(16):
                self.mm(bk[:, :ng], wo[:, k, ci * 128:(ci + 1) * 128], x.mT[:, k, :ng], k == 0, k == 15,
                        r=[SLB] + x.R1[16:32], w=[BK])
            self.dve(lambda: v.tensor_tensor(out=x.xT[:, cc, :ng], in0=x.xT[:, cc, :ng], in1=bk[:, :ng], op=ALU.add),
                     r=[], w=[BK, x.XT[cc]])

        def rms_stage(x):
            ng = x.ng
            bs, BS = self.bank()
            for cc in range(16):
                q_, Q_ = x.sqb[cc % 2], x.SQ[cc % 2]
                self.act(lambda q_=q_, cc=cc: a.activation(out=q_[:, :ng], in_=x.xT[:, cc, :ng], func=AF.Square),
                         r=[x.XT[cc]], w=[Q_])
                self.mm(bs[:, :ng], self.onesb[:, :], q_[:, :ng], cc == 0, cc == 15, r=[Q_, self.CB], w=[BS])
            self.act(lambda: a.activation(out=x.rstd[:, :ng], in_=bs[:, :ng], func=AF.Sqrt, scale=1.0 / D_MODEL,
                                          bias=self.epsc[:, 0:1]), r=[self.CB], w=[BS, x.RS])
            self.dve(lambda: v.reciprocal(out=x.rstd[:, :ng], in_=x.rstd[:, :ng]), w=[x.RS])
            for cc in range(16):
                self.dve(lambda cc=cc: v.scalar_tensor_tensor(out=x.r2[:, cc, :ng], in0=x.xT[:, cc, :ng], scalar=gff[:, cc:cc + 1],
                                                              in1=x.rstd[:, :ng], op0=ALU.mult, op1=ALU.mult),
                         r=[x.XT[cc], x.RS, self.CB], w=[x.R2[cc]])

        def up_chunk(x, j, SLB, wg, wu):
            ng, nseq, L = x.ng, x.nseq, x.L
            bg, BG = self.bank()
            for k in range(16):
                self.mm(bg[:, :ng], wg[:, k, :], x.r2[:, k, :ng], k == 0, k == 15, r=[SLB] + x.R2, w=[BG])
            bu, BU = self.bank()
            for k in range(16):
                self.mm(bu[:, :ng], wu[:, k, :], x.r2[:, k, :ng], k == 0, k == 15, r=[SLB] + x.R2, w=[BU])
            pb = j % 2
            GPB, TTB, SLB_ = x.GP[pb], x.TT_[pb], x.SL[pb]
            g3 = x.gpx[pb][:, 0:nseq * (L + 2)].rearrange("p (s l) -> p s l", l=L + 2)
            bg3 = bg[:, 0:ng].rearrange("p (s l) -> p s l", l=L)
            t3 = x.tt[pb][:, 0:ng].rearrange("p (s l) -> p s l", l=L)
            cyv = x.carry[:, j, 0:2 * nseq].rearrange("p (s r) -> p s r", r=2)
            yield
            self.pool(lambda: gp.tensor_copy(out=g3[:, :, 0:2], in_=cyv), r=[x.CY[j]], w=[GPB])
            self.act(lambda: a.activation(out=g3[:, :, 2:L + 2], in_=bg3, func=AF.Copy), r=[], w=[BG, GPB])
            yield
            self.act(lambda: a.activation(out=t3, in_=bg3, func=AF.Identity, scale=cw[:, 2, j:j + 1], bias=cb[:, j:j + 1]),
                     r=[self.CB], w=[BG, TTB])
            yield
            self.dve(lambda: v.scalar_tensor_tensor(out=t3, in0=g3[:, :, 1:L + 1], scalar=cw[:, 1, j:j + 1], in1=t3,
                                                    op0=ALU.mult, op1=ALU.add), r=[GPB, self.CB], w=[TTB])
            yield
            self.dve(lambda: v.scalar_tensor_tensor(out=t3, in0=g3[:, :, 0:L], scalar=cw[:, 0, j:j + 1], in1=t3,
                                                    op0=ALU.mult, op1=ALU.add), r=[GPB, self.CB], w=[TTB])
            yield
            self.act(lambda: a.activation(out=x.sl[pb][:, :ng], in_=x.tt[pb][:, :ng], func=AF.Silu), r=[TTB], w=[SLB_])
            yield
            self.dve(lambda: v.tensor_tensor(out=x.r1[:, j, :ng], in0=x.sl[pb][:, :ng], in1=bu[:, :ng], op=ALU.mult),
                     r=[SLB_], w=[BU, x.R1[j]])
            yield
            self.pool(lambda: gp.tensor_copy(out=cyv, in_=g3[:, :, L:L + 2]), r=[GPB], w=[x.CY[j]])

        def conv_state_out(x):
            for j in range(NJ):
                if x.smp:
                    for s_ in range(x.nseq):
                        self.dma(O["s_conv"][s_, :, j * 128:(j + 1) * 128].rearrange("r p -> p r"), x.carry[:, j, 2 * s_:2 * s_ + 2],
                                 r=[x.CY[j]], slow=True, q="pool")
                else:
                    self.dma(O["p_conv"][:, j * 128:(j + 1) * 128].rearrange("r p -> p r"), x.carry[:, j, 0:2], r=[x.CY[j]],
                             slow=True, q="pool")

        def down_chunk(x, cc, SLB, wd):
            ng = x.ng
            bk, BK = self.bank()
            for k in range(NJ):
                self.mm(bk[:, :ng], wd[:, k, :], x.r1[:, k, :ng], k == 0, k == NJ - 1, r=[SLB] + x.R1, w=[BK])
            self.dve(lambda: v.tensor_tensor(out=x.xT[:, cc, :ng], in0=x.xT[:, cc, :ng], in1=bk[:, :ng], op=ALU.add),
                     r=[], w=[BK, x.XT[cc]])

        def out_stage(x):
            tok0, ng = x.tok0, x.ng
            for q0 in range(0, ng, 128):
                nr = min(128, ng - q0)
                yo, YO = io[ioc[0] % 2], IO[ioc[0] % 2]
                ioc[0] += 1
                for k4 in range(4):
                    bk, BK = self.bank()
                    for kk in range(4):
                        k = 4 * k4 + kk
                        self.tr(bk[:nr, kk * 128:(kk + 1) * 128], x.xT[:, k, q0:q0 + nr], self.identf[:, :], r=[x.XT[k], self.CB], w=[BK])
                    self.evac(k4, yo[:nr, k4 * 512:(k4 + 1) * 512], bk[:nr, :], r=[], w=[BK, YO])
                dst = O["y_s"][:, :] if x.smp else O["y_p"][tok0 + q0:tok0 + q0 + nr, :]
                self.dma(dst, yo[:nr, :], r=[YO])

        pg = [(g0, min(NG, c.T - g0)) for g0 in range(0, c.T, NG)]
        for gi, (tok0, ng) in enumerate(pg):
            setg(P, tok0, ng, 1, ng, False)
            unit = [P]
            if gi == len(pg) - 1:
                setg(Q, c.T, c.TS, c.NS, c.DEC, True)
                unit.append(Q)
            for x in unit:
                load_group(x)
            for cc in range(16):
                sl_, SLB = slot()
                wom = sl_[:, 0:1024].rearrange("p (k n) -> p k n", n=128)
                wod = sl_[:, 1024:2048].rearrange("p (k n) -> p k n", n=128)
                wgm = sl_[:, 2048:4096].rearrange("p (k n) -> p k n", n=128)
                wgd = sl_[:, 4096:6144].rearrange("p (k n) -> p k n", n=128)
                cs = slice(cc * 128, (cc + 1) * 128)
                self.dma(wom, W["w_o_mla"][:, cs].rearrange("(k p) n -> p k n", p=128), r=[self.WB["w_o_mla"]], w=[SLB])
                self.dma(wod, W["w_o_dsa"][:, cs].rearrange("(k p) n -> p k n", p=128), r=[self.WB["w_o_dsa"]], w=[SLB])
                self.dma(wgm, W["w_in"][:, NA + cc * 128:NA + (cc + 1) * 128].rearrange("(k p) n -> p k n", p=128),
                         r=[self.WB["w_in"]], w=[SLB])
                self.dma(wgd, W["w_in"][:, NA + D_MODEL + cc * 128:NA + D_MODEL + (cc + 1) * 128]
                         .rearrange("(k p) n -> p k n", p=128), r=[self.WB["w_in"]], w=[SLB])
                self.run(*[merge_chunk(x, cc, SLB, wom, wod, wgm, wgd) for x in unit])
            for c4 in range(4):
                sl_, SLB = slot()
                wo = sl_[:, 0:8192].rearrange("p (k n) -> p k n", n=512)
                self.dma(wo, W["w_out"][:, c4 * 512:(c4 + 1) * 512].rearrange("(k p) n -> p k n", p=128),
                         r=[self.WB["w_out"]], w=[SLB])
                for ci in range(4):
                    for x in unit:
                        wout_chunk(x, 4 * c4 + ci, SLB, wo, ci)
            for x in unit:
                rms_stage(x)
            for j in range(NJ):
                sl_, SLB = slot()
                wg = sl_[:, 0:2048].rearrange("p (k n) -> p k n", n=128)
                wu = sl_[:, 2048:4096].rearrange("p (k n) -> p k n", n=128)
                self.dma(wg, W["w_ffn_up"][:, j * 128:(j + 1) * 128].rearrange("(k p) n -> p k n", p=128),
                         r=[self.WB["w_ffn_up"]], w=[SLB])
                self.dma(wu, W["w_ffn_up"][:, D_FF + j * 128:D_FF + (j + 1) * 128].rearrange("(k p) n -> p k n", p=128),
                         r=[self.WB["w_ffn_up"]], w=[SLB])
                self.run(*[up_chunk(x, j, SLB, wg, wu) for x in unit])
            for cc in range(16):
                sl_, SLB = slot()
                wd = sl_[:, 0:NJ * 128].rearrange("p (k n) -> p k n", n=128)
                self.dma(wd, W["w_ffn_down"][:, cc * 128:(cc + 1) * 128].rearrange("(k p) n -> p k n", p=128),
                         r=[self.WB["w_ffn_down"]], w=[SLB])
                for x in unit:
                    down_chunk(x, cc, SLB, wd)
            for x in unit:
                out_stage(x)
            for x in unit:
                if x.last:
                    conv_state_out(x)
        S.release(m)

    def finish(self):
        self.S.emit()
        return self.nc


PHASES = ["P0", "A1", "A2", "B", "C", "D"]


def build(cfg, upto="D", debug=False):
    b = Builder(cfg, debug=debug)
    b.declare()
    b.setup_consts()
    b.phase0()
    for ph in PHASES[1:PHASES.index(upto) + 1]:
        getattr(b, "phase" + ph)()
    nc = b.finish()
    return b, nc


def rope_table(n):
    out = np.zeros((n, 2, 56), np.float32)
    pos = np.arange(n, dtype=np.float32)
    c0 = 0
    for half in (32, 16, 8):
        inv = (np.float32(500000.0) ** (-(np.arange(half, dtype=np.float32) / np.float32(half)))).astype(np.float32)
        ang = (pos[:, None] * inv[None, :]).astype(np.float32)
        out[:, 0, c0:c0 + half] = np.cos(ang)
        out[:, 1, c0:c0 + half] = np.sin(ang)
        c0 += half
    return out


def core_inputs(cfg, inp, core):
    NS = cfg.NS
    sl = slice(core * NS, (core + 1) * NS)
    f = lambda a: np.ascontiguousarray(a, dtype=np.float32)
    d = {
        "xp": f(inp["x_prompt"][core]),
        "xs": f(inp["x_sample"][sl].reshape(cfg.TS, D_MODEL)),
        "c_ckv": f(inp["cache_mla_ckv"][0, sl]),
        "c_kpe": f(inp["cache_mla_kpe"][0, sl]),
        "c_dk": f(inp["cache_dsa_k"][0, sl].reshape(NS, cfg.PAST, DKV * DHD)),
        "c_dv": f(inp["cache_dsa_v"][0, sl].reshape(NS, cfg.PAST, DKV * DHD)),
        "c_ik": f(inp["cache_idx_k"][0, sl]),
        "c_conv": f(inp["state_ffn_conv"][0, sl]),
        "ffn_normT": f(inp["ffn_norm"][0].reshape(D_MODEL // 128, 128).T),
        "conv_wT": f(inp["conv_w"][0].reshape(3, D_FF // 128, 128).transpose(2, 0, 1)),
        "conv_bT": f(inp["conv_b"][0].reshape(D_FF // 128, 128).T),
        "ident": np.eye(128, dtype=np.float32),
        "ropet": rope_table(max(cfg.T, cfg.PAST + cfg.DEC)),
    }
    for nm in ("attn_norm", "q_a_norm", "kv_a_norm", "mla_q_nope_norm", "mla_q_rope_norm", "mla_k_nope_norm",
               "mla_k_rope_norm", "dsa_q_norm", "dsa_k_norm"):
        d[nm] = f(inp[nm][0:1])
    for nm in ("w_in", "w_q_up", "w_kv_up", "w_o_mla", "w_o_dsa", "w_out", "w_ffn_up", "w_ffn_down"):
        d[nm] = f(inp[nm][0])
    return d


def run_core_debug(cfg, inp, upto):
    b, nc = build(cfg, upto, debug=True)
    print("stats", b.S.stats, flush=True)
    res = run_bass_kernel_spmd(nc, [core_inputs(cfg, inp, 0)], core_ids=[0]).results[0]
    outs = {k: res[k] for k in b.O}
    dbg = {k: res[v.tensor.name] if hasattr(v, "tensor") else None for k, v in b.X.items()}
    return outs, dbg


def kernel(**inputs):
    cfg = Cfg()
    inp = {k: np.asarray(v) for k, v in inputs.items()}
    b, nc = build(cfg, "D", debug=False)
    in_maps = [core_inputs(cfg, inp, core) for core in range(8)]
    res = run_bass_kernel_spmd(nc, in_maps, core_ids=list(range(8))).results
    g = lambda k: [np.asarray(r[k], dtype=np.float32) for r in res]
    NS, DEC, T = cfg.NS, cfg.DEC, cfg.T
    y_p = np.stack(g("y_p"), 0)
    y_s = np.concatenate([a.reshape(NS, DEC, D_MODEL) for a in g("y_s")], 0)
    p_ckv = np.stack(g("p_ckv"), 0)[None]
    p_kpe = np.stack(g("p_kpe"), 0)[None]
    p_dk = np.stack(g("p_dk"), 0).reshape(1, 8, T, DKV, DHD)
    p_dv = np.stack(g("p_dv"), 0).reshape(1, 8, T, DKV, DHD)
    p_ik = np.stack(g("p_ik"), 0)[None]
    p_conv = np.stack(g("p_conv"), 0)[None]
    cat = lambda k, *shp: np.concatenate([a.reshape((NS, DEC) + shp) for a in g(k)], 0)[None]
    s_ckv = cat("s_ckv", KV_LORA)
    s_kpe = cat("s_kpe", ROPE)
    s_dk = cat("s_dk", DKV, DHD)
    s_dv = cat("s_dv", DKV, DHD)
    s_ik = cat("s_ik", IDIM)
    s_conv = np.concatenate(g("s_conv"), 0)[None]
    return (y_p, y_s, p_ckv, p_kpe, p_dk, p_dv, p_ik, p_conv, s_ckv, s_kpe, s_dk, s_dv, s_ik, s_conv)
```

```python
import numpy as np
import concourse.bass as bass
import concourse.mybir as mybir
from concourse.bass_utils import run_bass_kernel_spmd

F32 = mybir.dt.float32
BF16 = mybir.dt.bfloat16
AF = mybir.ActivationFunctionType
ALU = mybir.AluOpType
AX = mybir.AxisListType

D_MODEL = 2048
CHUNK = 64
EPS = 1e-6
NEG = -1e30
Q_LORA, KV_LORA = 512, 256
NOPE, ROPE, MV, MH = 128, 64, 128, 8
DH, DKV, DHD, DROT = 8, 2, 128, 32
IH, IDIM, IROT = 16, 64, 16
TOPK_MAX = 256
D_FF = 5632
NA1 = 1856
NA2 = 1616
NA = NA1 + NA2
IN_COLS = NA + 2 * D_MODEL


class Cfg:
    def __init__(self, T=4096, NS=4, PAST=1024, DEC=16):
        self.T, self.NS, self.PAST, self.DEC = T, NS, PAST, DEC
        self.NT = T // 128
        self.TS = NS * DEC
        self.TT = T + self.TS
        self.SS = PAST + DEC
        self.KT = T + NS * self.SS


class Buf:
    __slots__ = ("name", "w", "r")

    def __init__(self, name):
        self.name, self.w, self.r = name, None, []


class Op:
    __slots__ = ("eng", "fn", "deps", "dma")

    def __init__(self, eng, fn, deps, dma):
        self.eng, self.fn, self.deps, self.dma = eng, fn, deps, dma


class Sched:
    ND = {"sp": 8, "pool": 8, "act": 4}

    def __init__(self, nc):
        self.nc = nc
        self.ops = []
        self.eng = {"pe": nc.tensor, "act": nc.scalar, "dve": nc.vector, "pool": nc.gpsimd, "sp": nc.sync}
        self.sb_off = 16512
        self.sb_end = 229376

    def mark(self):
        return self.sb_off

    def release(self, m):
        self.sb_off = m

    def sb(self, name, shape, dtype):
        nbytes = int(np.prod(shape[1:])) * (4 if dtype == F32 else 2)
        off = (self.sb_off + 63) // 64 * 64
        assert off + nbytes <= self.sb_end, f"SBUF overflow at {name}: {off}+{nbytes}"
        self.sb_off = off + nbytes
        self._n = getattr(self, "_n", 0) + 1
        return self.nc.alloc_sbuf_tensor_at(f"{name}_{self._n}", list(shape), dtype, offset=off)

    def op(self, eng, fn, reads=(), writes=(), dma=False):
        deps = set()
        for b in reads:
            if b.w is not None:
                deps.add(b.w)
        for b in writes:
            if b.w is not None:
                deps.add(b.w)
            deps.update(b.r)
        i = len(self.ops)
        self.ops.append(Op(eng, fn, deps, dma))
        for b in reads:
            b.r.append(i)
        for b in writes:
            b.w = i
            b.r = []
        return i

    def dma(self, q, out, in_, reads=(), writes=(), slow=False):
        e = self.eng[q]
        if slow:
            return self.op(q, lambda: e.dma_start(out=out, in_=in_, allow_slow_non_contiguous=True), reads, writes, dma=True)
        return self.op(q, lambda: e.dma_start(out=out, in_=in_), reads, writes, dma=True)

    def emit(self):
        nc, ops = self.nc, self.ops
        n = len(ops)
        need = [False] * n
        for o in ops:
            for d in o.deps:
                p = ops[d]
                if p.dma:
                    continue
                if p.eng == "pe" and o.eng == "pe" and not o.dma:
                    continue
                need[d] = True
        sems = {}

        def sem(key):
            if key not in sems:
                sems[key] = nc.alloc_semaphore("s_" + "_".join(str(k) for k in key))
            return sems[key]

        cnt, dcnt, tok = {}, {}, [None] * n
        for i, o in enumerate(ops):
            if o.dma:
                k = dcnt.get(o.eng, 0)
                dcnt[o.eng] = k + 1
                nd = self.ND[o.eng]
                tok[i] = (("d", o.eng, k % nd), 16 * (k // nd + 1))
            elif need[i]:
                cnt[o.eng] = cnt.get(o.eng, 0) + 1
                tok[i] = (("c", o.eng), cnt[o.eng])
        seen = {e: {} for e in self.eng}
        nwaits = 0
        for i, o in enumerate(ops):
            E = o.eng
            waits = {}
            for d in o.deps:
                p = ops[d]
                if (not p.dma) and p.eng == "pe" and E == "pe" and not o.dma:
                    continue
                s, v = tok[d]
                if waits.get(s, 0) < v:
                    waits[s] = v
            if o.dma:
                s, v = tok[i]
                if v > 16 and waits.get(s, 0) < v - 16:
                    waits[s] = v - 16
            for s, v in waits.items():
                if seen[E].get(s, 0) >= v:
                    continue
                seen[E][s] = v
                self.eng[E].wait_ge(sem(s), v)
                nwaits += 1
            ins = o.fn()
            if tok[i] is not None:
                ins.then_inc(sem(tok[i][0]), 16 if o.dma else 1)
        for q, k in dcnt.items():
            nd = self.ND[q]
            for slot in range(min(nd, k)):
                last = (k - 1 - slot) // nd * nd + slot
                v = 16 * (last // nd + 1)
                if seen["sp"].get(("d", q, slot), 0) < v:
                    nc.sync.wait_ge(sem(("d", q, slot)), v)
        for e, c in cnt.items():
            nc.sync.wait_ge(sem(("c", e)), c)
        self.stats = dict(n_ops=n, n_waits=nwaits, n_sems=len(sems))


class Builder:
    def __init__(self, cfg, debug=False):
        self.cfg = cfg
        self.debug = debug
        self.nc = bass.Bass("TRN2", target_bir_lowering=False)
        self.S = Sched(self.nc)
        self._bufn = 0

    def B(self, name="b"):
        self._bufn += 1
        return Buf(f"{name}{self._bufn}")

    def din(self, name, shape, dt=F32):
        return self.nc.dram_tensor(name, list(shape), dt, kind="ExternalInput").ap()

    def dout(self, name, shape, dt=F32):
        return self.nc.dram_tensor(name, list(shape), dt, kind="ExternalOutput").ap()

    def dscr(self, name, shape, dt=BF16):
        kind = "ExternalOutput" if (self.debug and not name.startswith("w_")) else "Internal"
        return self.nc.dram_tensor(name, list(shape), dt, kind=kind).ap()

    def pe(self, fn, r=(), w=()):
        return self.S.op("pe", fn, r, w)

    def act(self, fn, r=(), w=()):
        return self.S.op("act", fn, r, w)

    def dve(self, fn, r=(), w=()):
        return self.S.op("dve", fn, r, w)

    def pool(self, fn, r=(), w=()):
        return self.S.op("pool", fn, r, w)

    def dma(self, out, in_, r=(), w=(), q="sp", slow=False):
        return self.S.dma(q, out, in_, r, w, slow=slow)

    def mm(self, out, lhsT, rhs, start, stop, r=(), w=()):
        t = self.nc.tensor
        return self.pe(lambda: t.matmul(out, lhsT=lhsT, rhs=rhs, start=start, stop=stop), r, w)

    def tr(self, out, in_, ident, r=(), w=()):
        t = self.nc.tensor
        return self.pe(lambda: t.transpose(out, in_, ident), r, w)

    def declare(self):
        c = self.cfg
        T, NS, PAST, TS, TT, KT = c.T, c.NS, c.PAST, c.TS, c.TT, c.KT
        I = {}
        I["xp"] = self.din("xp", [T, D_MODEL])
        I["xs"] = self.din("xs", [TS, D_MODEL])
        I["c_ckv"] = self.din("c_ckv", [NS, PAST, KV_LORA])
        I["c_kpe"] = self.din("c_kpe", [NS, PAST, ROPE])
        I["c_dk"] = self.din("c_dk", [NS, PAST, DKV * DHD])
        I["c_dv"] = self.din("c_dv", [NS, PAST, DKV * DHD])
        I["c_ik"] = self.din("c_ik", [NS, PAST, IDIM])
        I["c_conv"] = self.din("c_conv", [NS, 2, D_FF])
        I["attn_norm"] = self.din("attn_norm", [1, D_MODEL])
        I["w_in"] = self.din("w_in", [D_MODEL, IN_COLS])
        I["q_a_norm"] = self.din("q_a_norm", [1, Q_LORA])
        I["w_q_up"] = self.din("w_q_up", [Q_LORA, MH * (NOPE + ROPE)])
        I["kv_a_norm"] = self.din("kv_a_norm", [1, KV_LORA])
        I["w_kv_up"] = self.din("w_kv_up", [KV_LORA, MH * (NOPE + MV)])
        for nm, d in (("mla_q_nope_norm", NOPE), ("mla_q_rope_norm", ROPE), ("mla_k_nope_norm", NOPE),
                      ("mla_k_rope_norm", ROPE), ("dsa_q_norm", DHD), ("dsa_k_norm", DHD)):
            I[nm] = self.din(nm, [1, d])
        I["w_o_mla"] = self.din("w_o_mla", [MH * MV, D_MODEL])
        I["w_o_dsa"] = self.din("w_o_dsa", [DH * DHD, D_MODEL])
        I["w_out"] = self.din("w_out", [D_MODEL, D_MODEL])
        I["ffn_normT"] = self.din("ffn_normT", [128, D_MODEL // 128])
        I["w_ffn_up"] = self.din("w_ffn_up", [D_MODEL, 2 * D_FF])
        I["conv_wT"] = self.din("conv_wT", [128, 3, D_FF // 128])
        I["conv_bT"] = self.din("conv_bT", [128, D_FF // 128])
        I["w_ffn_down"] = self.din("w_ffn_down", [D_FF, D_MODEL])
        I["ident"] = self.din("ident", [128, 128])
        I["ropet"] = self.din("ropet", [max(T, PAST + c.DEC), 2, 56])
        self.I = I
        O = {}
        O["y_p"] = self.dout("y_p", [T, D_MODEL])
        O["y_s"] = self.dout("y_s", [TS, D_MODEL])
        O["p_ckv"] = self.dout("p_ckv", [T, KV_LORA])
        O["p_kpe"] = self.dout("p_kpe", [T, ROPE])
        O["p_dk"] = self.dout("p_dk", [T, DKV * DHD])
        O["p_dv"] = self.dout("p_dv", [T, DKV * DHD])
        O["p_ik"] = self.dout("p_ik", [T, IDIM])
        O["p_conv"] = self.dout("p_conv", [2, D_FF])
        O["s_ckv"] = self.dout("s_ckv", [TS, KV_LORA])
        O["s_kpe"] = self.dout("s_kpe", [TS, ROPE])
        O["s_dk"] = self.dout("s_dk", [TS, DKV * DHD])
        O["s_dv"] = self.dout("s_dv", [TS, DKV * DHD])
        O["s_ik"] = self.dout("s_ik", [TS, IDIM])
        O["s_conv"] = self.dout("s_conv", [NS, 2, D_FF])
        self.O = O
        W = {}
        for nm in ("w_in", "w_q_up", "w_kv_up", "w_o_mla", "w_o_dsa", "w_out", "w_ffn_up", "w_ffn_down"):
            W[nm] = self.dscr(nm + "_b", I[nm].shape)
        self.W = W
        X = {}
        X["hT"] = self.dscr("hT_s", [D_MODEL, TT])
        X["qnT"] = self.dscr("qnT_s", [MH * NOPE, TT])
        X["qrT"] = self.dscr("qrT_s", [MH * ROPE, TT])
        X["knT"] = self.dscr("knT_s", [MH * NOPE, KT])
        X["kpeT"] = self.dscr("kpeT_s", [ROPE, KT])
        X["v"] = self.dscr("v_s", [KT, MH * MV])
        X["qdT"] = self.dscr("qdT_s", [DH * DHD, TT])
        X["kdT"] = self.dscr("kdT_s", [DKV * DHD, KT])
        X["vd"] = self.dscr("vd_s", [KT, DKV * DHD])
        X["qiT"] = self.dscr("qiT_s", [IH * IDIM, TT])
        X["kiT"] = self.dscr("kiT_s", [IDIM, KT])
        X["wi"] = self.dscr("wi_s", [TT, 2, IH], F32)
        X["omT"] = self.dscr("omT_s", [MH * MV, TT])
        X["odT"] = self.dscr("odT_s", [DH * DHD, TT])
        self.X = X
        self.XB = {k: self.B("x_" + k) for k in X}
        self.WB = {k: self.B("w_" + k) for k in W}
        if self.debug:
            self.DBG = {}

    def token_tiles(self):
        c = self.cfg
        tiles = [(i * 128, 128, False) for i in range(c.NT)]
        tiles.append((c.T, c.TS, True))
        return tiles

    def setup_consts(self):
        S, I, c = self.S, self.I, self.cfg
        self.psum = [self.nc.alloc_psum_tensor(f"ps{i}", [128, 512], F32) for i in range(8)]
        self.PB = [self.B(f"psb{i}") for i in range(8)]
        self.identf = S.sb("identf", [128, 128], F32)
        self.identb = S.sb("identb", [128, 128], BF16)
        self.onesb = S.sb("onesb", [128, 128], BF16)
        self.CB = self.B("consts")
        self.dma(self.identf[:], I["ident"][:, :], w=[self.CB])
        v = self.nc.vector
        self.dve(lambda: v.tensor_copy(out=self.identb[:], in_=self.identf[:]), r=[self.CB], w=[self.CB])
        self.dve(lambda: v.memset(self.onesb[:], 1.0), w=[self.CB])
        self.NI = 14
        self.ctab = S.sb("ctab", [128, self.NI], F32)
        for i in range(self.NI):
            self.dve(lambda i=i: v.memset(self.ctab[:, i:i + 1], float(2.0 ** -(i + 1))), w=[self.CB])
        self.epsc = S.sb("epsc", [128, 1], F32)
        self.dve(lambda: v.memset(self.epsc[:], EPS), w=[self.CB])

    def rep_gain(self, name, d):
        t = self.S.sb("g_" + name, [128, d], F32)
        self.dma(t[:], self.I[name][0:1, :].partition_broadcast(128), w=[self.CB])
        return t

    EARLY_W = ("w_in", "w_q_up", "w_kv_up")

    def cast_weights(self, names):
        for nm in names:
            w, src = self.W[nm], self.I[nm]
            rows = src.shape[0]
            for r0 in range(0, rows, 128):
                self.dma(w[r0:r0 + 128, :], src[r0:r0 + 128, :], w=[self.WB[nm]], q="pool")

    def phase0(self):
        self.cast_weights(self.EARLY_W)

    def rms_gen(self, nr, src, H, D, gain, dst, sq, ss, rB, wB, tmpB):
        v = self.nc.vector
        a = self.nc.scalar
        sqv = sq[:nr, 0:H * D].rearrange("p (h d) -> p h d", d=D)
        self.dve(lambda: v.tensor_tensor(out=sqv, in0=src, in1=src, op=ALU.mult), r=rB, w=[tmpB])
        yield
        ssv = ss[:nr, 0:H]
        self.dve(lambda: v.tensor_reduce(out=ssv, in_=sqv, axis=AX.X, op=ALU.add), r=[tmpB], w=[tmpB])
        yield
        self.act(lambda: a.activation(out=ssv, in_=ssv, func=AF.Sqrt, scale=1.0 / D, bias=self.epsc[:nr, 0:1]),
                 r=[tmpB, self.CB], w=[tmpB])
        yield
        self.dve(lambda: v.reciprocal(out=ssv, in_=ssv), r=[tmpB], w=[tmpB])
        yield
        rsb = ssv.unsqueeze(2).to_broadcast([nr, H, D])
        self.dve(lambda: v.tensor_tensor(out=sqv, in0=src, in1=rsb, op=ALU.mult), r=list(rB) + [tmpB], w=[tmpB])
        yield
        gb = gain[:nr, :].unsqueeze(1).to_broadcast([nr, H, D])
        self.dve(lambda: v.tensor_tensor(out=dst, in0=sqv, in1=gb, op=ALU.mult), r=[tmpB, self.CB], w=wB)
        yield

    @staticmethod
    def run(*gens):
        gens = list(gens)
        while gens:
            for g in list(gens):
                try:
                    next(g)
                except StopIteration:
                    gens.remove(g)

    def rms_heads(self, nr, src, H, D, gain, dst, sq, ss, rB, wB, tmpB):
        self.run(self.rms_gen(nr, src, H, D, gain, dst, sq, ss, rB, wB, tmpB))

    def rope(self, nr, src, dst, H, half, cos, sin, tmp, rB, wB, tmpB, eng="pool"):
        g = self.nc.gpsimd if eng == "pool" else self.nc.vector
        opf = self.pool if eng == "pool" else self.dve
        x1, x2 = src[:, :, 0:half], src[:, :, half:2 * half]
        cb = cos.unsqueeze(1).to_broadcast([nr, H, half])
        sn = sin.unsqueeze(1).to_broadcast([nr, H, half])
        t = [tmp[:nr, k * H * half:(k + 1) * H * half].rearrange("p (h d) -> p h d", d=half) for k in range(4)]
        opf(lambda: g.tensor_tensor(out=t[0], in0=x1, in1=cb, op=ALU.mult), r=rB, w=[tmpB])
        opf(lambda: g.tensor_tensor(out=t[1], in0=x2, in1=sn, op=ALU.mult), r=rB, w=[tmpB])
        opf(lambda: g.tensor_tensor(out=t[2], in0=x2, in1=cb, op=ALU.mult), r=rB, w=[tmpB])
        opf(lambda: g.tensor_tensor(out=t[3], in0=x1, in1=sn, op=ALU.mult), r=rB, w=[tmpB])
        opf(lambda: g.tensor_tensor(out=dst[:, :, 0:half], in0=t[0], in1=t[1], op=ALU.subtract), r=[tmpB], w=wB)
        opf(lambda: g.tensor_tensor(out=dst[:, :, half:2 * half], in0=t[2], in1=t[3], op=ALU.add), r=[tmpB], w=wB)

    def bank(self):
        self._bk = (getattr(self, "_bk", -1) + 1) % 8
        return self.psum[self._bk], self.PB[self._bk]

    def key_dsts(self, t0, nr, is_sample):
        c = self.cfg
        if not is_sample:
            return [(t0, 0, nr)]
        return [(c.T + s * c.SS + c.PAST, s * c.DEC, c.DEC) for s in range(c.NS)]

    def evac(self, k, out, in_, r, w):
        if k % 2 == 0:
            a = self.nc.scalar
            return self.act(lambda: a.activation(out=out, in_=in_, func=AF.Copy), r, w)
        v = self.nc.vector
        return self.dve(lambda: v.tensor_copy(out=out, in_=in_), r, w)

    def transpose_blocks(self, nr, srcs, rB, dst, wB, k0=0):
        n = len(srcs)
        for g0 in range(0, n, 8):
            g = srcs[g0:g0 + 8]
            bk, BK = self.bank()
            bb = bk[:].bitcast(BF16)
            for j, s_ap in enumerate(g):
                wd = s_ap.shape[1]
                self.tr(bb[:wd, j * 128:j * 128 + nr], s_ap, self.identb[:nr, :nr], r=list(rB) + [self.CB], w=[BK])
            src = bb[:, 0:len(g) * 128].rearrange("p (j t) -> p j t", t=128)[:, :, :nr]
            self.evac(k0 + g0 // 8, dst[:, g0:g0 + len(g), :nr], src, r=[], w=[BK] + list(wB))

    def kside_mla(self, nr, ckvb, kpeb, inB, dsts, R, split=False):
        X, v, a = self.X, self.nc.vector, self.nc.scalar
        self.transpose_blocks(nr, [ckvb[:nr, 0:128], ckvb[:nr, 128:256]], inB, R["ckvT"], [R["CKVT"]])
        for n0 in range(4):
            bk, BK = self.bank()
            for k in range(2):
                self.mm(bk[:nr, :], R["ckvT"][:, k, :nr], R["wkv"][:, k, n0 * 512:(n0 + 1) * 512], k == 0, k == 1,
                        r=[R["CKVT"], R["WKV"]], w=[BK])
            self.evac(n0, R["kvf"][:nr, n0 * 512:(n0 + 1) * 512], bk[:nr, :], r=[], w=[BK, R["KVF"][n0]])
        if split:
            return
        self.run(self.kside_kn_gen(nr, R))
        self.kside_b(nr, kpeb, inB, dsts, R)

    def kside_kn_gen(self, nr, R):
        kv3 = R["kvf"][:nr, :].rearrange("p (h d) -> p h d", d=256)
        knb3 = R["knb"][:nr, :].rearrange("p (h d) -> p h d", d=128)
        return self.rms_gen(nr, kv3[:, :, 0:128], 8, 128, R["g_kn"], knb3, R["sq2"][:, 1024:2048], R["ss"][:, 16:24], R["KVF"],
                            [R["KNB"]], R["TMPK"])

    def kside_b(self, nr, kpeb, inB, dsts, R):
        X, v, a = self.X, self.nc.vector, self.nc.scalar
        kv3 = R["kvf"][:nr, :].rearrange("p (h d) -> p h d", d=256)
        vb3 = R["vb"][:nr, :].rearrange("p (h d) -> p h d", d=128)
        self.act(lambda: a.activation(out=vb3, in_=kv3[:, :, 128:256], func=AF.Copy), r=R["KVF"], w=[R["VB"]])
        blocks = [R["knb"][:nr, h * 128:(h + 1) * 128] for h in range(8)]
        self.transpose_blocks(nr, blocks, [R["KNB"]], R["tstk"], [R["TSTK"]], k0=1)
        self.transpose_blocks(nr, [kpeb[:nr, 0:64]], inB, R["tstk"][:, 8:9, :], [R["TSTK"]])
        for (koff, c0, n) in dsts:
            self.dma(X["knT"][:, koff:koff + n].rearrange("(h p) t -> p h t", p=128), R["tstk"][:, 0:8, c0:c0 + n],
                     r=[R["TSTK"]], w=[self.XB["knT"]])
            self.dma(X["kpeT"][:, koff:koff + n], R["tstk"][0:64, 8, c0:c0 + n], r=[R["TSTK"]], w=[self.XB["kpeT"]])
            self.dma(X["v"][koff:koff + n, :], R["vb"][c0:c0 + n, :], r=[R["VB"]], w=[self.XB["v"]])

    def phaseA1(self):
        S, I, O, X, W, c = self.S, self.I, self.O, self.X, self.W, self.cfg
        nc = self.nc
        v, a, gp = nc.vector, nc.scalar, nc.gpsimd
        m = S.mark()
        R = {}
        wA = S.sb("wA1", [128, 16, NA1], BF16)
        WA = self.B()
        for k in range(16):
            self.dma(wA[:, k, :], W["w_in"][k * 128:(k + 1) * 128, 0:NA1], r=[self.WB["w_in"]], w=[WA])
        wq = S.sb("wq", [128, 4, 1536], BF16)
        WQ = self.B()
        for k in range(4):
            self.dma(wq[:, k, :], W["w_q_up"][k * 128:(k + 1) * 128, :], r=[self.WB["w_q_up"]], w=[WQ])
        R["wkv"] = S.sb("wkv", [128, 2, 2048], BF16)
        R["WKV"] = self.B()
        for k in range(2):
            self.dma(R["wkv"][:, k, :], W["w_kv_up"][k * 128:(k + 1) * 128, :], r=[self.WB["w_kv_up"]], w=[R["WKV"]])
        g_attn = self.rep_gain("attn_norm", D_MODEL)
        g_qa = self.rep_gain("q_a_norm", Q_LORA)
        g_kva = self.rep_gain("kv_a_norm", KV_LORA)
        g_qn = self.rep_gain("mla_q_nope_norm", NOPE)
        g_qr = self.rep_gain("mla_q_rope_norm", ROPE)
        R["g_kn"] = self.rep_gain("mla_k_nope_norm", NOPE)
        g_kr = self.rep_gain("mla_k_rope_norm", ROPE)
        g_dq = self.rep_gain("dsa_q_norm", DHD)
        xbuf = [S.sb("x", [128, D_MODEL], F32) for _ in range(2)]
        XBF = [self.B() for _ in range(2)]
        R["sq"] = S.sb("sq", [128, 2048], F32)
        R["sq2"] = S.sb("sq2", [128, 2048], F32)
        R["ss"] = S.sb("ss", [128, 32], F32)
        R["TMP"] = self.B()
        R["TMPK"] = self.B()
        TM = [self.B() for _ in range(4)]
        ss0 = S.sb("ss0", [128, 2], F32)
        SS0 = self.B()
        sqj = S.sb("sqj", [128, 2048], BF16)
        SQJ = self.B()
        hb = S.sb("hb", [128, D_MODEL], BF16)
        HB = self.B()
        hT = [S.sb("hT", [128, 16, 128], BF16) for _ in range(2)]
        HT = [self.B() for _ in range(2)]
        pjs = [S.sb("pj", [128, NA1], F32) for _ in range(2)]
        PJS = [[self.B() for _ in range(4)] for _ in range(2)]
        cqb = S.sb("cqb", [128, 512], BF16)
        CQB = self.B()
        ckvf = S.sb("ckvf", [128, 256], F32)
        CKVF = self.B()
        ckvb = S.sb("ckvb", [128, 256], BF16)
        CKVB = self.B()
        kpef = S.sb("kpef", [128, 64], F32)
        KPEF = self.B()
        kpeb = S.sb("kpeb", [128, 64], BF16)
        KPEB = self.B()
        cqT = S.sb("cqT", [128, 4, 128], BF16)
        CQT = self.B()
        R["ckvT"] = S.sb("ckvT", [128, 2, 128], BF16)
        R["CKVT"] = self.B()
        qf = S.sb("qf", [128, 1536], F32)
        QF = [self.B() for _ in range(3)]
        qnb = S.sb("qnb", [128, 1024], BF16)
        QNB = self.B()
        qrf = S.sb("qrf", [128, 512], F32)
        QRF = self.B()
        qrb = S.sb("qrb", [128, 512], BF16)
        QRB = self.B()
        R["kvf"] = S.sb("kvf", [128, 2048], F32)
        R["KVF"] = [self.B() for _ in range(4)]
        R["knb"] = S.sb("knb", [128, 1024], BF16)
        R["KNB"] = self.B()
        R["vb"] = S.sb("vb", [128, 1024], BF16)
        R["VB"] = self.B()
        R["tstk"] = S.sb("tstk", [128, 9, 128], BF16)
        R["TSTK"] = self.B()
        tstq = S.sb("tstq", [128, 12, 128], BF16)
        TSTQ = self.B()
        tstd = S.sb("tstd", [128, 8, 128], BF16)
        TSTD = self.B()
        qdb = S.sb("qdb", [128, 1024], BF16)
        QDB = self.B()
        rtmp = S.sb("rtmp", [128, 1024], F32)
        RTMP = self.B()
        ropes = [S.sb("ropes", [128, 2, 56], F32) for _ in range(2)]
        RP = [self.B() for _ in range(2)]
        cin = S.sb("cin", [128, 320], F32)
        CIN = self.B()

        tiles_ = self.token_tiles()

        def front(idx):
            t0, nr, smp = tiles_[idx]
            pb = idx % 2
            xt, XT = xbuf[pb], XBF[pb]
            rp, RPB = ropes[pb], RP[pb]
            pj, PJ = pjs[pb], PJS[pb]
            if smp:
                self.dma(xt[:nr, :], I["xs"][:, :], w=[XT])
                for s in range(c.NS):
                    self.dma(rp[s * c.DEC:(s + 1) * c.DEC, :, :], I["ropet"][c.PAST:c.PAST + c.DEC, :, :], w=[RPB])
            else:
                self.dma(xt[:nr, :], I["xp"][t0:t0 + nr, :], w=[XT])
                self.dma(rp[:nr, :, :], I["ropet"][t0:t0 + nr, :, :], w=[RPB])
            self.act(lambda xt=xt, nr=nr: a.activation(out=sqj[:nr, :], in_=xt[:nr, :], func=AF.Square,
                                                         accum_out=ss0[:nr, 0:1]), r=[XT], w=[SQJ, SS0])
            self.act(lambda nr=nr: a.activation(out=ss0[:nr, 0:1], in_=ss0[:nr, 0:1], func=AF.Sqrt, scale=1.0 / D_MODEL,
                                                 bias=self.epsc[:nr, 0:1]), r=[self.CB], w=[SS0])
            self.dve(lambda nr=nr: v.reciprocal(out=ss0[:nr, 0:1], in_=ss0[:nr, 0:1]), w=[SS0])
            self.dve(lambda xt=xt, nr=nr: v.scalar_tensor_tensor(out=hb[:nr, :], in0=xt[:nr, :], scalar=ss0[:nr, 0:1],
                                                                  in1=g_attn[:nr, :], op0=ALU.mult, op1=ALU.mult),
                     r=[XT, SS0, self.CB], w=[HB])
            self.transpose_blocks(nr, [hb[:nr, k * 128:(k + 1) * 128] for k in range(16)], [HB], hT[pb], [HT[pb]])
            self.dma(X["hT"][:, t0:t0 + nr].rearrange("(k p) t -> p k t", p=128), hT[pb][:, :, :nr], r=[HT[pb]],
                     w=[self.XB["hT"]])
            for gi, (c0, cw) in enumerate(((0, 512), (512, 512), (1024, 512), (1536, 320))):
                bk, BK = self.bank()
                for k in range(16):
                    self.mm(bk[:nr, :cw], hT[pb][:, k, :nr], wA[:, k, c0:c0 + cw], k == 0, k == 15, r=[HT[pb], WA], w=[BK])
                self.evac(gi, pj[:nr, c0:c0 + cw], bk[:nr, :cw], r=[], w=[BK, PJ[gi]])
        def back(idx):
            t0, nr, smp = tiles_[idx]
            pb = idx % 2
            rp, RPB = ropes[pb], RP[pb]
            pj, PJ = pjs[pb], PJS[pb]
            sq, sq2, ss = R["sq"], R["sq2"], R["ss"]
            kp3 = kpef[:nr, :].rearrange("p (h d) -> p h d", d=64)
            qd3 = pj[:nr, 832:1856].rearrange("p (h d) -> p h d", d=128)
            qdf3 = sq2[:nr, 1024:2048].rearrange("p (h d) -> p h d", d=128)
            self.run(
                self.rms_gen(nr, pj[:nr, 0:512].rearrange("p (h d) -> p h d", d=512), 1, 512, g_qa,
                             cqb[:nr, :].rearrange("p (h d) -> p h d", d=512), sq[:, 0:512], ss[:, 0:1], [PJ[0]], [CQB], TM[0]),
                self.rms_gen(nr, pj[:nr, 512:768].rearrange("p (h d) -> p h d", d=256), 1, 256, g_kva,
                             ckvf[:nr, :].rearrange("p (h d) -> p h d", d=256), sq[:, 512:768], ss[:, 1:2], [PJ[1]], [CKVF], TM[1]),
                self.rms_gen(nr, pj[:nr, 768:832].rearrange("p (h d) -> p h d", d=64), 1, 64, g_kr, kp3, sq[:, 768:832], ss[:, 2:3],
                             [PJ[1]], [KPEF], TM[2]),
                self.rms_gen(nr, qd3, 8, 128, g_dq, qdf3, sq2[:, 0:1024], ss[:, 3:11], [PJ[1], PJ[2], PJ[3]], [TM[3]], TM[3]),
            )
            self.transpose_blocks(nr, [cqb[:nr, k * 128:(k + 1) * 128] for k in range(4)], [CQB], cqT, [CQT])
            for n0 in range(3):
                bk, BK = self.bank()
                for k in range(4):
                    self.mm(bk[:nr, :], cqT[:, k, :nr], wq[:, k, n0 * 512:(n0 + 1) * 512], k == 0, k == 3, r=[CQT, WQ], w=[BK])
                self.evac(n0 + 1, qf[:nr, n0 * 512:(n0 + 1) * 512], bk[:nr, :], r=[], w=[BK, QF[n0]])
            self.act(lambda: a.activation(out=ckvb[:nr, :], in_=ckvf[:nr, :], func=AF.Copy), r=[CKVF], w=[CKVB])
            self.rope(nr, kp3, kp3, 1, 32, rp[:nr, 0, 0:32], rp[:nr, 1, 0:32], rtmp, [KPEF, RPB], [KPEF], RTMP)
            self.act(lambda: a.activation(out=kpeb[:nr, :], in_=kpef[:nr, :], func=AF.Copy), r=[KPEF], w=[KPEB])
            if smp:
                self.dma(O["s_ckv"][:, :], ckvf[:nr, :], r=[CKVF])
                self.dma(O["s_kpe"][:, :], kpef[:nr, :], r=[KPEF])
            else:
                self.dma(O["p_ckv"][t0:t0 + nr, :], ckvf[:nr, :], r=[CKVF])
                self.dma(O["p_kpe"][t0:t0 + nr, :], kpef[:nr, :], r=[KPEF])
            self.kside_mla(nr, ckvb, kpeb, [CKVB, KPEB], None, R, split=True)
            self.rope(nr, qdf3[:, :, 0:32], qdf3[:, :, 0:32], 8, 16, rp[:nr, 0, 32:48], rp[:nr, 1, 32:48], rtmp,
                      [TM[3], RPB], [TM[3]], RTMP)
            self.act(lambda: a.activation(out=qdb[:nr, :].rearrange("p (h d) -> p h d", d=128), in_=qdf3, func=AF.Copy),
                     r=[TM[3]], w=[QDB])
            q3 = qf[:nr, :].rearrange("p (h d) -> p h d", d=192)
            qr3 = qrf[:nr, :].rearrange("p (h d) -> p h d", d=64)
            self.run(
                self.rms_gen(nr, q3[:, :, 0:128], 8, 128, g_qn, qnb[:nr, :].rearrange("p (h d) -> p h d", d=128),
                             sq[:, 0:1024], ss[:, 0:8], QF, [QNB], TM[0]),
                self.rms_gen(nr, q3[:, :, 128:192], 8, 64, g_qr, qr3, sq[:, 1024:1536], ss[:, 8:16], QF, [QRF], TM[1]),
                self.kside_kn_gen(nr, R),
            )
            self.transpose_blocks(nr, [qdb[:nr, h * 128:(h + 1) * 128] for h in range(8)], [QDB], tstd, [TSTD], k0=1)
            self.dma(X["qdT"][:, t0:t0 + nr].rearrange("(h p) t -> p h t", p=128), tstd[:, :, :nr], r=[TSTD],
                     w=[self.XB["qdT"]])
            self.rope(nr, qr3, qrb[:nr, :].rearrange("p (h d) -> p h d", d=64), 8, 32, rp[:nr, 0, 0:32], rp[:nr, 1, 0:32],
                      rtmp, [QRF, RPB], [QRB], RTMP)
            self.kside_b(nr, kpeb, [CKVB, KPEB], self.key_dsts(t0, nr, smp), R)
            blocks = [qnb[:nr, h * 128:(h + 1) * 128] for h in range(8)] + [qrb[:nr, j * 128:(j + 1) * 128] for j in range(4)]
            self.transpose_blocks(nr, blocks, [QNB, QRB], tstq, [TSTQ])
            self.dma(X["qnT"][:, t0:t0 + nr].rearrange("(h p) t -> p h t", p=128), tstq[:, 0:8, :nr], r=[TSTQ],
                     w=[self.XB["qnT"]])
            self.dma(X["qrT"][:, t0:t0 + nr].rearrange("(h p) t -> p h t", p=128), tstq[:, 8:12, :nr], r=[TSTQ],
                     w=[self.XB["qrT"]])

        front(0)
        for idx in range(len(tiles_)):
            if idx + 1 < len(tiles_):
                front(idx + 1)
            back(idx)
        self.cast_weights([k for k in self.W if k not in self.EARLY_W])
        for s in range(c.NS):
            for j in range(c.PAST // 128):
                self.dma(cin[:, 0:256], I["c_ckv"][s, j * 128:(j + 1) * 128, :], w=[CIN])
                self.dma(cin[:, 256:320], I["c_kpe"][s, j * 128:(j + 1) * 128, :], w=[CIN])
                self.dve(lambda: v.tensor_copy(out=ckvb[:, :], in_=cin[:, 0:256]), r=[CIN], w=[CKVB])
                self.dve(lambda: v.tensor_copy(out=kpeb[:, :], in_=cin[:, 256:320]), r=[CIN], w=[KPEB])
                self.kside_mla(128, ckvb, kpeb, [CKVB, KPEB], [(c.T + s * c.SS + j * 128, 0, 128)], R)
        S.release(m)

    def phaseA2(self):
        S, I, O, X, W, c = self.S, self.I, self.O, self.X, self.W, self.cfg
        nc = self.nc
        v, a, gp = nc.vector, nc.scalar, nc.gpsimd
        m = S.mark()
        wA = S.sb("wA2", [128, 16, NA2], BF16)
        WA = self.B()
        for k in range(16):
            self.dma(wA[:, k, :], W["w_in"][k * 128:(k + 1) * 128, NA1:NA], r=[self.WB["w_in"]], w=[WA])
        g_dk = self.rep_gain("dsa_k_norm", DHD)
        hT = [S.sb("hT2", [128, 16, 128], BF16) for _ in range(2)]
        HT = [self.B() for _ in range(2)]
        ropes = [S.sb("ropes2", [128, 2, 56], F32) for _ in range(2)]
        RP = [self.B() for _ in range(2)]
        pjs = [S.sb("pj2", [128, NA2], F32) for _ in range(2)]
        PJS = [[self.B() for _ in range(4)] for _ in range(2)]
        sq = S.sb("sq2", [128, 1024], F32)
        ss = S.sb("ss2", [128, 16], F32)
        TMP = self.B()
        rtmp = S.sb("rtmp2", [128, 1024], F32)
        RTMP = self.B()
        kdf = S.sb("kdf", [128, 256], F32)
        KDF = self.B()
        kdb = S.sb("kdb", [128, 256], BF16)
        KDB = self.B()
        vdb = S.sb("vdb", [128, 256], BF16)
        VDB = self.B()
        qib = S.sb("qib", [128, 1024], BF16)
        QIB = self.B()
        kib = S.sb("kib", [128, 64], BF16)
        KIB = self.B()
        tst = S.sb("tst2", [128, 11, 128], BF16)
        TST = self.B()
        wtmp = S.sb("wtmp", [128, 2 * IH], F32)
        WT = self.B()
        wout = S.sb("wout", [128, 2, IH], F32)
        WO = self.B()
        cin = S.sb("cin2", [128, 576], F32)
        CIN = self.B()

        def kside(nr, inB, dsts):
            self.transpose_blocks(nr, [kdb[:nr, 0:128], kdb[:nr, 128:256], kib[:nr, 0:64]], inB, tst[:, 8:11, :], [TST])
            for (koff, c0, n) in dsts:
                self.dma(X["kdT"][:, koff:koff + n].rearrange("(g p) t -> p g t", p=128), tst[:, 8:10, c0:c0 + n], r=[TST],
                         w=[self.XB["kdT"]])
                self.dma(X["kiT"][:, koff:koff + n], tst[0:64, 10, c0:c0 + n], r=[TST], w=[self.XB["kiT"]])
                self.dma(X["vd"][koff:koff + n, :], vdb[c0:c0 + n, :], r=[VDB], w=[self.XB["vd"]])

        tiles_ = self.token_tiles()

        def front(idx):
            t0, nr, smp = tiles_[idx]
            pb = idx % 2
            rp, RPB = ropes[pb], RP[pb]
            pj, PJ = pjs[pb], PJS[pb]
            self.dma(hT[pb][:, :, :nr], X["hT"][:, t0:t0 + nr].rearrange("(k p) t -> p k t", p=128), r=[self.XB["hT"]],
                     w=[HT[pb]])
            if smp:
                for s in range(c.NS):
                    self.dma(rp[s * c.DEC:(s + 1) * c.DEC, :, :], I["ropet"][c.PAST:c.PAST + c.DEC, :, :], w=[RPB])
            else:
                self.dma(rp[:nr, :, :], I["ropet"][t0:t0 + nr, :, :], w=[RPB])
            for gi, (c0, cw) in enumerate(((0, 512), (512, 512), (1024, 512), (1536, 80))):
                bk, BK = self.bank()
                for k in range(16):
                    self.mm(bk[:nr, :cw], hT[pb][:, k, :nr], wA[:, k, c0:c0 + cw], k == 0, k == 15, r=[HT[pb], WA], w=[BK])
                self.evac(gi, pj[:nr, c0:c0 + cw], bk[:nr, :cw], r=[], w=[BK, PJ[gi]])
        def back(idx):
            t0, nr, smp = tiles_[idx]
            pb = idx % 2
            rp, RPB = ropes[pb], RP[pb]
            pj, PJ = pjs[pb], PJS[pb]
            kv3 = pj[:nr, 0:512].rearrange("p (h d) -> p h d", d=128)
            kd3 = kdf[:nr, :].rearrange("p (h d) -> p h d", d=128)
            self.rms_heads(nr, kv3[:, 0:2, :], 2, 128, g_dk, kd3, sq, ss, [PJ[0]], [KDF], TMP)
            self.act(lambda nr=nr: a.activation(out=vdb[:nr, :], in_=pj[:nr, 256:512], func=AF.Copy), r=[PJ[0]], w=[VDB])
            qi3 = pj[:nr, 512:1536].rearrange("p (h d) -> p h d", d=64)
            self.rope(nr, qi3[:, :, 0:16], qi3[:, :, 0:16], 16, 8, rp[:nr, 0, 48:56], rp[:nr, 1, 48:56], rtmp,
                      [PJ[1], PJ[2], RPB], [PJ[1], PJ[2]], RTMP)
            self.act(lambda nr=nr: a.activation(out=qib[:nr, :], in_=pj[:nr, 512:1536], func=AF.Copy), r=[PJ[1], PJ[2]], w=[QIB])
            ki3 = pj[:nr, 1536:1600].rearrange("p (h d) -> p h d", d=64)
            self.rope(nr, ki3[:, :, 0:16], ki3[:, :, 0:16], 1, 8, rp[:nr, 0, 48:56], rp[:nr, 1, 48:56], rtmp, [PJ[3], RPB],
                      [PJ[3]], RTMP)
            self.act(lambda nr=nr: a.activation(out=kib[:nr, :], in_=pj[:nr, 1536:1600], func=AF.Copy), r=[PJ[3]], w=[KIB])
            self.rope(nr, kd3[:, :, 0:32], kd3[:, :, 0:32], 2, 16, rp[:nr, 0, 32:48], rp[:nr, 1, 32:48], rtmp, [KDF, RPB],
                      [KDF], RTMP)
            self.act(lambda nr=nr: a.activation(out=kdb[:nr, :], in_=kdf[:nr, :], func=AF.Copy), r=[KDF], w=[KDB])
            if smp:
                self.dma(O["s_dk"][:, :], kdf[:nr, :], r=[KDF])
                self.dma(O["s_dv"][:, :], pj[:nr, 256:512], r=[PJ[0]])
                self.dma(O["s_ik"][:, :], pj[:nr, 1536:1600], r=[PJ[3]])
            else:
                self.dma(O["p_dk"][t0:t0 + nr, :], kdf[:nr, :], r=[KDF])
                self.dma(O["p_dv"][t0:t0 + nr, :], pj[:nr, 256:512], r=[PJ[0]])
                self.dma(O["p_ik"][t0:t0 + nr, :], pj[:nr, 1536:1600], r=[PJ[3]])
            self.transpose_blocks(nr, [qib[:nr, j * 128:(j + 1) * 128] for j in range(8)], [QIB], tst[:, 0:8, :], [TST])
            self.dma(X["qiT"][:, t0:t0 + nr].rearrange("(j p) t -> p j t", p=128), tst[:, 0:8, :nr], r=[TST],
                     w=[self.XB["qiT"]])
            kside(nr, [KDB, KIB], self.key_dsts(t0, nr, smp))
            wi = pj[:nr, 1600:1616]
            self.dve(lambda nr=nr, wi=wi: v.tensor_scalar(out=wtmp[:nr, 0:IH], in0=wi, scalar1=-1.0, scalar2=None, op0=ALU.mult),
                     r=[PJ[3]], w=[WT])
            self.dve(lambda nr=nr, wi=wi: v.tensor_tensor(out=wtmp[:nr, 0:IH], in0=wtmp[:nr, 0:IH], in1=wi, op=ALU.max),
                     r=[PJ[3]], w=[WT])
            self.dve(lambda nr=nr: v.tensor_scalar(out=wout[:nr, 0, :], in0=wtmp[:nr, 0:IH], scalar1=0.03125,
                                                   scalar2=None, op0=ALU.mult), r=[WT], w=[WO])
            self.dve(lambda nr=nr, wi=wi: v.tensor_scalar(out=wtmp[:nr, IH:2 * IH], in0=wi, scalar1=0.0, scalar2=2.0,
                                                          op0=ALU.is_gt, op1=ALU.mult), r=[PJ[3]], w=[WT])
            self.dve(lambda nr=nr: v.tensor_scalar(out=wout[:nr, 1, :], in0=wtmp[:nr, IH:2 * IH], scalar1=-1.0,
                                                   scalar2=None, op0=ALU.add), r=[WT], w=[WO])
            self.dma(X["wi"][t0:t0 + nr, :, :], wout[:nr, :, :], r=[WO], w=[self.XB["wi"]])

        front(0)
        for idx in range(len(tiles_)):
            if idx + 1 < len(tiles_):
                front(idx + 1)
            back(idx)
        for s in range(c.NS):
            for j in range(c.PAST // 128):
                rows = slice(j * 128, (j + 1) * 128)
                self.dma(cin[:, 0:256], I["c_dk"][s, rows, :], w=[CIN])
                self.dma(cin[:, 256:512], I["c_dv"][s, rows, :], w=[CIN])
                self.dma(cin[:, 512:576], I["c_ik"][s, rows, :], w=[CIN])
                self.dve(lambda: v.tensor_copy(out=kdb[:, :], in_=cin[:, 0:256]), r=[CIN], w=[KDB])
                self.act(lambda: a.activation(out=vdb[:, :], in_=cin[:, 256:512], func=AF.Copy), r=[CIN], w=[VDB])
                self.dve(lambda: v.tensor_copy(out=kib[:, :], in_=cin[:, 512:576]), r=[CIN], w=[KIB])
                kside(128, [KDB, KIB], [(c.T + s * c.SS + j * 128, 0, 128)])
        S.release(m)

    def bank_of(self, pool, key):
        cnt = getattr(self, "_bkc", {})
        self._bkc = cnt
        i = cnt.get(key, 0)
        cnt[key] = i + 1
        b = pool[i % len(pool)]
        return self.psum[b], self.PB[b]

    def seqs(self):
        c = self.cfg
        out = [dict(koff=0, S=c.T, qoff=0, nq=c.T, causal=True)]
        for s in range(c.NS):
            out.append(dict(koff=c.T + s * c.SS, S=c.SS, qoff=c.T + s * c.DEC, nq=c.DEC, causal=False))
        return out

    def phaseB(self):
        S, X, c = self.S, self.X, self.cfg
        nc = self.nc
        v, a, gp = nc.vector, nc.scalar, nc.gpsimd
        m = S.mark()
        Smax = max(c.T, c.SS)
        nktmax = (Smax + 127) // 128
        nqmax = max(c.T, c.DEC)
        kn = [S.sb("kn", [128, Smax], BF16) for _ in range(2)]
        vt = [S.sb("vt", [128, nktmax, 128], BF16) for _ in range(2)]
        qn = [S.sb("qn", [128, nqmax], BF16) for _ in range(2)]
        qr = [S.sb("qr", [64, nqmax], BF16) for _ in range(2)]
        HB = [self.B() for _ in range(2)]
        kpe = S.sb("kpe", [64, Smax], BF16)
        KPE = self.B()
        pT = [S.sb("pT", [128, 512], BF16) for _ in range(5)]
        PT = [self.B() for _ in range(5)]
        rden = S.sb("rden", [128, 512], F32)
        RD = self.B()
        oT = [S.sb("oT", [128, 512], BF16) for _ in range(2)]
        OT = [self.B() for _ in range(2)]
        scale = float((NOPE + ROPE) ** -0.5)
        rdeps = [self.XB[k] for k in ("knT", "kpeT", "v", "qnT", "qrT")]

        def load(sq, h, pb):
            koff, Sk, qoff, nq = sq["koff"], sq["S"], sq["qoff"], sq["nq"]
            self.dma(kn[pb][:, :Sk], X["knT"][h * 128:(h + 1) * 128, koff:koff + Sk], r=rdeps, w=[HB[pb]])
            nfull = Sk // 128
            if nfull:
                self.dma(vt[pb][:, 0:nfull, :],
                         X["v"][koff:koff + nfull * 128, h * 128:(h + 1) * 128].rearrange("(k p) d -> p k d", p=128),
                         r=rdeps, w=[HB[pb]])
            rem = Sk - nfull * 128
            if rem:
                self.dma(vt[pb][:rem, nfull, :], X["v"][koff + nfull * 128:koff + Sk, h * 128:(h + 1) * 128], r=rdeps, w=[HB[pb]])
            self.dma(qn[pb][:, :nq], X["qnT"][h * 128:(h + 1) * 128, qoff:qoff + nq], r=rdeps, w=[HB[pb]])
            self.dma(qr[pb][:, :nq], X["qrT"][h * 64:(h + 1) * 64, qoff:qoff + nq], r=rdeps, w=[HB[pb]])

        work = [(sq, h) for sq in self.seqs() for h in range(MH)]
        load(work[0][0], work[0][1], 0)
        pc = [0]
        ocount = 0
        for wi_, (sq, h) in enumerate(work):
            pb = wi_ % 2
            if wi_ + 1 < len(work):
                load(work[wi_ + 1][0], work[wi_ + 1][1], (wi_ + 1) % 2)
            koff, Sk, qoff, nq, causal = sq["koff"], sq["S"], sq["qoff"], sq["nq"], sq["causal"]
            if h == 0:
                self.dma(kpe[:, :Sk], X["kpeT"][:, koff:koff + Sk], r=rdeps, w=[KPE])
            for qb0 in range(0, nq, 512):
                nqb = min(512, nq - qb0)
                nkt = (qb0 + nqb) // 128 if causal else (Sk + 127) // 128
                ao, AO = self.bank_of([0, 1], "bo")
                ad, AD = self.bank_of([2, 3], "bd")
                pend = {}

                def front(kt):
                    kr = min(128, Sk - kt * 128)
                    qa = max(0, kt * 128 - qb0) if causal else 0
                    n = nqb - qa
                    sb_, SB_ = self.bank_of([4, 5, 6, 7], "bs")
                    self.mm(sb_[:kr, :n], kn[pb][:, kt * 128:kt * 128 + kr], qn[pb][:, qb0 + qa:qb0 + nqb], True, False,
                            r=[HB[pb]], w=[SB_])
                    self.mm(sb_[:kr, :n], kpe[:, kt * 128:kt * 128 + kr], qr[pb][:, qb0 + qa:qb0 + nqb], False, True,
                            r=[HB[pb], KPE], w=[SB_])
                    p, P = pT[pc[0] % 5], PT[pc[0] % 5]
                    pc[0] += 1
                    self.act(lambda: a.activation(out=p[:kr, :n], in_=sb_[:kr, :n], func=AF.Exp, scale=scale), r=[], w=[SB_, P])
                    if causal and kt * 128 >= qb0:
                        self.pool(lambda: gp.memset(p[64:128, 0:64], 0.0), r=[], w=[P])
                    pend[kt] = (p, P, kr, qa, n)

                def back(kt):
                    p, P, kr, qa, n = pend.pop(kt)
                    first, last = kt == 0, kt == nkt - 1
                    self.mm(ao[:, qa:nqb], vt[pb][:kr, kt, :], p[:kr, :n], first, last, r=[HB[pb], P], w=[AO])
                    self.mm(ad[:, qa:nqb], self.onesb[:kr, :], p[:kr, :n], first, last, r=[self.CB, P], w=[AD])

                LA = 2
                for k_ in range(nkt + LA):
                    if k_ < nkt:
                        front(k_)
                    if k_ >= LA:
                        back(k_ - LA)
                o, OB = oT[ocount % 2], OT[ocount % 2]
                ocount += 1
                self.dve(lambda ad=ad, nqb=nqb: v.reciprocal(out=rden[:, :nqb], in_=ad[:, :nqb]), r=[], w=[AD, RD])
                self.dve(lambda o=o, ao=ao, nqb=nqb: v.tensor_tensor(out=o[:, :nqb], in0=ao[:, :nqb], in1=rden[:, :nqb],
                                                                      op=ALU.mult), r=[RD], w=[AO, OB])
                self.dma(X["omT"][h * 128:(h + 1) * 128, qoff + qb0:qoff + qb0 + nqb], o[:, :nqb], r=[OB], w=[self.XB["omT"]])
        S.release(m)

    def phaseC(self):
        S, X, c = self.S, self.X, self.cfg
        nc = self.nc
        v, a, gp = nc.vector, nc.scalar, nc.gpsimd
        m = S.mark()
        Smax = max(c.T, c.SS)
        nktmax = (Smax + 127) // 128
        ki2 = S.sb("ki2", [128, Smax], BF16)
        kd = S.sb("kd", [128, 2, Smax], BF16)
        vd = S.sb("vd", [128, nktmax, 256], BF16)
        SEQB = self.B()
        qi = [S.sb("qi", [128, 8, 128], BF16) for _ in range(2)]
        qd = [S.sb("qd", [128, 8, 128], BF16) for _ in range(2)]
        wv = [S.sb("wv", [128, 2, IH], F32) for _ in range(2)]
        QB = [self.B() for _ in range(2)]
        scores = [S.sb("score", [128, Smax], F32) for _ in range(2)]
        SCB = [self.B() for _ in range(2)]
        m8 = S.sb("m8", [128, 8], F32)
        M8 = self.B()
        thr = S.sb("thr", [128, 1], F32)
        bs_ = S.sb("bsct", [128, 4], F32)
        rtab = S.sb("rtab", [128, self.NI], F32)
        cntb = S.sb("cntb", [128, self.NI], F32)
        mask = S.sb("mask", [128, Smax], BF16)
        MK = self.B()
        maskTs = [S.sb("maskT", [128, nktmax, 128], BF16) for _ in range(2)]
        MTB = [self.B() for _ in range(2)]
        tmp = [S.sb("itmp", [128, 512], F32) for _ in range(3)]
        TM = [self.B() for _ in range(3)]
        NEB = 4
        eT = [S.sb("eT", [128, 512], BF16) for _ in range(NEB)]
        ET = [self.B() for _ in range(NEB)]
        pT = [S.sb("pTd", [128, 512], BF16) for _ in range(NEB)]
        PT = [self.B() for _ in range(NEB)]
        rden = S.sb("rdend", [128, 512], F32)
        RD = self.B()
        oT = [S.sb("oTd", [128, 512], BF16) for _ in range(2)]
        OT = [self.B() for _ in range(2)]
        scale = float(DHD ** -0.5)
        kdeps = [self.XB[k] for k in ("kiT", "kdT", "vd")]
        qdeps = [self.XB[k] for k in ("qiT", "qdT", "wi")]
        tiles = []
        for si, sq in enumerate(self.seqs()):
            if sq["causal"]:
                for qt in range(sq["nq"] // 128):
                    tiles.append((si, sq, sq["qoff"] + qt * 128, 128, (qt + 1) * 128, True))
            else:
                tiles.append((si, sq, sq["qoff"], sq["nq"], sq["S"], False))

        def loadq(tl, pb):
            tok0, nq = tl[2], tl[3]
            self.dma(qi[pb][:, :, :nq], X["qiT"][:, tok0:tok0 + nq].rearrange("(j p) t -> p j t", p=128), r=qdeps, w=[QB[pb]])
            self.dma(qd[pb][:, :, :nq], X["qdT"][:, tok0:tok0 + nq].rearrange("(j p) t -> p j t", p=128), r=qdeps, w=[QB[pb]])
            self.dma(wv[pb][:nq, :, :], X["wi"][tok0:tok0 + nq, :, :], r=qdeps, w=[QB[pb]])

        st = dict(tcount=0, ecount=0, ocount=0, cur_seq=-1)
        acc = {}

        def seq_load(tl):
            si, sq = tl[0], tl[1]
            if si == st["cur_seq"]:
                return
            st["cur_seq"] = si
            koff, Sk = sq["koff"], sq["S"]
            for half in range(2):
                self.dma(ki2[half * 64:(half + 1) * 64, :Sk], X["kiT"][:, koff:koff + Sk], r=kdeps, w=[SEQB])
            self.dma(kd[:, :, :Sk], X["kdT"][:, koff:koff + Sk].rearrange("(g p) t -> p g t", p=128), r=kdeps, w=[SEQB])
            nfull = Sk // 128
            if nfull:
                self.dma(vd[:, 0:nfull, :], X["vd"][koff:koff + nfull * 128, :].rearrange("(k p) d -> p k d", p=128),
                         r=kdeps, w=[SEQB])
            if Sk - nfull * 128:
                self.dma(vd[:Sk - nfull * 128, nfull, :], X["vd"][koff + nfull * 128:koff + Sk, :], r=kdeps, w=[SEQB])

        def idx(ti):
            si, sq, tok0, nq, W, causal = tiles[ti]
            pb = ti % 2
            score, SC = scores[pb], SCB[pb]
            for kb in range(0, W, 512):
                wb = min(512, W - kb)
                for j in range(8):
                    banks = [self.bank_of([4, 5, 6, 7], "bs") for _ in range(2)]
                    for hh in range(2):
                        bk, BK = banks[hh]
                        self.mm(bk[:nq, :wb], qi[pb][hh * 64:(hh + 1) * 64, j, :nq], ki2[hh * 64:(hh + 1) * 64, kb:kb + wb],
                                True, True, r=[QB[pb], SEQB], w=[BK])
                    for hh in range(2):
                        bk, BK = banks[hh]
                        h = 2 * j + hh
                        t_, T_ = tmp[st["tcount"] % 3], TM[st["tcount"] % 3]
                        st["tcount"] += 1
                        self.act(lambda t_=t_, bk=bk, nq=nq, wb=wb, pb=pb, h=h: a.activation(
                            out=t_[:nq, :wb], in_=bk[:nq, :wb], func=AF.Relu, scale=wv[pb][:nq, 0, h:h + 1]),
                            r=[QB[pb]], w=[BK, T_])
                        if h == 0:
                            self.dve(lambda t_=t_, nq=nq, wb=wb, kb=kb, pb=pb, h=h, score=score: v.tensor_scalar(
                                out=score[:nq, kb:kb + wb], in0=t_[:nq, :wb], scalar1=wv[pb][:nq, 1, h:h + 1], scalar2=None,
                                op0=ALU.mult), r=[T_, QB[pb]], w=[SC])
                        else:
                            self.dve(lambda t_=t_, nq=nq, wb=wb, kb=kb, pb=pb, h=h, score=score: v.scalar_tensor_tensor(
                                out=score[:nq, kb:kb + wb], in0=t_[:nq, :wb], scalar=wv[pb][:nq, 1, h:h + 1],
                                in1=score[:nq, kb:kb + wb], op0=ALU.mult, op1=ALU.add), r=[T_, QB[pb]], w=[SC])

        def bisect(ti):
            si, sq, tok0, nq, W, causal = tiles[ti]
            pb = ti % 2
            score, SC = scores[pb], SCB[pb]
            topk = min(TOPK_MAX, sq["S"] // 4)
            if W > topk:
                NI = self.NI
                self.dve(lambda: v.tensor_reduce(out=bs_[:nq, 0:1], in_=score[:nq, :W], axis=AX.X, op=ALU.max), r=[SC], w=[M8])
                self.dve(lambda: v.tensor_reduce(out=thr[:nq, 0:1], in_=score[:nq, :W], axis=AX.X, op=ALU.min), r=[SC], w=[M8])
                if causal:
                    self.dve(lambda: v.memset(score[0:64, W - 64:W], NEG), r=[M8], w=[SC])
                self.dve(lambda: v.tensor_tensor(out=bs_[:nq, 0:1], in0=bs_[:nq, 0:1], in1=thr[:nq, 0:1], op=ALU.subtract),
                         r=[], w=[M8])
                self.dve(lambda: v.tensor_scalar(out=bs_[:nq, 0:1], in0=bs_[:nq, 0:1], scalar1=1.0001, scalar2=1e-6,
                                                 op0=ALU.mult, op1=ALU.add), r=[], w=[M8])
                self.dve(lambda: v.tensor_scalar(out=rtab[:nq, :], in0=self.ctab[:nq, :], scalar1=bs_[:nq, 0:1], scalar2=None,
                                                 op0=ALU.mult), r=[self.CB], w=[M8])
                self.dve(lambda: v.memset(cntb[:nq, :], 0.0), r=[], w=[M8])
                for it in range(NI):
                    self.dve(lambda it=it: v.tensor_tensor(out=bs_[:nq, 1:2], in0=thr[:nq, 0:1], in1=rtab[:nq, it:it + 1],
                                                           op=ALU.add), r=[], w=[M8])
                    self.dve(lambda it=it: v.tensor_scalar(out=mask[:nq, :W], in0=score[:nq, :W], scalar1=bs_[:nq, 1:2],
                                                           scalar2=0.0, op0=ALU.is_ge, op1=ALU.add,
                                                           accum_out=cntb[:nq, it:it + 1]), r=[SC], w=[M8, MK])
                    self.dve(lambda it=it: v.tensor_scalar(out=bs_[:nq, 2:3], in0=cntb[:nq, it:it + 1],
                                                           scalar1=float(topk) - 0.5, scalar2=rtab[:nq, it:it + 1],
                                                           op0=ALU.is_ge, op1=ALU.mult), r=[], w=[M8])
                    self.dve(lambda: v.tensor_tensor(out=thr[:nq, 0:1], in0=thr[:nq, 0:1], in1=bs_[:nq, 2:3], op=ALU.add),
                             r=[], w=[M8])
            else:
                if causal:
                    self.dve(lambda: v.memset(score[0:64, W - 64:W], NEG), r=[], w=[SC])
                self.dve(lambda: v.memset(thr[:nq, :], -1e29), r=[], w=[M8])

        def mask_tr(ti):
            si, sq, tok0, nq, W, causal = tiles[ti]
            pb = ti % 2
            score, SC = scores[pb], SCB[pb]
            nkt = (W + 127) // 128
            self.dve(lambda: v.tensor_scalar(out=mask[:nq, :W], in0=score[:nq, :W], scalar1=thr[:nq, 0:1],
                                             scalar2=None, op0=ALU.is_ge), r=[SC, M8], w=[MK])
            srcs = [mask[:nq, kt * 128:min(W, (kt + 1) * 128)] for kt in range(nkt)]
            for g0 in range(0, nkt, 8):
                g = srcs[g0:g0 + 8]
                bk, BK = self.bank_of([4, 5, 6, 7], "bs")
                bb = bk[:].bitcast(BF16)
                for j, s_ap in enumerate(g):
                    wd = s_ap.shape[1]
                    self.tr(bb[:wd, j * 128:j * 128 + nq], s_ap, self.identb[:nq, :nq], r=[MK, self.CB], w=[BK])
                src = bb[:, 0:len(g) * 128].rearrange("p (j t) -> p j t", t=128)[:, :, :nq]
                self.evac(g0 // 8, maskTs[pb][:, g0:g0 + len(g), :nq], src, r=[], w=[BK, MTB[pb]])

        def att(ti):
            si, sq, tok0, nq, W, causal = tiles[ti]
            pb = ti % 2
            nkt = (W + 127) // 128
            maskT, MT = maskTs[pb], MTB[pb]
            for g in range(DKV):
                ao, AO = self.bank_of([0, 1], "bo")
                ad, AD = self.bank_of([2, 3], "bd")
                acc[(ti, g)] = (ao, AO, ad, AD)
            steps = [(g, kt) for g in range(DKV) for kt in range(nkt)]
            pend = {}

            def front(g, kt):
                kr = min(128, W - kt * 128)
                sb_, SB_ = self.bank_of([4, 5, 6, 7], "bs")
                s3 = sb_[:kr, 0:4 * nq].rearrange("p (r q) -> p r q", q=nq)
                self.mm(s3, kd[:, g, kt * 128:kt * 128 + kr], qd[pb][:, 4 * g:4 * g + 4, :nq], True, True,
                        r=[SEQB, QB[pb]], w=[SB_])
                e_, E_ = eT[st["ecount"] % NEB], ET[st["ecount"] % NEB]
                p_, P_ = pT[st["ecount"] % NEB], PT[st["ecount"] % NEB]
                st["ecount"] += 1
                self.act(lambda: a.activation(out=e_[:kr, :4 * nq], in_=sb_[:kr, :4 * nq], func=AF.Exp, scale=scale),
                         r=[], w=[SB_, E_])
                e3 = e_[:kr, 0:4 * nq].rearrange("p (r q) -> p r q", q=nq)
                p3 = p_[:kr, 0:4 * nq].rearrange("p (r q) -> p r q", q=nq)
                mb = maskT[:kr, kt, :nq].unsqueeze(1).to_broadcast([kr, 4, nq])
                self.pool(lambda: gp.tensor_tensor(out=p3, in0=e3, in1=mb, op=ALU.mult), r=[E_, MT], w=[P_])
                pend[(g, kt)] = (p_, P_, kr)

            def back(g, kt):
                p_, P_, kr = pend.pop((g, kt))
                ao, AO, ad, AD = acc[(ti, g)]
                first, last = kt == 0, kt == nkt - 1
                self.mm(ao[:, :4 * nq], vd[:kr, kt, g * 128:(g + 1) * 128], p_[:kr, :4 * nq], first, last, r=[SEQB, P_], w=[AO])
                self.mm(ad[:, :4 * nq], self.onesb[:kr, :], p_[:kr, :4 * nq], first, last, r=[self.CB, P_], w=[AD])

            LA = 2
            for k_ in range(len(steps) + LA):
                if k_ < len(steps):
                    front(*steps[k_])
                if k_ >= LA:
                    back(*steps[k_ - LA])

        def norm(ti):
            si, sq, tok0, nq, W, causal = tiles[ti]
            for g in range(DKV):
                ao, AO, ad, AD = acc.pop((ti, g))
                o, OB = oT[st["ocount"] % 2], OT[st["ocount"] % 2]
                st["ocount"] += 1
                self.dve(lambda ad=ad: v.reciprocal(out=rden[:, :4 * nq], in_=ad[:, :4 * nq]), r=[], w=[AD, RD])
                self.dve(lambda o=o, ao=ao: v.tensor_tensor(out=o[:, :4 * nq], in0=ao[:, :4 * nq], in1=rden[:, :4 * nq],
                                                             op=ALU.mult), r=[RD], w=[AO, OB])
                self.dma(X["odT"][g * 512:(g + 1) * 512, tok0:tok0 + nq].rearrange("(r p) t -> p r t", p=128),
                         o[:, :4 * nq].rearrange("p (r q) -> p r q", q=nq), r=[OB], w=[self.XB["odT"]])

        n = len(tiles)
        loadq(tiles[0], 0)
        seq_load(tiles[0])
        if n > 1:
            loadq(tiles[1], 1)
        idx(0)
        bisect(0)
        mask_tr(0)
        for ti in range(n):
            nxt = ti + 1 if ti + 1 < n else None
            same = nxt is not None and tiles[nxt][0] == tiles[ti][0]
            if same:
                idx(nxt)
            att(ti)
            if same:
                bisect(nxt)
            norm(ti)
            if nxt is not None and not same:
                seq_load(tiles[nxt])
                idx(nxt)
                bisect(nxt)
            if nxt is not None:
                mask_tr(nxt)
                if ti + 2 < n:
                    loadq(tiles[ti + 2], ti % 2)
        S.release(m)

    def phaseD(self):
        S, I, O, X, W, c = self.S, self.I, self.O, self.X, self.W, self.cfg
        nc = self.nc
        v, a, gp = nc.vector, nc.scalar, nc.gpsimd
        m = S.mark()
        NG = 512
        NJ = D_FF // 128
        ring = [S.sb("wring", [128, 8192], BF16) for _ in range(3)]
        RG = [self.B() for _ in range(3)]
        io = [S.sb("ioD", [128, D_MODEL], F32) for _ in range(2)]
        IO = [self.B() for _ in range(2)]
        gff = S.sb("gff", [128, 16], F32)
        cw = S.sb("cw", [128, 3, NJ], F32)
        cb = S.sb("cb", [128, NJ], F32)
        self.dma(gff[:], I["ffn_normT"][:, :], w=[self.CB])
        self.dma(cw[:], I["conv_wT"][:, :, :], w=[self.CB])
        self.dma(cb[:], I["conv_bT"][:, :], w=[self.CB])
        rc = [0]
        ioc = [0]

        def slot():
            i = rc[0] % 3
            rc[0] += 1
            return ring[i], RG[i]

        class Ctx:
            pass

        def mk(ngmax, nseqmax, tag):
            x = Ctx()
            x.xT = S.sb("xT" + tag, [128, 16, ngmax], F32)
            x.XT = [self.B() for _ in range(16)]
            x.r2 = S.sb("r2" + tag, [128, 16, ngmax], BF16)
            x.R2 = [self.B() for _ in range(16)]
            x.r1 = S.sb("r1" + tag, [128, NJ, ngmax], BF16)
            x.R1 = [self.B() for _ in range(NJ)]
            x.omT, x.odT, x.mT = x.r1[:, 0:8, :], x.r1[:, 8:16, :], x.r1[:, 16:32, :]
            x.sg = [S.sb("sg" + tag, [128, ngmax], F32) for _ in range(2)]
            x.SG = [self.B() for _ in range(2)]
            x.t12 = [S.sb("t12" + tag, [128, ngmax], F32) for _ in range(2)]
            x.T12 = [self.B() for _ in range(2)]
            x.rstd = S.sb("rstdD" + tag, [128, ngmax], F32)
            x.RS = self.B()
            x.sqb = [S.sb("sqbD" + tag, [128, ngmax], BF16) for _ in range(2)]
            x.SQ = [self.B() for _ in range(2)]
            x.gpx = [S.sb("gpx" + tag, [128, ngmax + 2 * nseqmax], F32) for _ in range(2)]
            x.GP = [self.B() for _ in range(2)]
            x.tt = [S.sb("ttD" + tag, [128, ngmax], F32) for _ in range(2)]
            x.TT_ = [self.B() for _ in range(2)]
            x.sl = [S.sb("slD" + tag, [128, ngmax], F32) for _ in range(2)]
            x.SL = [self.B() for _ in range(2)]
            x.carry = S.sb("carry" + tag, [128, NJ, 2 * nseqmax], F32)
            x.CY = [self.B() for _ in range(NJ)]
            return x

        P = mk(NG, 1, "p")
        Q = mk(c.TS, c.NS, "s")
        for j in range(NJ):
            self.pool(lambda j=j: gp.memset(P.carry[:, j, :], 0.0), w=[P.CY[j]])

        def setg(x, tok0, ng, nseq, L, smp):
            x.tok0, x.ng, x.nseq, x.L, x.smp = tok0, ng, nseq, L, smp
            x.last = smp or tok0 + ng == c.T

        def load_group(x):
            tok0, ng = x.tok0, x.ng
            if x.smp:
                for j in range(NJ):
                    self.dma(x.carry[:, j, 0:2 * x.nseq].rearrange("p (s r) -> p s r", r=2),
                             I["c_conv"][:, :, j * 128:(j + 1) * 128].rearrange("s r p -> p s r"), w=[x.CY[j]], slow=True)
            self.dma(x.r2[:, :, :ng], X["hT"][:, tok0:tok0 + ng].rearrange("(k p) t -> p k t", p=128), r=[self.XB["hT"]], w=x.R2)
            self.dma(x.omT[:, :, :ng], X["omT"][:, tok0:tok0 + ng].rearrange("(k p) t -> p k t", p=128), r=[self.XB["omT"]],
                     w=x.R1[0:8])
            self.dma(x.odT[:, :, :ng], X["odT"][:, tok0:tok0 + ng].rearrange("(k p) t -> p k t", p=128), r=[self.XB["odT"]],
                     w=x.R1[8:16])
            for q0 in range(0, ng, 128):
                nr = min(128, ng - q0)
                xi, XI = io[ioc[0] % 2], IO[ioc[0] % 2]
                ioc[0] += 1
                src = I["xs"][:, :] if x.smp else I["xp"][tok0 + q0:tok0 + q0 + nr, :]
                self.dma(xi[:nr, :], src, w=[XI])
                for k4 in range(4):
                    bk, BK = self.bank()
                    for kk in range(4):
                        k = 4 * k4 + kk
                        self.tr(bk[:, kk * 128:kk * 128 + nr], xi[:nr, k * 128:(k + 1) * 128], self.identf[:nr, :nr],
                                r=[XI, self.CB], w=[BK])
                    self.evac(k4, x.xT[:, 4 * k4:4 * k4 + 4, q0:q0 + nr],
                              bk[:, 0:512].rearrange("p (j t) -> p j t", t=128)[:, :, :nr], r=[], w=[BK] + x.XT[4 * k4:4 * k4 + 4])

        def merge_chunk(x, cc, SLB, wom, wod, wgm, wgd):
            ng = x.ng
            sg, SG, t12, T12 = x.sg, x.SG, x.t12, x.T12
            ba, BA = self.bank()
            for k in range(8):
                self.mm(ba[:, :ng], wom[:, k, :], x.omT[:, k, :ng], k == 0, k == 7, r=[SLB] + x.R1[0:8], w=[BA])
            bb, BB = self.bank()
            for k in range(8):
                self.mm(bb[:, :ng], wod[:, k, :], x.odT[:, k, :ng], k == 0, k == 7, r=[SLB] + x.R1[8:16], w=[BB])
            bgm, BGM = self.bank()
            for k in range(16):
                self.mm(bgm[:, :ng], wgm[:, k, :], x.r2[:, k, :ng], k == 0, k == 15, r=[SLB] + x.R2, w=[BGM])
            bgd, BGD = self.bank()
            for k in range(16):
                self.mm(bgd[:, :ng], wgd[:, k, :], x.r2[:, k, :ng], k == 0, k == 15, r=[SLB] + x.R2, w=[BGD])
            yield
            self.act(lambda: a.activation(out=sg[0][:, :ng], in_=bgm[:, :ng], func=AF.Sigmoid), w=[BGM, SG[0]])
            yield
            self.act(lambda: a.activation(out=sg[1][:, :ng], in_=bgd[:, :ng], func=AF.Sigmoid), w=[BGD, SG[1]])
            yield
            self.dve(lambda: v.tensor_tensor(out=t12[0][:, :ng], in0=sg[0][:, :ng], in1=ba[:, :ng], op=ALU.mult),
                     r=[SG[0]], w=[BA, T12[0]])
            yield
            self.dve(lambda: v.tensor_tensor(out=t12[1][:, :ng], in0=sg[1][:, :ng], in1=bb[:, :ng], op=ALU.mult),
                     r=[SG[1]], w=[BB, T12[1]])
            yield
            self.pool(lambda: gp.tensor_tensor(out=x.mT[:, cc, :ng], in0=t12[0][:, :ng], in1=t12[1][:, :ng], op=ALU.add),
                      r=T12, w=[x.R1[16 + cc]])
            yield

        def wout_chunk(x, cc, SLB, wo, ci):
            ng = x.ng
            bk, BK = self.bank()
            for k in range(16):
                self.mm(bk[:, :ng], wo[:, k, ci * 128:(ci + 1) * 128], x.mT[:, k, :ng], k == 0, k == 15,
                        r=[SLB] + x.R1[16:32], w=[BK])
            self.dve(lambda: v.tensor_tensor(out=x.xT[:, cc, :ng], in0=x.xT[:, cc, :ng], in1=bk[:, :ng], op=ALU.add),
                     r=[], w=[BK, x.XT[cc]])

        def rms_stage(x):
            ng = x.ng
            bs, BS = self.bank()
            for cc in range(16):
                q_, Q_ = x.sqb[cc % 2], x.SQ[cc % 2]
                self.act(lambda q_=q_, cc=cc: a.activation(out=q_[:, :ng], in_=x.xT[:, cc, :ng], func=AF.Square),
                         r=[x.XT[cc]], w=[Q_])
                self.mm(bs[:, :ng], self.onesb[:, :], q_[:, :ng], cc == 0, cc == 15, r=[Q_, self.CB], w=[BS])
            self.act(lambda: a.activation(out=x.rstd[:, :ng], in_=bs[:, :ng], func=AF.Sqrt, scale=1.0 / D_MODEL,
                                          bias=self.epsc[:, 0:1]), r=[self.CB], w=[BS, x.RS])
            self.dve(lambda: v.reciprocal(out=x.rstd[:, :ng], in_=x.rstd[:, :ng]), w=[x.RS])
            for cc in range(16):
                self.dve(lambda cc=cc: v.scalar_tensor_tensor(out=x.r2[:, cc, :ng], in0=x.xT[:, cc, :ng], scalar=gff[:, cc:cc + 1],
                                                              in1=x.rstd[:, :ng], op0=ALU.mult, op1=ALU.mult),
                         r=[x.XT[cc], x.RS, self.CB], w=[x.R2[cc]])

        def up_chunk(x, j, SLB, wg, wu):
            ng, nseq, L = x.ng, x.nseq, x.L
            bg, BG = self.bank()
            for k in range(16):
                self.mm(bg[:, :ng], wg[:, k, :], x.r2[:, k, :ng], k == 0, k == 15, r=[SLB] + x.R2, w=[BG])
            bu, BU = self.bank()
            for k in range(16):
                self.mm(bu[:, :ng], wu[:, k, :], x.r2[:, k, :ng], k == 0, k == 15, r=[SLB] + x.R2, w=[BU])
            pb = j % 2
            GPB, TTB, SLB_ = x.GP[pb], x.TT_[pb], x.SL[pb]
            g3 = x.gpx[pb][:, 0:nseq * (L + 2)].rearrange("p (s l) -> p s l", l=L + 2)
            bg3 = bg[:, 0:ng].rearrange("p (s l) -> p s l", l=L)
            t3 = x.tt[pb][:, 0:ng].rearrange("p (s l) -> p s l", l=L)
            cyv = x.carry[:, j, 0:2 * nseq].rearrange("p (s r) -> p s r", r=2)
            yield
            self.pool(lambda: gp.tensor_copy(out=g3[:, :, 0:2], in_=cyv), r=[x.CY[j]], w=[GPB])
            self.act(lambda: a.activation(out=g3[:, :, 2:L + 2], in_=bg3, func=AF.Copy), r=[], w=[BG, GPB])
            yield
            self.act(lambda: a.activation(out=t3, in_=bg3, func=AF.Identity, scale=cw[:, 2, j:j + 1], bias=cb[:, j:j + 1]),
                     r=[self.CB], w=[BG, TTB])
            yield
            self.dve(lambda: v.scalar_tensor_tensor(out=t3, in0=g3[:, :, 1:L + 1], scalar=cw[:, 1, j:j + 1], in1=t3,
                                                    op0=ALU.mult, op1=ALU.add), r=[GPB, self.CB], w=[TTB])
            yield
            self.dve(lambda: v.scalar_tensor_tensor(out=t3, in0=g3[:, :, 0:L], scalar=cw[:, 0, j:j + 1], in1=t3,
                                                    op0=ALU.mult, op1=ALU.add), r=[GPB, self.CB], w=[TTB])
            yield
            self.act(lambda: a.activation(out=x.sl[pb][:, :ng], in_=x.tt[pb][:, :ng], func=AF.Silu), r=[TTB], w=[SLB_])
            yield
            self.dve(lambda: v.tensor_tensor(out=x.r1[:, j, :ng], in0=x.sl[pb][:, :ng], in1=bu[:, :ng], op=ALU.mult),
                     r=[SLB_], w=[BU, x.R1[j]])
            yield
            self.pool(lambda: gp.tensor_copy(out=cyv, in_=g3[:, :, L:L + 2]), r=[GPB], w=[x.CY[j]])

        def conv_state_out(x):
            for j in range(NJ):
                if x.smp:
                    for s_ in range(x.nseq):
                        self.dma(O["s_conv"][s_, :, j * 128:(j + 1) * 128].rearrange("r p -> p r"), x.carry[:, j, 2 * s_:2 * s_ + 2],
                                 r=[x.CY[j]], slow=True, q="pool")
                else:
                    self.dma(O["p_conv"][:, j * 128:(j + 1) * 128].rearrange("r p -> p r"), x.carry[:, j, 0:2], r=[x.CY[j]],
                             slow=True, q="pool")

        def down_chunk(x, cc, SLB, wd):
            ng = x.ng
            bk, BK = self.bank()
            for k in range(NJ):
                self.mm(bk[:, :ng], wd[:, k, :], x.r1[:, k, :ng], k == 0, k == NJ - 1, r=[SLB] + x.R1, w=[BK])
            self.dve(lambda: v.tensor_tensor(out=x.xT[:, cc, :ng], in0=x.xT[:, cc, :ng], in1=bk[:, :ng], op=ALU.add),
                     r=[], w=[BK, x.XT[cc]])

        def out_stage(x):
            tok0, ng = x.tok0, x.ng
            for q0 in range(0, ng, 128):
                nr = min(128, ng - q0)
                yo, YO = io[ioc[0] % 2], IO[ioc[0] % 2]
                ioc[0] += 1
                for k4 in range(4):
                    bk, BK = self.bank()
                    for kk in range(4):
                        k = 4 * k4 + kk
                        self.tr(bk[:nr, kk * 128:(kk + 1) * 128], x.xT[:, k, q0:q0 + nr], self.identf[:, :], r=[x.XT[k], self.CB], w=[BK])
                    self.evac(k4, yo[:nr, k4 * 512:(k4 + 1) * 512], bk[:nr, :], r=[], w=[BK, YO])
                dst = O["y_s"][:, :] if x.smp else O["y_p"][tok0 + q0:tok0 + q0 + nr, :]
                self.dma(dst, yo[:nr, :], r=[YO])

        pg = [(g0, min(NG, c.T - g0)) for g0 in range(0, c.T, NG)]
        for gi, (tok0, ng) in enumerate(pg):
            setg(P, tok0, ng, 1, ng, False)
            unit = [P]
            if gi == len(pg) - 1:
                setg(Q, c.T, c.TS, c.NS, c.DEC, True)
                unit.append(Q)
            for x in unit:
                load_group(x)
            for cc in range(16):
                sl_, SLB = slot()
                wom = sl_[:, 0:1024].rearrange("p (k n) -> p k n", n=128)
                wod = sl_[:, 1024:2048].rearrange("p (k n) -> p k n", n=128)
                wgm = sl_[:, 2048:4096].rearrange("p (k n) -> p k n", n=128)
                wgd = sl_[:, 4096:6144].rearrange("p (k n) -> p k n", n=128)
                cs = slice(cc * 128, (cc + 1) * 128)
                self.dma(wom, W["w_o_mla"][:, cs].rearrange("(k p) n -> p k n", p=128), r=[self.WB["w_o_mla"]], w=[SLB])
                self.dma(wod, W["w_o_dsa"][:, cs].rearrange("(k p) n -> p k n", p=128), r=[self.WB["w_o_dsa"]], w=[SLB])
                self.dma(wgm, W["w_in"][:, NA + cc * 128:NA + (cc + 1) * 128].rearrange("(k p) n -> p k n", p=128),
                         r=[self.WB["w_in"]], w=[SLB])
                self.dma(wgd, W["w_in"][:, NA + D_MODEL + cc * 128:NA + D_MODEL + (cc + 1) * 128]
                         .rearrange("(k p) n -> p k n", p=128), r=[self.WB["w_in"]], w=[SLB])
                self.run(*[merge_chunk(x, cc, SLB, wom, wod, wgm, wgd) for x in unit])
            for c4 in range(4):
                sl_, SLB = slot()
                wo = sl_[:, 0:8192].rearrange("p (k n) -> p k n", n=512)
                self.dma(wo, W["w_out"][:, c4 * 512:(c4 + 1) * 512].rearrange("(k p) n -> p k n", p=128),
                         r=[self.WB["w_out"]], w=[SLB])
                for ci in range(4):
                    for x in unit:
                        wout_chunk(x, 4 * c4 + ci, SLB, wo, ci)
            for x in unit:
                rms_stage(x)
            for j in range(NJ):
                sl_, SLB = slot()
                wg = sl_[:, 0:2048].rearrange("p (k n) -> p k n", n=128)
                wu = sl_[:, 2048:4096].rearrange("p (k n) -> p k n", n=128)
                self.dma(wg, W["w_ffn_up"][:, j * 128:(j + 1) * 128].rearrange("(k p) n -> p k n", p=128),
                         r=[self.WB["w_ffn_up"]], w=[SLB])
                self.dma(wu, W["w_ffn_up"][:, D_FF + j * 128:D_FF + (j + 1) * 128].rearrange("(k p) n -> p k n", p=128),
                         r=[self.WB["w_ffn_up"]], w=[SLB])
                self.run(*[up_chunk(x, j, SLB, wg, wu) for x in unit])
            for cc in range(16):
                sl_, SLB = slot()
                wd = sl_[:, 0:NJ * 128].rearrange("p (k n) -> p k n", n=128)
                self.dma(wd, W["w_ffn_down"][:, cc * 128:(cc + 1) * 128].rearrange("(k p) n -> p k n", p=128),
                         r=[self.WB["w_ffn_down"]], w=[SLB])
                for x in unit:
                    down_chunk(x, cc, SLB, wd)
            for x in unit:
                out_stage(x)
            for x in unit:
                if x.last:
                    conv_state_out(x)
        S.release(m)

    def finish(self):
        self.S.emit()
        return self.nc


PHASES = ["P0", "A1", "A2", "B", "C", "D"]


def build(cfg, upto="D", debug=False):
    b = Builder(cfg, debug=debug)
    b.declare()
    b.setup_consts()
    b.phase0()
    for ph in PHASES[1:PHASES.index(upto) + 1]:
        getattr(b, "phase" + ph)()
    nc = b.finish()
    return b, nc


def rope_table(n):
    out = np.zeros((n, 2, 56), np.float32)
    pos = np.arange(n, dtype=np.float32)
    c0 = 0
    for half in (32, 16, 8):
        inv = (np.float32(500000.0) ** (-(np.arange(half, dtype=np.float32) / np.float32(half)))).astype(np.float32)
        ang = (pos[:, None] * inv[None, :]).astype(np.float32)
        out[:, 0, c0:c0 + half] = np.cos(ang)
        out[:, 1, c0:c0 + half] = np.sin(ang)
        c0 += half
    return out


def core_inputs(cfg, inp, core):
    NS = cfg.NS
    sl = slice(core * NS, (core + 1) * NS)
    f = lambda a: np.ascontiguousarray(a, dtype=np.float32)
    d = {
        "xp": f(inp["x_prompt"][core]),
        "xs": f(inp["x_sample"][sl].reshape(cfg.TS, D_MODEL)),
        "c_ckv": f(inp["cache_mla_ckv"][0, sl]),
        "c_kpe": f(inp["cache_mla_kpe"][0, sl]),
        "c_dk": f(inp["cache_dsa_k"][0, sl].reshape(NS, cfg.PAST, DKV * DHD)),
        "c_dv": f(inp["cache_dsa_v"][0, sl].reshape(NS, cfg.PAST, DKV * DHD)),
        "c_ik": f(inp["cache_idx_k"][0, sl]),
        "c_conv": f(inp["state_ffn_conv"][0, sl]),
        "ffn_normT": f(inp["ffn_norm"][0].reshape(D_MODEL // 128, 128).T),
        "conv_wT": f(inp["conv_w"][0].reshape(3, D_FF // 128, 128).transpose(2, 0, 1)),
        "conv_bT": f(inp["conv_b"][0].reshape(D_FF // 128, 128).T),
        "ident": np.eye(128, dtype=np.float32),
        "ropet": rope_table(max(cfg.T, cfg.PAST + cfg.DEC)),
    }
    for nm in ("attn_norm", "q_a_norm", "kv_a_norm", "mla_q_nope_norm", "mla_q_rope_norm", "mla_k_nope_norm",
               "mla_k_rope_norm", "dsa_q_norm", "dsa_k_norm"):
        d[nm] = f(inp[nm][0:1])
    for nm in ("w_in", "w_q_up", "w_kv_up", "w_o_mla", "w_o_dsa", "w_out", "w_ffn_up", "w_ffn_down"):
        d[nm] = f(inp[nm][0])
    return d


def run_core_debug(cfg, inp, upto):
    b, nc = build(cfg, upto, debug=True)
    print("stats", b.S.stats, flush=True)
    res = run_bass_kernel_spmd(nc, [core_inputs(cfg, inp, 0)], core_ids=[0]).results[0]
    outs = {k: res[k] for k in b.O}
    dbg = {k: res[v.tensor.name] if hasattr(v, "tensor") else None for k, v in b.X.items()}
    return outs, dbg


def kernel(**inputs):
    cfg = Cfg()
    inp = {k: np.asarray(v) for k, v in inputs.items()}
    b, nc = build(cfg, "D", debug=False)
    in_maps = [core_inputs(cfg, inp, core) for core in range(8)]
    res = run_bass_kernel_spmd(nc, in_maps, core_ids=list(range(8))).results
    g = lambda k: [np.asarray(r[k], dtype=np.float32) for r in res]
    NS, DEC, T = cfg.NS, cfg.DEC, cfg.T
    y_p = np.stack(g("y_p"), 0)
    y_s = np.concatenate([a.reshape(NS, DEC, D_MODEL) for a in g("y_s")], 0)
    p_ckv = np.stack(g("p_ckv"), 0)[None]
    p_kpe = np.stack(g("p_kpe"), 0)[None]
    p_dk = np.stack(g("p_dk"), 0).reshape(1, 8, T, DKV, DHD)
    p_dv = np.stack(g("p_dv"), 0).reshape(1, 8, T, DKV, DHD)
    p_ik = np.stack(g("p_ik"), 0)[None]
    p_conv = np.stack(g("p_conv"), 0)[None]
    cat = lambda k, *shp: np.concatenate([a.reshape((NS, DEC) + shp) for a in g(k)], 0)[None]
    s_ckv = cat("s_ckv", KV_LORA)
    s_kpe = cat("s_kpe", ROPE)
    s_dk = cat("s_dk", DKV, DHD)
    s_dv = cat("s_dv", DKV, DHD)
    s_ik = cat("s_ik", IDIM)
    s_conv = np.concatenate(g("s_conv"), 0)[None]
    return (y_p, y_s, p_ckv, p_kpe, p_dk, p_dv, p_ik, p_conv, s_ckv, s_kpe, s_dk, s_dv, s_ik, s_conv)
```
